# Optimizing a Trainium2 kernel written in Bass

```python
import math
import jax
import jax.numpy as jnp
from jax import lax
import numpy as np

D_MODEL = 2048
BATCH = 8
SEQ = 4096
DEPTH = 2
DEC_BATCH = 8
DEC_SEQ = 64
PAST_LEN = 1024

CHUNK = 64
N_EVEN = (DEPTH + 1) // 2
N_ODD = DEPTH // 2

A_HEADS = 8
A_HEAD_DIM = 128
A_WIDTH = A_HEADS * A_HEAD_DIM
IDX_HEADS = 16
IDX_DIM = 64
IDX_W_SCALE = (IDX_HEADS * IDX_DIM) ** -0.5
TOPK_MAX = 256
A_QBLOCK = 32
N_BUCKETS = 32
MAX_DISTANCE = 128

B_HEAD_DIM = 64
B_WIDTH = D_MODEL
B_HEADS = B_WIDTH // B_HEAD_DIM
B_GROUPS = 4
B_HPG = B_HEADS // B_GROUPS
B_STATE = 128
CONV_W = 4
B_CONV_DIM = B_WIDTH + 2 * B_GROUPS * B_STATE
DT_MIN = 1e-3
DT_MAX = 1e-1

C_HEADS = 16
Q_LORA = 512
KV_LORA = 512
QK_NOPE = 128
QK_ROPE = 64
V_DIM = 128
C_WIDTH = C_HEADS * V_DIM
ROPE_THETA = 10000.0
MLA_SCALE = (QK_NOPE + QK_ROPE) ** -0.5
C_QBLOCK = 128

ALPHA = (2 * DEPTH) ** 0.25
BETA = (8 * DEPTH) ** -0.25
EPS = 1e-5

SPLIT0 = (A_WIDTH, A_WIDTH, A_WIDTH, A_WIDTH, IDX_HEADS * IDX_DIM, IDX_DIM, IDX_HEADS, B_WIDTH, B_CONV_DIM, B_HEADS)
SPLIT1 = (Q_LORA, KV_LORA, QK_ROPE, C_WIDTH)
W_IN0 = sum(SPLIT0)
W_IN1 = sum(SPLIT1)
F32 = jnp.float32

kernel_name = 'hybrid_dsa_ssd_mla_stream_step'


def split_cols(h, sizes):
    return jnp.split(h, np.cumsum(sizes)[:-1].tolist(), axis=-1)


def layer_norm(x, g, b):
    xf = x.astype(F32)
    mu = jnp.mean(xf, axis=-1, keepdims=True)
    var = jnp.mean(jnp.square(xf - mu), axis=-1, keepdims=True)
    return ((xf - mu) * lax.rsqrt(var + EPS) * g.astype(F32) + b.astype(F32)).astype(x.dtype)


def rms_norm(x, g):
    xf = x.astype(F32)
    return (xf * lax.rsqrt(jnp.mean(jnp.square(xf), axis=-1, keepdims=True) + EPS) * g.astype(F32)).astype(x.dtype)


def chunk_visible(q_pos, k_pos):
    return (k_pos[None, :] // CHUNK) <= (q_pos[:, None] // CHUNK)


def to_blocks(a, size):
    b, t = a.shape[:2]
    return jnp.moveaxis(a.reshape((b, t // size, size) + a.shape[2:]), 1, 0)


def from_blocks(a):
    nb, b, size = a.shape[:3]
    return jnp.moveaxis(a, 0, 1).reshape((b, nb * size) + a.shape[3:])


def t5_bucket(rel):
    half = N_BUCKETS // 2
    max_exact = half // 2
    ret = jnp.where(rel < 0, half, 0)
    n = jnp.abs(rel)
    nf = jnp.maximum(n, 1).astype(F32)
    large = max_exact + (jnp.log(nf / max_exact) / math.log(MAX_DISTANCE / max_exact) * (half - max_exact)).astype(jnp.int32)
    large = jnp.minimum(large, half - 1)
    return ret + jnp.where(n < max_exact, n, large)


def rope(x, pos):
    half = QK_ROPE // 2
    inv = ROPE_THETA ** (-jnp.arange(half, dtype=F32) / half)
    ang = pos.astype(F32)[:, None] * inv[None, :]
    shape = (1, x.shape[1]) + (1,) * (x.ndim - 3) + (half,)
    cos = jnp.cos(ang).reshape(shape)
    sin = jnp.sin(ang).reshape(shape)
    x1 = x[..., :half].astype(F32)
    x2 = x[..., half:].astype(F32)
    return jnp.concatenate([x1 * cos - x2 * sin, x1 * sin + x2 * cos], axis=-1).astype(x.dtype)


def dsa_attend(q, qi, wi, q_pos, k, v, ki, k_pos, n_sel, bias_table):
    dots = jnp.einsum('bqhd,bld->bqhl', qi.astype(F32), ki.astype(F32))
    score = jnp.einsum('bqh,bqhl->bql', wi.astype(F32), jax.nn.relu(dots))
    score = jnp.where(chunk_visible(q_pos, k_pos)[None], score, -jnp.inf)
    _, sel = lax.top_k(score, n_sel)
    sel_pos = k_pos[sel]
    valid = (sel_pos // CHUNK) <= (q_pos[None, :, None] // CHUNK)
    kg = jax.vmap(lambda kb, ib: kb[ib])(k, sel)
    vg = jax.vmap(lambda vb, ib: vb[ib])(v, sel)
    logits = jnp.einsum('bqhd,bqnhd->bqhn', q, kg).astype(F32) * (A_HEAD_DIM ** -0.5)
    bias = bias_table[t5_bucket(q_pos[None, :, None] - sel_pos)]
    logits = logits + jnp.swapaxes(bias, -1, -2).astype(F32)
    logits = jnp.where(valid[:, :, None, :], logits, -jnp.inf)
    p = jax.nn.softmax(logits, axis=-1).astype(v.dtype)
    return jnp.einsum('bqhn,bqnhd->bqhd', p, vg)


def dsa_prompt(q, qi, wi, k, v, ki, bias_table):
    t = q.shape[1]
    n_sel = min(TOPK_MAX, t // 4)
    pos = jnp.arange(t, dtype=jnp.int32)

    def step(args):
        qb, qib, wib, pb = args
        return dsa_attend(qb, qib, wib, pb, k, v, ki, pos, n_sel, bias_table)

    out = lax.map(step, (to_blocks(q, A_QBLOCK), to_blocks(qi, A_QBLOCK), to_blocks(wi, A_QBLOCK),
                         pos.reshape(t // A_QBLOCK, A_QBLOCK)))
    return from_blocks(out)


def causal_conv(xpad, w, b):
    t = xpad.shape[1] - (CONV_W - 1)
    return b + sum(xpad[:, j:j + t] * w[j] for j in range(CONV_W))


def ssd_chunk(h0, x, dt, bm, cm, a):
    x = x.astype(F32)
    bm = bm.astype(F32)
    cm = cm.astype(F32)
    l = x.shape[1]
    acum = jnp.cumsum(dt * a, axis=1)
    seg = acum[:, :, None] - acum[:, None]
    causal = jnp.tril(jnp.ones((l, l), dtype=bool))[None, :, :, None, None]
    decay = jnp.exp(jnp.where(causal, seg, -jnp.inf))
    cb = jnp.einsum('btgn,bsgn->btsg', cm, bm)
    w = cb[..., None] * decay * dt[:, None]
    y = jnp.einsum('btsgr,bsgrp->btgrp', w, x)
    y = y + jnp.einsum('btgn,bgrpn->btgrp', cm, h0) * jnp.exp(acum)[..., None]
    tail = jnp.exp(acum[:, -1:] - acum) * dt
    h = h0 * jnp.exp(acum[:, -1])[..., None, None] + jnp.einsum('bsgr,bsgn,bsgrp->bgrpn', tail, bm, x)
    return h, y


def mamba_mix(z, xbc_pad, dt_raw, h0, conv_w, conv_b, dt_bias, a_log, d_skip, norm_w, scan_chunks):
    b, t = z.shape[:2]
    xbc = jax.nn.silu(causal_conv(xbc_pad, conv_w, conv_b))
    xs, bm, cm = split_cols(xbc, (B_WIDTH, B_GROUPS * B_STATE, B_GROUPS * B_STATE))
    xs = xs.reshape(b, t, B_GROUPS, B_HPG, B_HEAD_DIM)
    bm = bm.reshape(b, t, B_GROUPS, B_STATE)
    cm = cm.reshape(b, t, B_GROUPS, B_STATE)
    dt = jax.nn.softplus((dt_raw + dt_bias).astype(F32)).reshape(b, t, B_GROUPS, B_HPG)
    a = -jnp.exp(a_log.astype(F32)).reshape(B_GROUPS, B_HPG)
    hinit = h0.astype(F32).reshape(b, B_GROUPS, B_HPG, B_HEAD_DIM, B_STATE)
    if scan_chunks:
        h, ys = lax.scan(lambda hc, inp: ssd_chunk(hc, inp[0], inp[1], inp[2], inp[3], a), hinit,
                         (to_blocks(xs, CHUNK), to_blocks(dt, CHUNK), to_blocks(bm, CHUNK), to_blocks(cm, CHUNK)))
        y = from_blocks(ys)
    else:
        h, y = ssd_chunk(hinit, xs, dt, bm, cm, a)
    y = y + d_skip.astype(F32).reshape(B_GROUPS, B_HPG)[..., None] * xs.astype(F32)
    y = y.reshape(b, t, B_WIDTH) * jax.nn.silu(z.astype(F32))
    yg = y.reshape(b, t, B_GROUPS, B_WIDTH // B_GROUPS)
    yg = yg * lax.rsqrt(jnp.mean(jnp.square(yg), axis=-1, keepdims=True) + EPS)
    y = (yg.reshape(b, t, B_WIDTH) * norm_w.astype(F32)).astype(z.dtype)
    return y, h.reshape(b, B_HEADS, B_HEAD_DIM, B_STATE).astype(h0.dtype)


def even_layer(x, past, w_in, w_out, conv_w, conv_b, dt_bias, a_log, d_skip, norm_w, ln_g, ln_b, t5_bias):
    b, t, _ = x.shape
    aq, ak, av, ag, iq, ik, iw, bz, bxbc, bdt = split_cols(x @ w_in, SPLIT0)
    aq = aq.reshape(b, t, A_HEADS, A_HEAD_DIM)
    ak = ak.reshape(b, t, A_HEADS, A_HEAD_DIM)
    av = av.reshape(b, t, A_HEADS, A_HEAD_DIM)
    iq = iq.reshape(b, t, IDX_HEADS, IDX_DIM)
    iw = iw * IDX_W_SCALE
    if past is None:
        a_out = dsa_prompt(aq, iq, iw, ak, av, ik, t5_bias)
        xbc_pad = jnp.pad(bxbc, ((0, 0), (CONV_W - 1, 0), (0, 0)))
        h0 = jnp.zeros((b, B_HEADS, B_HEAD_DIM, B_STATE), x.dtype)
        scan_chunks = True
    else:
        pk, pv, pki, pconv, pssm = past
        p_len = pk.shape[1]
        q_pos = p_len + jnp.arange(t, dtype=jnp.int32)
        k_pos = jnp.arange(p_len + t, dtype=jnp.int32)
        n_sel = min(TOPK_MAX, (p_len + t) // 4)
        a_out = dsa_attend(aq, iq, iw, q_pos,
                           jnp.concatenate([pk, ak], axis=1), jnp.concatenate([pv, av], axis=1),
                           jnp.concatenate([pki, ik], axis=1), k_pos, n_sel, t5_bias)
        xbc_pad = jnp.concatenate([pconv, bxbc], axis=1)
        h0 = pssm
        scan_chunks = False
    b_out, ssm_new = mamba_mix(bz, xbc_pad, bdt, h0, conv_w, conv_b, dt_bias, a_log, d_skip, norm_w, scan_chunks)
    a_out = a_out.reshape(b, t, A_WIDTH) * jax.nn.silu(ag)
    mix = jnp.concatenate([a_out, b_out], axis=-1) @ w_out
    y = layer_norm(ALPHA * x + mix, ln_g, ln_b)
    return y, (ak, av, ik, xbc_pad[:, -(CONV_W - 1):], ssm_new)


def mla_attend(q_nope, q_rope, q_pos, k_nope, k_rope, v, k_pos):
    s = jnp.einsum('bqhd,bkhd->bhqk', q_nope, k_nope) + jnp.einsum('bqhd,bkd->bhqk', q_rope, k_rope)
    s = jnp.where(chunk_visible(q_pos, k_pos)[None, None], s.astype(F32) * MLA_SCALE, -jnp.inf)
    p = jax.nn.softmax(s, axis=-1).astype(v.dtype)
    return jnp.einsum('bhqk,bkhd->bqhd', p, v)


def mla_prompt(q_nope, q_rope, k_nope, k_rope, v):
    t = q_nope.shape[1]
    pos = jnp.arange(t, dtype=jnp.int32)

    def step(args):
        qn, qr, pb = args
        return mla_attend(qn, qr, pb, k_nope, k_rope, v, pos)

    out = lax.map(step, (to_blocks(q_nope, C_QBLOCK), to_blocks(q_rope, C_QBLOCK),
                         pos.reshape(t // C_QBLOCK, C_QBLOCK)))
    return from_blocks(out)


def odd_layer(x, past, w_in, q_norm_w, kv_norm_w, w_uq, w_ukv, w_out, ln_g, ln_b):
    b, t, _ = x.shape
    cq, ckv, kr, gate = split_cols(x @ w_in, SPLIT1)
    cq = rms_norm(cq, q_norm_w)
    ckv = rms_norm(ckv, kv_norm_w)
    q = (cq @ w_uq).reshape(b, t, C_HEADS, QK_NOPE + QK_ROPE)
    p_len = 0 if past is None else past[0].shape[1]
    q_pos = p_len + jnp.arange(t, dtype=jnp.int32)
    q_nope = q[..., :QK_NOPE]
    q_rope = rope(q[..., QK_NOPE:], q_pos)
    kr = rope(kr, q_pos)
    if past is None:
        lat, kr_all = ckv, kr
    else:
        lat = jnp.concatenate([past[0], ckv], axis=1)
        kr_all = jnp.concatenate([past[1], kr], axis=1)
    kv = (lat @ w_ukv).reshape(b, p_len + t, C_HEADS, QK_NOPE + V_DIM)
    k_nope = kv[..., :QK_NOPE]
    v = kv[..., QK_NOPE:]
    if past is None:
        o = mla_prompt(q_nope, q_rope, k_nope, kr_all, v)
    else:
        o = mla_attend(q_nope, q_rope, q_pos, k_nope, kr_all, v, jnp.arange(p_len + t, dtype=jnp.int32))
    o = o.reshape(b, t, C_WIDTH) * jax.nn.silu(gate)
    y = layer_norm(ALPHA * x + o @ w_out, ln_g, ln_b)
    return y, (ckv, kr)


def setup_inputs(seed: int = 0) -> dict:
    key = jax.random.key(seed)
    ks = iter(jax.random.split(key, 32))

    def nrm(shape, scale=1.0):
        return jax.random.normal(next(ks), shape, F32) * scale

    x_prompt = nrm((BATCH, SEQ, D_MODEL))
    x_sample = nrm((DEC_BATCH, DEC_SEQ, D_MODEL))
    cache_a_k = nrm((N_EVEN, DEC_BATCH, PAST_LEN, A_HEADS, A_HEAD_DIM))
    cache_a_v = nrm((N_EVEN, DEC_BATCH, PAST_LEN, A_HEADS, A_HEAD_DIM))
    cache_a_kidx = nrm((N_EVEN, DEC_BATCH, PAST_LEN, IDX_DIM))
    state_b_conv = nrm((N_EVEN, DEC_BATCH, CONV_W - 1, B_CONV_DIM))
    state_b_ssm = nrm((N_EVEN, DEC_BATCH, B_HEADS, B_HEAD_DIM, B_STATE), 0.1)
    cache_c_latent = nrm((N_ODD, DEC_BATCH, PAST_LEN, KV_LORA))
    cache_c_krope = nrm((N_ODD, DEC_BATCH, PAST_LEN, QK_ROPE))
    t5_bias = nrm((N_BUCKETS, A_HEADS), 0.5)
    w_in0 = nrm((N_EVEN, D_MODEL, W_IN0), D_MODEL ** -0.5)
    w_out0 = nrm((N_EVEN, A_WIDTH + B_WIDTH, D_MODEL), BETA * (A_WIDTH + B_WIDTH) ** -0.5)
    conv_w = nrm((N_EVEN, CONV_W, B_CONV_DIM), CONV_W ** -0.5)
    conv_b = nrm((N_EVEN, B_CONV_DIM), 0.02)
    u = jax.random.uniform(next(ks), (N_EVEN, B_HEADS), F32)
    dt0 = jnp.exp(u * (math.log(DT_MAX) - math.log(DT_MIN)) + math.log(DT_MIN))
    dt_bias = dt0 + jnp.log(-jnp.expm1(-dt0))
    a_log = jnp.log(jax.random.uniform(next(ks), (N_EVEN, B_HEADS), F32, 1.0, 16.0))
    d_skip = 1.0 + nrm((N_EVEN, B_HEADS), 0.1)
    ssm_norm_w = 1.0 + nrm((N_EVEN, B_WIDTH), 0.1)
    ln0_g = 1.0 + nrm((N_EVEN, D_MODEL), 0.1)
    ln0_b = nrm((N_EVEN, D_MODEL), 0.02)
    w_in1 = nrm((N_ODD, D_MODEL, W_IN1), D_MODEL ** -0.5)
    q_norm_w = 1.0 + nrm((N_ODD, Q_LORA), 0.1)
    kv_norm_w = 1.0 + nrm((N_ODD, KV_LORA), 0.1)
    w_uq = nrm((N_ODD, Q_LORA, C_HEADS * (QK_NOPE + QK_ROPE)), Q_LORA ** -0.5)
    w_ukv = nrm((N_ODD, KV_LORA, C_HEADS * (QK_NOPE + V_DIM)), KV_LORA ** -0.5)
    w_out1 = nrm((N_ODD, C_WIDTH, D_MODEL), BETA * C_WIDTH ** -0.5)
    ln1_g = 1.0 + nrm((N_ODD, D_MODEL), 0.1)
    ln1_b = nrm((N_ODD, D_MODEL), 0.02)
    return {'x_prompt': x_prompt, 'x_sample': x_sample,
            'cache_a_k': cache_a_k, 'cache_a_v': cache_a_v, 'cache_a_kidx': cache_a_kidx,
            'state_b_conv': state_b_conv, 'state_b_ssm': state_b_ssm,
            'cache_c_latent': cache_c_latent, 'cache_c_krope': cache_c_krope,
            't5_bias': t5_bias, 'w_in0': w_in0, 'w_out0': w_out0, 'conv_w': conv_w, 'conv_b': conv_b,
            'dt_bias': dt_bias, 'a_log': a_log, 'd_skip': d_skip, 'ssm_norm_w': ssm_norm_w,
            'ln0_g': ln0_g, 'ln0_b': ln0_b, 'w_in1': w_in1, 'q_norm_w': q_norm_w, 'kv_norm_w': kv_norm_w,
            'w_uq': w_uq, 'w_ukv': w_ukv, 'w_out1': w_out1, 'ln1_g': ln1_g, 'ln1_b': ln1_b}


def reference(x_prompt, x_sample, cache_a_k, cache_a_v, cache_a_kidx, state_b_conv, state_b_ssm,
              cache_c_latent, cache_c_krope, t5_bias, w_in0, w_out0, conv_w, conv_b, dt_bias, a_log,
              d_skip, ssm_norm_w, ln0_g, ln0_b, w_in1, q_norm_w, kv_norm_w, w_uq, w_ukv, w_out1,
              ln1_g, ln1_b):
    yp, ys = x_prompt, x_sample
    even_p, even_s, odd_p, odd_s = [], [], [], []
    for layer in range(DEPTH):
        if layer % 2 == 0:
            e = layer // 2
            prm = (w_in0[e], w_out0[e], conv_w[e], conv_b[e], dt_bias[e], a_log[e], d_skip[e],
                   ssm_norm_w[e], ln0_g[e], ln0_b[e], t5_bias)
            yp, st_p = even_layer(yp, None, *prm)
            past = (cache_a_k[e], cache_a_v[e], cache_a_kidx[e], state_b_conv[e], state_b_ssm[e])
            ys, st_s = even_layer(ys, past, *prm)
            even_p.append(st_p)
            even_s.append(st_s)
        else:
            o = layer // 2
            prm = (w_in1[o], q_norm_w[o], kv_norm_w[o], w_uq[o], w_ukv[o], w_out1[o], ln1_g[o], ln1_b[o])
            yp, st_p = odd_layer(yp, None, *prm)
            ys, st_s = odd_layer(ys, (cache_c_latent[o], cache_c_krope[o]), *prm)
            odd_p.append(st_p)
            odd_s.append(st_s)
    a_k_p, a_v_p, a_ki_p, b_conv_p, b_ssm_p = [jnp.stack(s) for s in zip(*even_p)]
    a_k_s, a_v_s, a_ki_s, b_conv_s, b_ssm_s = [jnp.stack(s) for s in zip(*even_s)]
    c_lat_p, c_kr_p = [jnp.stack(s) for s in zip(*odd_p)]
    c_lat_s, c_kr_s = [jnp.stack(s) for s in zip(*odd_s)]
    return (yp, ys, a_k_p, a_k_s, a_v_p, a_v_s, a_ki_p, a_ki_s, b_conv_p, b_conv_s,
            b_ssm_p, b_ssm_s, c_lat_p, c_lat_s, c_kr_p, c_kr_s)
```

```python
import math
from contextlib import ExitStack
import numpy as np
import concourse.bass as bass
import concourse.mybir as mybir
from concourse.bass_utils import run_bass_kernel_spmd

F32 = mybir.dt.float32
BF16 = mybir.dt.bfloat16
ALU = mybir.AluOpType
AF = mybir.ActivationFunctionType
AX = mybir.AxisListType

ENGS = ("tensor", "vector", "scalar", "gpsimd", "sync")

D_MODEL = 2048
SEQ = 4096
DEC_SEQ = 64
PAST = 1024
A_HEADS = 8
IDX_HEADS = 16
TOPK = 256
B_HEADS = 32
B_GROUPS = 4
C_HEADS = 16
EPS = 1e-5
ALPHA = (2 * 2) ** 0.25
A_SCALE = 128 ** -0.5
MLA_SCALE = (128 + 64) ** -0.5
W_IN0 = 10352
NEG = -30000.0


class Sem:
    def __init__(self, h, name, dma=False):
        self.h = h
        self.name = name
        self.dma = dma
        self.cum = 0
        self.bounds = []


class DSem:
    def __init__(self, kb):
        self.kb = kb
        self.hw = None
        self.sw = None

    def get(self, qn):
        if qn == "gpsimd":
            if self.sw is None:
                self.sw = self.kb.free_sw.pop()
            return self.sw
        if self.hw is None:
            self.hw = self.kb.free_hw.pop()
        return self.hw

    def release(self):
        if self.hw is not None:
            self.kb.free_hw.append(self.hw)
        if self.sw is not None:
            self.kb.free_sw.append(self.sw)
        self.hw = self.sw = None


class Buf:
    __slots__ = ("name", "w", "rs", "psum")

    def __init__(self, name="b", psum=False):
        self.name = name
        self.w = None
        self.rs = []
        self.psum = psum


class Eng:
    def __init__(self, name, sem):
        self.name = name
        self.sem = sem
        self.prog = []
        self.waited = {}
        self.pend_r = []
        self.pend_w = []


class KB:
    def __init__(self, nc):
        self.nc = nc
        self.E = {}
        self.nins = 0

    def setup(self, stack, n_hw=64, n_sw=32):
        for n in ENGS:
            h = stack.enter_context(self.nc.semaphore("e_" + n))
            self.E[n] = Eng(n, Sem(h, "e_" + n))
        self.free_hw = []
        self.free_sw = []
        for i in range(n_hw):
            h = stack.enter_context(self.nc.semaphore("dh%d" % i))
            self.free_hw.append(Sem(h, "dh%d" % i, dma=True))
        for i in range(n_sw):
            h = stack.enter_context(self.nc.semaphore("ds%d" % i))
            self.free_sw.append(Sem(h, "ds%d" % i, dma=True))
        self.all_dsems = self.free_hw + self.free_sw

    def dsem(self):
        return DSem(self)

    def _wait(self, e, ev):
        sem, val = ev
        if val is None:
            assert sem is e.sem, "dependency on pending instruction of another engine"
            return
        if sem.dma:
            b = None
            for x in reversed(sem.bounds):
                if x >= val:
                    b = x
                else:
                    break
            if b is None:
                b = sem.cum
                sem.bounds.append(b)
                if len(sem.bounds) > 8:
                    sem.bounds = sem.bounds[-8:]
            val = b
        if sem is e.sem and e.name in ("tensor", "sync"):
            return
        if e.waited.get(sem, 0) >= val:
            return
        e.waited[sem] = val
        h = sem.h
        e.prog.append(lambda eng, h=h, val=val: eng.wait_ge(h, val))
        self.nins += 1

    def _deps(self, e, r, w):
        for b in r:
            if b.w is not None:
                self._wait(e, b.w)
            if b.psum:
                for ev in b.rs:
                    if ev[0] is not e.sem:
                        self._wait(e, ev)
        for b in w:
            if b.w is not None:
                self._wait(e, b.w)
            for ev in b.rs:
                self._wait(e, ev)
            for e2 in self.E.values():
                if e2 is not e and e2.pend_r:
                    for pb in e2.pend_r:
                        assert pb is not b, "WAR on a pending (non-inc) read of %s by %s" % (b.name, e2.name)

    def op(self, en, fn, r=(), w=(), inc=True):
        e = self.E[en]
        rec = _Rec()
        fn(rec)
        assert len(rec.calls) == 1
        cname, cargs, ckw = rec.calls[0]
        fn = lambda eng, cname=cname, cargs=cargs, ckw=ckw: getattr(eng, cname)(*cargs, **ckw)
        self._deps(e, r, w)
        self.nins += 1
        if not inc:
            e.pend_r += list(r)
            e.pend_w += list(w)
            e.prog.append(lambda eng, fn=fn: fn(eng))
            for b in w:
                b.w = (e.sem, None)
                b.rs = []
            return
        e.sem.cum += 1
        ev = (e.sem, e.sem.cum)
        h = e.sem.h
        e.prog.append(lambda eng, fn=fn, h=h: fn(eng).then_inc(h, 1))
        for b in e.pend_r + list(r):
            b.rs.append(ev)
            if len(b.rs) > 12:
                b.rs = _prune(b.rs)
        for b in e.pend_w + list(w):
            b.w = ev
            b.rs = []
        e.pend_r = []
        e.pend_w = []

    def dma(self, qn, out, in_, sem, r=(), w=(), **kw):
        e = self.E[qn]
        sem = sem.get(qn)
        self._deps(e, r, w)
        if sem.bounds:
            b = sem.bounds[-1]
            if e.waited.get(sem, 0) < b:
                e.waited[sem] = b
                h = sem.h
                e.prog.append(lambda eng, h=h, b=b: eng.wait_ge(h, b))
        sem.cum += 16
        ev = (sem, sem.cum)
        h = sem.h
        e.prog.append(lambda eng, out=out, in_=in_, h=h, kw=kw:
                      eng.dma_start(out=out, in_=in_, **kw).then_inc(h, 16))
        self.nins += 1
        for b in r:
            b.rs.append(ev)
            if len(b.rs) > 12:
                b.rs = _prune(b.rs)
        for b in w:
            b.w = ev
            b.rs = []

    def barrier(self):
        evs = []
        for n in ENGS:
            e = self.E[n]
            assert not e.pend_r and not e.pend_w, "pending at barrier on " + n
            if e.sem.cum > 0:
                evs.append((e.sem, e.sem.cum))
        for s in self.all_dsems:
            if s.cum > 0:
                evs.append((s, s.cum))
        for n in ENGS:
            for ev in evs:
                self._wait(self.E[n], ev)

    def emit(self):
        nc = self.nc
        progs = {n: self.E[n].prog for n in ENGS}
        for n in ENGS:
            self.E[n].prog = []
        with nc.Block() as block:
            @block.tensor
            def _(eng):
                for f in progs["tensor"]:
                    f(eng)

            @block.vector
            def _(eng):
                for f in progs["vector"]:
                    f(eng)

            @block.scalar
            def _(eng):
                for f in progs["scalar"]:
                    f(eng)

            @block.gpsimd
            def _(eng):
                for f in progs["gpsimd"]:
                    f(eng)

            @block.sync
            def _(eng):
                for f in progs["sync"]:
                    f(eng)


def run_lanes(lanes):
    n = len(lanes)
    clocks = [0.0] * n
    done = [False] * n
    waiting = [None] * n
    flags = {}
    while not all(done):
        progressed = False
        for i in sorted(range(n), key=lambda i: clocks[i]):
            if done[i]:
                continue
            if waiting[i] is not None:
                if waiting[i] not in flags:
                    continue
                clocks[i] = max(clocks[i], flags[waiting[i]])
                waiting[i] = None
            try:
                v = next(lanes[i])
            except StopIteration:
                done[i] = True
                progressed = True
                break
            if isinstance(v, tuple):
                if v[0] == "wait":
                    if v[1] in flags:
                        clocks[i] = max(clocks[i], flags[v[1]])
                    else:
                        waiting[i] = v[1]
                else:
                    flags[v[1]] = clocks[i]
            else:
                clocks[i] += v
            progressed = True
            break
        assert progressed, "lane scheduler deadlock"


class _Rec:
    def __init__(self):
        self.calls = []

    def __getattr__(self, name):
        def f(*a, **k):
            self.calls.append((name, a, k))
            return None
        return f


def _prune(rs):
    best = {}
    for s, v in rs:
        if s not in best or best[s] < v:
            best[s] = v
    return [(s, v) for s, v in best.items()]


class Ring:
    def __init__(self, K, name, shape, dtype, n, dma=True, psum=False):
        self.items = []
        self.i = 0
        self.sems = []
        for j in range(n):
            if psum:
                t = K.ps("%s%d" % (name, j), shape, dtype)
            else:
                t = K.sb("%s%d" % (name, j), shape, dtype)
            s = K.kb.dsem() if dma else None
            if s is not None:
                K.phase_sems.append(s)
            self.items.append((t, Buf("%s%d" % (name, j), psum=psum), s))

    def next(self):
        it = self.items[self.i % len(self.items)]
        self.i += 1
        return it


class Kern:
    def __init__(self, NB, debug=()):
        self.NB = NB
        self.TP = 512 * NB
        self.TALL = self.TP + DEC_SEQ
        self.debug = set(debug)
        self.nc = bass.Bass("TRN2", target_bir_lowering=False)
        self.kb = KB(self.nc)
        self.D = {}
        self.phase_sems = []
        self.blocks = [(i * 512, 512) for i in range(NB)] + [(self.TP, DEC_SEQ)]

    def din(self, name, shape, dt=F32):
        self.D[name] = self.nc.dram_tensor(name, list(shape), dt, kind="ExternalInput").ap()
        return self.D[name]

    def dout(self, name, shape, dt=F32):
        self.D[name] = self.nc.dram_tensor(name, list(shape), dt, kind="ExternalOutput").ap()
        return self.D[name]

    def dscr(self, name, shape, dt):
        kind = "ExternalOutput" if name in self.debug else "Internal"
        self.D[name] = self.nc.dram_tensor(name, list(shape), dt, kind=kind).ap()
        self.DB[name] = Buf(name)
        return self.D[name]

    def uname(self, name):
        self.uid = getattr(self, "uid", 0) + 1
        return "%s_u%d" % (name, self.uid)

    def sb(self, name, shape, dt):
        return self.st.enter_context(self.nc.sbuf_tensor(self.uname(name), list(shape), dt))

    def ps(self, name, shape, dt=F32):
        return self.st.enter_context(self.nc.psum_tensor(self.uname(name), list(shape), dt))

    def begin(self):
        self.st = ExitStack()
        self.phase_sems = []

    def end(self):
        self.kb.barrier()
        self.kb.emit()
        self.st.close()
        for s_ in self.phase_sems:
            s_.release()
        self.phase_sems = []

    def sem(self):
        s = self.kb.dsem()
        self.phase_sems.append(s)
        return s

    def declare(self):
        T = self.TALL
        self.DB = {}
        din, dout, dscr = self.din, self.dout, self.dscr
        din("xT", [2048, T])
        din("w0fm", [73, 128, 16, 128])
        din("w0v", [2, 128, 16, 512])
        din("w0s", [128, 16, 48])
        din("convw", [128, 24, 4])
        din("convb", [128, 24])
        din("sconvT", [3072, 3])
        din("ident", [128, 128])
        din("antiid", [128, 128])
        din("t5", [32, 8])
        din("t5oh", [32, 1152])
        din("ckT", [1024, PAST])
        din("cv", [PAST, 1024])
        din("ckidxT2", [128, PAST])
        din("dtbias", [1, 32])
        din("alog", [1, 32])
        din("dskT", [128, 16])
        din("nwT", [128, 16])
        din("h0T", [128, 2048])
        din("utri", [128, 128])
        din("negm", [128, 128])
        dout("o_ssmT", [2, 128, 2048])
        din("wo0", [128, 24, 2048]); din("wo1", [128, 16, 2048])
        din("ln0g", [128, 16]); din("ln0b", [128, 16]); din("ln1g", [128, 16]); din("ln1b", [128, 16])
        din("w1fm", [24, 128, 16, 128]); din("w1kr", [2, 128, 16, 64])
        din("w1q", [16, 128, 4, 256]); din("w1kv", [16, 128, 4, 256])
        din("qnw", [128, 4]); din("kvnw", [128, 4])
        din("rcos", [64, T]); din("rsin", [64, T])
        din("clatT", [512, PAST]); din("ckrT", [64, PAST]); din("dmask", [128, 128])
        dout("o_yT", [2048, T]); dout("o_clatT", [512, T]); dout("o_ckrT", [64, T])
        dscr("Y0T", [2048, T], F32); dscr("CQN", [512, T], BF16); dscr("LAT", [512, T], BF16)
        dscr("KR", [64, T], BF16); dscr("GS", [2048, T], F32); dscr("O1T", [2048, T], BF16)
        dout("o_akT", [1024, T])
        dout("o_av", [T, 1024])
        dout("o_kidxT", [64, T])
        dout("o_convT", [3072, 6])
        dscr("QT", [1024, T], BF16)
        dscr("KT", [1024, T], BF16)
        dscr("IQ", [1024, T], BF16)
        dscr("IK2", [128, T], BF16)
        dscr("V", [T, 1024], BF16)
        dscr("AGS", [1024, T], F32)
        dscr("IWDT", [T, 48], F32)
        dscr("ZS", [2048, T], F32)
        dscr("XC", [3072, T], F32)
        dscr("FD", [8, 1152], F32)
        dscr("MIXT", [3072, T], BF16)

    def phase1(self):
        K = self
        kb, nc, D, DB = self.kb, self.nc, self.D, self.DB
        T = self.TALL
        self.begin()
        xsb = self.sb("xsb", [128, 16, T], BF16)
        b_x = [Buf("x%d" % k) for k in range(16)]
        s_x = self.sem()
        for kc in range(16):
            kb.dma("gpsimd", xsb[:, kc, :], D["xT"][kc * 128:(kc + 1) * 128, :], s_x, w=[b_x[kc]],
                   max_dma_last_dim=4096)
        cw = self.sb("cw", [128, 24, 4], F32)
        cb = self.sb("cb", [128, 24], F32)
        b_c = Buf("convc")
        s_c = self.sem()
        kb.dma("sync", cw[:], D["convw"], s_c, w=[b_c])
        kb.dma("sync", cb[:], D["convb"], s_c, w=[b_c])

        wr = Ring(K, "wfm", [128, 16, 128], BF16, 3)
        psr = Ring(K, "ps1", [128, 512], F32, 4, dma=False, psum=True)
        sbf = Ring(K, "sbf", [128, 512], BF16, 3)
        sf32 = Ring(K, "sf32", [128, 512], F32, 3)
        csr = Ring(K, "cs", [128, 515], F32, 2)
        accr = Ring(K, "acc", [128, 512], F32, 2, dma=False)

        def mm_group(pst, pb, wt, wb, t0, tn):
            for kc in range(16):
                kb.op("tensor", lambda e, kc=kc: e.matmul(
                    pst[:, 0:tn], wt[:, kc, :], xsb[:, kc, t0:t0 + tn],
                    start=(kc == 0), stop=(kc == 15)),
                    r=[wb, b_x[kc]], w=[pb], inc=(kc == 15))

        tiles = ([("q", i) for i in range(8)] + [("k", i) for i in range(8)] + [("ag", i) for i in range(8)] +
                 [("iq", i) for i in range(8)] + [("ik2", 0)] + [("z", i) for i in range(16)] +
                 [("xbc", i) for i in range(24)])
        assert len(tiles) == 73
        import os
        P1M = os.environ.get("P1MASK", "axt")
        for ct, (kind, idx) in enumerate(tiles):
            if kind == "xbc" and "x" not in P1M:
                continue
            if kind != "xbc" and "a" not in P1M:
                continue
            if os.environ.get("P1KINDS") and kind not in os.environ["P1KINDS"].split(","):
                continue
            wt, wb, ws = wr.next()
            kb.dma("gpsimd", wt[:], D["w0fm"][ct], ws, w=[wb], max_dma_last_dim=4096)
            rows = slice(idx * 128, (idx + 1) * 128)
            for tb, (t0, tn) in enumerate(self.blocks):
                pst, pb, _ = psr.next()
                mm_group(pst, pb, wt, wb, t0, tn)
                cols = slice(t0, t0 + tn)
                if kind in ("q", "k", "iq", "ik2"):
                    name = {"q": "QT", "k": "KT", "iq": "IQ", "ik2": "IK2"}[kind]
                    st, sb_, ss = sbf.next()
                    kb.op("scalar", lambda e, st=st, pst=pst, tn=tn: e.activation(
                        out=st[:, 0:tn], in_=pst[:, 0:tn], func=AF.Copy), r=[pb], w=[sb_])
                    kb.dma("sync", D[name][rows, cols], st[:, 0:tn], ss, r=[sb_], w=[])
                    if kind in ("k", "ik2"):
                        s2, s2b, s2s = sf32.next()
                        if os.environ.get("P1ACT"):
                            kb.op("scalar", lambda e, s2=s2, pst=pst, tn=tn: e.activation(
                                out=s2[:, 0:tn], in_=pst[:, 0:tn], func=AF.Copy), r=[pb], w=[s2b])
                        else:
                            kb.op("vector", lambda e, s2=s2, pst=pst, tn=tn: e.tensor_copy(
                                out=s2[:, 0:tn], in_=pst[:, 0:tn]), r=[pb], w=[s2b])
                        if kind == "k":
                            kb.dma("sync", D["o_akT"][rows, cols], s2[:, 0:tn], s2s, r=[s2b])
                        else:
                            kb.dma("sync", D["o_kidxT"][:, cols], s2[0:64, 0:tn], s2s, r=[s2b])
                elif kind in ("ag", "z"):
                    name = {"ag": "AGS", "z": "ZS"}[kind]
                    s2, s2b, s2s = sf32.next()
                    kb.op("scalar", lambda e, s2=s2, pst=pst, tn=tn: e.activation(
                        out=s2[:, 0:tn], in_=pst[:, 0:tn], func=AF.Silu), r=[pb], w=[s2b])
                    kb.dma("sync", D[name][rows, cols], s2[:, 0:tn], s2s, r=[s2b])
                else:
                    cs, csb, css = csr.next()
                    if tb == 0:
                        kb.op("gpsimd", lambda e, cs=cs: e.memset(cs[:, 0:3], 0.0), w=[csb])
                    elif tb == self.NB:
                        kb.dma("sync", cs[:, 0:3], D["sconvT"][rows, :], css, w=[csb])
                    else:
                        pcs, pcsb, _ = csr.items[(csr.i - 2) % 2]
                        kb.op("vector", lambda e, cs=cs, pcs=pcs: e.tensor_copy(
                            out=cs[:, 0:3], in_=pcs[:, 512:515]), r=[pcsb], w=[csb])
                    kb.op("scalar", lambda e, cs=cs, pst=pst, tn=tn: e.activation(
                        out=cs[:, 3:3 + tn], in_=pst[:, 0:tn], func=AF.Copy), r=[pb], w=[csb])
                    if tb == self.NB - 1:
                        kb.dma("sync", D["o_convT"][rows, 0:3], cs[:, 512:515], css, r=[csb])
                    if tb == self.NB:
                        kb.dma("sync", D["o_convT"][rows, 3:6], cs[:, 64:67], css, r=[csb])
                    ac, acb, _ = accr.next()
                    kb.op("scalar", lambda e, ac=ac, cs=cs, tn=tn, idx=idx: e.activation(
                        out=ac[:, 0:tn], in_=cs[:, 0:tn], func=AF.Identity,
                        scale=cw[:, idx, 0:1], bias=cb[:, idx:idx + 1]), r=[csb, b_c], w=[acb])
                    for j in range(1, 4):
                        kb.op("vector", lambda e, ac=ac, cs=cs, tn=tn, idx=idx, j=j: e.scalar_tensor_tensor(
                            out=ac[:, 0:tn], in0=cs[:, j:j + tn], scalar=cw[:, idx, j:j + 1], in1=ac[:, 0:tn],
                            op0=ALU.mult, op1=ALU.add), r=[csb, b_c, acb], w=[acb])
                    s2, s2b, s2s = sf32.next()
                    kb.op("scalar", lambda e, s2=s2, ac=ac, tn=tn: e.activation(
                        out=s2[:, 0:tn], in_=ac[:, 0:tn], func=AF.Silu), r=[acb], w=[s2b])
                    kb.dma("sync", D["XC"][rows, cols], s2[:, 0:tn], s2s, r=[s2b])

        wv = self.sb("wv", [128, 16, 512], BF16)
        wvb = Buf("wv")
        wvs = self.sem()
        wsm = self.sb("wsm", [128, 16, 48], BF16)
        wsb = Buf("wsm")
        kb.dma("gpsimd", wsm[:], D["w0s"], wvs, w=[wsb], max_dma_last_dim=4096)
        ttiles = [(i * 128, 128) for i in range(self.TP // 128)] + [(self.TP, DEC_SEQ)]
        for g in range(3):
            if "t" not in P1M:
                continue
            if g < 2:
                kb.dma("gpsimd", wv[:], D["w0v"][g], wvs, w=[wvb], max_dma_last_dim=4096)
            for (t0, tn) in ttiles:
                pst, pb, _ = psr.next()
                ncol = 512 if g < 2 else 48
                wt_, wb_ = (wv, wvb) if g < 2 else (wsm, wsb)
                for kc in range(16):
                    kb.op("tensor", lambda e, kc=kc, pst=pst, wt_=wt_, t0=t0, tn=tn, ncol=ncol: e.matmul(
                        pst[0:tn, 0:ncol], xsb[:, kc, t0:t0 + tn], wt_[:, kc, 0:ncol],
                        start=(kc == 0), stop=(kc == 15)),
                        r=[wb_, b_x[kc]], w=[pb], inc=(kc == 15))
                if g < 2:
                    st, sb_, ss = sbf.next()
                    kb.op("scalar", lambda e, st=st, pst=pst, tn=tn: e.activation(
                        out=st[0:tn, :], in_=pst[0:tn, :], func=AF.Copy), r=[pb], w=[sb_])
                    kb.dma("sync", D["V"][t0:t0 + tn, g * 512:(g + 1) * 512], st[0:tn, :], ss, r=[sb_])
                    s2, s2b, s2s = sf32.next()
                    kb.op("vector", lambda e, s2=s2, pst=pst, tn=tn: e.tensor_copy(
                        out=s2[0:tn, :], in_=pst[0:tn, :]), r=[pb], w=[s2b])
                    kb.dma("sync", D["o_av"][t0:t0 + tn, g * 512:(g + 1) * 512], s2[0:tn, :], s2s, r=[s2b])
                else:
                    s2, s2b, s2s = sf32.next()
                    kb.op("vector", lambda e, s2=s2, pst=pst, tn=tn: e.tensor_copy(
                        out=s2[0:tn, 0:48], in_=pst[0:tn, 0:48]), r=[pb], w=[s2b])
                    kb.dma("sync", D["IWDT"][t0:t0 + tn, :], s2[0:tn, 0:48], s2s, r=[s2b])
        self.end()


    def phase0(self, gst):
        kb, nc, D, DB = self.kb, self.nc, self.D, self.DB
        C = {}
        def g(name, shape, dt):
            C[name] = gst.enter_context(nc.sbuf_tensor("c_" + name, list(shape), dt))
            return C[name]
        self.C = C
        self.CB = Buf("consts")
        cbuf = self.CB
        g("idf", [128, 128], F32); g("idb", [128, 128], BF16); g("jf", [128, 128], F32)
        g("onesb", [128, 128], BF16); g("onesf", [128, 128], F32); g("b15", [128, 8], F32)
        self.begin()
        s0 = self.sem()
        kb.dma("sync", C["idf"][:], D["ident"], s0, w=[cbuf])
        kb.dma("sync", C["jf"][:], D["antiid"], s0, w=[cbuf])
        kb.dma("sync", C["b15"][:], D["t5"][15:16, :].broadcast_to([128, 8]), s0, w=[cbuf])
        kb.op("vector", lambda e: e.tensor_copy(out=C["idb"][:], in_=C["idf"][:]), r=[cbuf], w=[cbuf])
        kb.op("vector", lambda e: e.memset(C["onesb"][:], 1.0), w=[cbuf])
        kb.op("vector", lambda e: e.memset(C["onesf"][:], 1.0), w=[cbuf])
        tb = self.sb("t5sb", [32, 8], F32)
        oh = self.sb("t5ohsb", [32, 1152], F32)
        fsb = self.sb("fsb", [8, 1152], F32)
        b1 = Buf(); b2 = Buf("fps", psum=True); b3 = Buf()
        kb.dma("sync", tb[:], D["t5"], s0, w=[b1])
        kb.dma("sync", oh[:], D["t5oh"], s0, w=[b1])
        fps = self.ps("fps", [8, 512], F32)
        for i in range(3):
            kb.op("tensor", lambda e, i=i: e.matmul(fps[:, 0:384], tb[:], oh[:, i * 384:(i + 1) * 384],
                                                    start=True, stop=True), r=[b1], w=[b2])
            kb.op("scalar", lambda e, i=i: e.activation(out=fsb[:, i * 384:(i + 1) * 384], in_=fps[:, 0:384],
                                                        func=AF.Copy, scale=1.0 / A_SCALE), r=[b2], w=[b3])
        kb.dma("sync", D["FD"], fsb[:], s0, r=[b3])
        self.end()

    def phase2(self):
        K = self
        import os
        DBG = os.environ.get("P2DBG", "")
        kb, nc, D, DB, C = self.kb, self.nc, self.D, self.DB, self.C
        NB, TP, T = self.NB, self.TP, self.TALL
        cb_ = self.CB
        self.begin()
        LS = PAST + DEC_SEQ
        SW = max(TP, LS)
        ik2 = self.sb("ik2", [128, TP], BF16); b_ik = Buf("ik2")
        ik2s = self.sb("ik2s", [128, LS], BF16); b_iks = Buf("ik2s")
        s_l = self.sem()
        kb.dma("sync", ik2[:], D["IK2"][:, 0:TP], s_l, w=[b_ik])
        kb.dma("gpsimd", ik2s[:, 0:PAST], D["ckidxT2"], s_l, w=[b_iks], max_dma_last_dim=4096)
        kb.dma("sync", ik2s[:, PAST:LS], D["IK2"][:, TP:T], s_l, w=[b_iks])
        iqr = Ring(K, "iqz", [128, 2, 8, 512], BF16, 1)
        iwr = Ring(K, "iwt", [128, 16], F32, 2)
        dgr = Ring(K, "dg", [128, 16, 128], BF16, 1, dma=False)
        trr = Ring(K, "trl", [128, 512], BF16, 3, dma=False)
        scr = Ring(K, "score", [128, SW], F32, 2, dma=False)
        smals = [self.sb("smal%d" % i, [128, 16], F32) for i in range(2)]
        b_sms = [Buf("smal0"), Buf("smal1")]
        Mr = Ring(K, "Msel", [128, SW], BF16, 2, dma=False)
        NKT = max(TP // 128, 9)
        MTs = [self.sb("MT%d" % i, [128, max(NKT - 4 * (1 - i), 9), 512], BF16) for i in range(2)]
        b_MTs = [Buf("MT0"), Buf("MT1")]
        ktr = Ring(K, "kth", [128, SW], BF16, 2)
        vr = Ring(K, "vh", [128, NKT, 128], BF16, 2)
        qr = Ring(K, "qth", [128, 512], BF16, 2)
        wbr = Ring(K, "wbh", [128, 1024], F32, 1)
        ptr = Ring(K, "pt", [128, 512], BF16, 3, dma=False)
        agr = Ring(K, "agt", [128, 512], F32, 1)
        rsb = self.sb("rsb", [128, 512], F32); b_rs = Buf("rs")
        tsb = rsb; b_ts = b_rs
        outr = Ring(K, "aout", [128, 512], BF16, 1)
        dpr = Ring(K, "dps", [128, 512], F32, 2, dma=False, psum=True)
        scp = self.ps("scp", [128, 512], F32); b_scp = Buf("scp", psum=True)
        trp = Ring(K, "trp", [128, 512], BF16, 1, dma=False, psum=True)
        stp = Ring(K, "stp", [128, 512], F32, 2, dma=False, psum=True)
        otp = self.ps("otp", [128, 512], F32); b_otp = Buf("otp", psum=True)
        smp = self.ps("smp", [128, 512], F32); b_smp = Buf("smp", psum=True)

        class Tile:
            pass

        def gen_scores(tl):
            P, Lv = tl.P, tl.Lv
            iwt, iwb, iws = iwr.next()
            kb.dma("sync", iwt[0:P, :], D["IWDT"][tl.iwrows, 0:16], iws, w=[iwb])
            dg, dgb, _ = dgr.next()
            kb.op("vector", lambda e: e.tensor_tensor(
                out=dg[0:P, :, 0:P],
                in0=C["idb"][0:P, 0:P].unsqueeze(1).broadcast_to([P, 16, P]),
                in1=iwt[0:P, :].unsqueeze(2).broadcast_to([P, 16, P]), op=ALU.mult),
                r=[cb_, iwb], w=[dgb])
            tl.sc, tl.scb, _ = scr.next()
            sc, scb = tl.sc, tl.scb
            nkb = (Lv + 511) // 512
            for kbi in range(nkb):
                wk = min(512, Lv - 512 * kbi)
                pend = None
                for h in range(16):
                    dp, dpb, _ = dpr.next()
                    po = (h % 2) * 64
                    kb.op("tensor", lambda e: e.matmul(
                        dp[0:P, 0:wk], tl.iq_l(h), tl.iksb[:, kbi * 512:kbi * 512 + wk],
                        start=True, stop=True), r=[tl.iqbuf, tl.ikb], w=[dpb])
                    tr, trb, _ = trr.next()
                    kb.op("scalar", lambda e: e.activation(
                        out=tr[0:P, 0:wk], in_=dp[0:P, 0:wk], func=AF.Relu), r=[dpb], w=[trb])
                    if pend is not None:
                        pend()
                    def acc(h=h, tr=tr, trb=trb, wk=wk):
                        kb.op("tensor", lambda e: e.matmul(
                            scp[0:P, 0:wk], dg[0:P, h, 0:P], tr[0:P, 0:wk], start=(h == 0), stop=(h == 15)),
                            r=[dgb, trb], w=[b_scp], inc=(h == 15))
                    pend = acc
                    yield 0.45
                pend()
                kb.op("scalar", lambda e: e.activation(
                    out=sc[0:P, kbi * 512:kbi * 512 + wk], in_=scp[0:P, 0:wk], func=AF.Copy),
                    r=[b_scp], w=[scb])
                yield 0.5

        def gen_select(tl):
            P, Lv = tl.P, tl.Lv
            sc, scb = tl.sc, tl.scb
            sm = smals[tl.lane]
            b_sm = b_sms[tl.lane]
            S = sc[0:P, 0:Lv]
            big = Lv / 960.0 + 0.15

            def col(i):
                return sm[0:P, i:i + 1]
            kb.op("vector", lambda e: e.tensor_reduce(out=col(0), in_=S, axis=AX.X, op=ALU.max), r=[scb], w=[b_sm])
            kb.op("vector", lambda e: e.tensor_reduce(out=col(1), in_=S, axis=AX.X, op=ALU.min), r=[scb], w=[b_sm])
            yield 2 * big
            kb.op("vector", lambda e: e.tensor_tensor(out=col(2), in0=col(0), in1=col(1), op=ALU.subtract), r=[b_sm], w=[b_sm])
            kb.op("vector", lambda e: e.tensor_scalar(out=col(2), in0=col(2), scalar1=1e-30, scalar2=None, op0=ALU.max), r=[b_sm], w=[b_sm])
            kb.op("vector", lambda e: e.reciprocal(out=col(3), in_=col(2)), r=[b_sm], w=[b_sm])
            kb.op("vector", lambda e: e.tensor_scalar(out=S, in0=S, scalar1=col(1), scalar2=col(3),
                                                      op0=ALU.subtract, op1=ALU.mult), r=[scb, b_sm], w=[scb])
            if tl.mreg:
                kb.op("vector", lambda e: e.memset(sc[0:64, Lv - 64:Lv], -1.0), w=[scb])
            kb.op("vector", lambda e: e.memset(col(4), 0.0), w=[b_sm])
            yield big + 0.6
            tl.Ms, tl.Mb, _ = Mr.next()
            Ms, Mb = tl.Ms, tl.Mb
            if tl.do_topk and tl.lane == 1:
                bigA = Lv / 1250.0 + 0.2
                for it in range(1, 25):
                    kb.op("scalar", lambda e: e.activation(out=col(5), in_=col(4), func=AF.Identity,
                                                           scale=-(2.0 ** (1 - it)), bias=-(2.0 ** (-it))),
                          r=[b_sm], w=[b_sm])
                    kb.op("scalar", lambda e: e.activation(out=col(9), in_=col(4), func=AF.Identity,
                                                           scale=2.0, bias=0.5), r=[b_sm], w=[b_sm])
                    kb.op("scalar", lambda e: e.activation(out=Ms[0:P, 0:Lv], in_=S, func=AF.Sign, bias=col(5),
                                                           accum_out=col(6)), r=[scb, b_sm], w=[Mb, b_sm])
                    kb.op("scalar", lambda e: e.activation(out=col(7), in_=col(6), func=AF.Sign,
                                                           bias=float(Lv) - (2 * TOPK - 1.5)), r=[b_sm], w=[b_sm])
                    kb.op("scalar", lambda e: e.activation(out=col(4), in_=col(7), func=AF.Identity, scale=0.5,
                                                           bias=col(9)), r=[b_sm], w=[b_sm])
                    yield bigA + 0.8
            elif tl.do_topk:
                for it in range(1, 25):
                    kb.op("vector", lambda e: e.tensor_scalar(
                        out=col(5), in0=col(4), scalar1=2.0 ** (1 - it), scalar2=2.0 ** (-it),
                        op0=ALU.mult, op1=ALU.add), r=[b_sm], w=[b_sm])
                    kb.op("vector", lambda e: e.tensor_scalar(
                        out=Ms[0:P, 0:Lv], in0=S, scalar1=col(5), scalar2=None, op0=ALU.is_ge, op1=ALU.add,
                        accum_out=col(6)), r=[scb, b_sm], w=[Mb, b_sm])
                    kb.op("vector", lambda e: e.tensor_single_scalar(
                        out=col(7), in_=col(6), scalar=TOPK - 0.5, op=ALU.is_ge), r=[b_sm], w=[b_sm])
                    kb.op("vector", lambda e: e.scalar_tensor_tensor(
                        out=col(4), in0=col(4), scalar=2.0, in1=col(7), op0=ALU.mult, op1=ALU.add), r=[b_sm], w=[b_sm])
                    yield big + 0.55
            kb.op("vector", lambda e: e.tensor_scalar(out=col(8), in0=col(4), scalar1=2.0 ** -24, scalar2=None,
                                                      op0=ALU.mult), r=[b_sm], w=[b_sm])
            kb.op("vector", lambda e: e.tensor_scalar(out=Ms[0:P, 0:Lv], in0=S, scalar1=col(8), scalar2=None,
                                                      op0=ALU.is_ge), r=[scb, b_sm], w=[Mb])
            yield big + 0.2

        def gen_transpose(tl, MT, b_MT, c):
            P, Lv = tl.P, tl.Lv
            Ms, Mb = tl.Ms, tl.Mb
            nkt = (Lv + 127) // 128
            for j0 in range(0, nkt, 4):
                ng = min(4, nkt - j0)
                tp, tpb, _ = trp.next()
                kn_last = 128
                for jj in range(ng):
                    j = j0 + jj
                    kn = min(128, Lv - 128 * j)
                    kn_last = kn
                    kb.op("tensor", lambda e: e.transpose(
                        tp[0:kn, jj * 128:jj * 128 + P], Ms[0:P, j * 128:j * 128 + kn], C["idb"][0:P, 0:P]),
                        r=[Mb, cb_], w=[tpb], inc=(jj == ng - 1))
                nfull = ng if kn_last == 128 else ng - 1
                if nfull > 0:
                    kb.op("scalar", lambda e: e.activation(
                        out=MT[:, j0:j0 + nfull, c * 128:c * 128 + P],
                        in_=tp[:, 0:nfull * 128].rearrange("p (a b) -> p a b", b=128)[:, :, 0:P],
                        **(dict(func=AF.Copy) if "A" in DBG else dict(func=AF.Identity, scale=-NEG, bias=NEG))), r=[tpb], w=[b_MT])
                if kn_last != 128:
                    kb.op("scalar", lambda e: e.activation(
                        out=MT[0:kn_last, j0 + ng - 1, c * 128:c * 128 + P],
                        in_=tp[0:kn_last, (ng - 1) * 128:(ng - 1) * 128 + P],
                        **(dict(func=AF.Copy) if "A" in DBG else dict(func=AF.Identity, scale=-NEG, bias=NEG))), r=[tpb], w=[b_MT])
                yield 0.5

        def gen_attend(h, Q, qcol0, tiles, load_k, load_v, MT, b_MT):
            kt, ktb, kts = ktr.next()
            load_k(kt, ktb, kts)
            vt, vtb, vts = vr.next()
            load_v(vt, vtb, vts)
            qt, qtb, qts = qr.next()
            kb.dma("sync", qt[:, 0:Q], D["QT"][h * 128:(h + 1) * 128, qcol0:qcol0 + Q], qts, w=[qtb])
            wb, wbb, wbs = wbr.next()
            kb.dma("sync", wb[:], bass.AP(D["FD"].tensor, h * 1152, [[1, 128], [1, 1024]]), wbs, w=[wbb])
            ag, agb, ags = agr.next()
            kb.dma("sync", ag[:, 0:Q], D["AGS"][h * 128:(h + 1) * 128, qcol0:qcol0 + Q], ags, w=[agb])
            n = len(tiles)
            sts = [None] * n

            def emit_st(j):
                kn, c0, win = tiles[j]
                st_, stb, _ = stp.next()
                sts[j] = (st_, stb)
                kb.op("tensor", lambda e: e.matmul(st_[0:kn, c0:Q], kt[:, j * 128:j * 128 + kn], qt[:, c0:Q],
                                                   start=True, stop=False),
                      r=[ktb, qtb], w=[stb], inc=False)
                if win is not None:
                    off = 128 - kn
                    kb.op("tensor", lambda e: e.matmul(st_[0:kn, c0:Q], C["jf"][0:kn, off:128],
                                                       wb[0:kn, win + off + c0:win + off + Q], start=False, stop=False),
                          r=[cb_, wbb], w=[stb], inc=False)
                if "B" in DBG:
                    kb.op("tensor", lambda e: e.matmul(st_[0:kn, c0:Q], C["idb"][0:kn, 0:kn], qt[0:kn, c0:Q],
                                                       start=False, stop=True), r=[cb_, qtb], w=[stb])
                else:
                    kb.op("tensor", lambda e: e.matmul(st_[0:kn, c0:Q], C["idb"][0:kn, 0:kn], MT[0:kn, j, c0:Q],
                                                       start=False, stop=True), r=[cb_, b_MT], w=[stb])
            emit_st(0)
            for j in range(n):
                kn, c0, win = tiles[j]
                if j + 1 < n:
                    emit_st(j + 1)
                st_, stb = sts[j]
                pt, ptb, _ = ptr.next()
                if win is None:
                    kb.op("scalar", lambda e: e.activation(
                        out=pt[0:kn, c0:Q], in_=st_[0:kn, c0:Q], func=AF.Exp, scale=A_SCALE,
                        bias=C["b15"][0:kn, h:h + 1]), r=[stb, cb_], w=[ptb])
                else:
                    kb.op("scalar", lambda e: e.activation(
                        out=pt[0:kn, c0:Q], in_=st_[0:kn, c0:Q], func=AF.Exp, scale=A_SCALE),
                        r=[stb], w=[ptb])
                if "C" in DBG:
                    kb.op("gpsimd", lambda e: e.tensor_tensor(
                        out=pt[0:kn, c0:Q], in0=pt[0:kn, c0:Q], in1=MT[0:kn, j, c0:Q], op=ALU.mult),
                        r=[ptb, b_MT], w=[ptb])
                kb.op("tensor", lambda e: e.matmul(
                    otp[:, c0:Q], vt[0:kn, j, :], pt[0:kn, c0:Q], start=(j == 0), stop=(j == n - 1)),
                    r=[vtb, ptb], w=[b_otp], inc=(j == n - 1))
                kb.op("tensor", lambda e: e.matmul(
                    smp[:, c0:Q], C["onesb"][0:kn, :], pt[0:kn, c0:Q], start=(j == 0), stop=(j == n - 1)),
                    r=[cb_, ptb], w=[b_smp], inc=(j == n - 1))
                yield (1.1 if win is None else 2.0) * (Q - c0) / 512.0 + 0.05
            kb.op("vector", lambda e: e.reciprocal(out=rsb[:, 0:Q], in_=smp[:, 0:Q]), r=[b_smp], w=[b_rs])
            kb.op("vector", lambda e: e.tensor_tensor(out=tsb[:, 0:Q], in0=otp[:, 0:Q], in1=rsb[:, 0:Q], op=ALU.mult),
                  r=[b_otp, b_rs], w=[b_ts])
            ot, otb, ots = outr.next()
            kb.op("vector", lambda e: e.tensor_tensor(out=ot[:, 0:Q], in0=tsb[:, 0:Q], in1=ag[:, 0:Q], op=ALU.mult),
                  r=[b_ts, agb], w=[otb])
            kb.dma("sync", D["MIXT"][h * 128:(h + 1) * 128, qcol0:qcol0 + Q], ot[:, 0:Q], ots, r=[otb])
            yield 0.3

        def prompt_tiles(qb):
            tiles = []
            for j in range(4 * qb + 4):
                r_ = j - 4 * qb
                tiles.append((128, 128 * max(r_, 0), (128 * (3 - r_)) if r_ >= -1 else None))
            return tiles

        def attn_block(qb):
            Lk = 512 * (qb + 1)
            tiles = prompt_tiles(qb)
            gens = []
            for h in range(A_HEADS):
                def load_k(kt, ktb, kts, h=h, Lk=Lk):
                    kb.dma("sync", kt[:, 0:Lk], D["KT"][h * 128:(h + 1) * 128, 0:Lk], kts, w=[ktb])
                def load_v(vt, vtb, vts, h=h, Lk=Lk):
                    kb.dma("sync", vt[:, 0:Lk // 128, :],
                           D["V"][0:Lk, h * 128:(h + 1) * 128].rearrange("(j p) d -> p j d", p=128), vts, w=[vtb])
                gens.append(gen_attend(h, 512, qb * 512, tiles, load_k, load_v, MTs[qb % 2], b_MTs[qb % 2]))
            return gens

        def index_tiles(nb):
            iq, iqb_, iqs = iqr.next()
            src = D["IQ"][:, nb * 512:(nb + 1) * 512].rearrange("(a two p) t -> two p a t", two=2, p=64)
            kb.dma("sync", iq[0:64, 0, :, :], src[0], iqs, w=[iqb_])
            kb.dma("sync", iq[64:128, 1, :, :], src[1], iqs, w=[iqb_])
            tls = []
            for c in range(4):
                i = nb * 4 + c
                tl = Tile()
                tl.P = 128; tl.Lv = 128 * (i + 1); tl.iksb = ik2; tl.ikb = b_ik; tl.iqbuf = iqb_
                tl.iq_l = (lambda h, iq=iq, c=c: iq[:, h % 2, h // 2, c * 128:(c + 1) * 128])
                tl.iwrows = slice(i * 128, (i + 1) * 128); tl.mreg = True; tl.do_topk = (i >= 2); tl.lane = c % 2
                tls.append(tl)
            return tls

        for it_ in iqr.items:
            kb.op("gpsimd", lambda e: e.memset(it_[0][:], 0.0), w=[it_[1]])
        kb.op("gpsimd", lambda e: e.memset(MTs[0][:], 0.0), w=[b_MTs[0]])
        kb.op("gpsimd", lambda e: e.memset(MTs[1][:], 0.0), w=[b_MTs[1]])
        for step in range(NB + 1):
            nb = step if step < NB else None
            qb = step - 1 if step >= 1 else None
            if "dbgMT" in self.debug and step == 1:
                dm = self.dout("dbgMT", list(MTs[0].shape), BF16)
                kb.dma("sync", dm, MTs[0][:], s_l, r=[b_MTs[0]])
            tls = index_tiles(nb) if nb is not None else None
            agens = attn_block(qb) if qb is not None else None

            def pe_lane():
                if tls is not None:
                    yield from gen_scores(tls[0])
                    yield ("set", ("I1", 0))
                for c in range(4):
                    if tls is not None and c < 3:
                        yield from gen_scores(tls[c + 1])
                        yield ("set", ("I1", c + 1))
                    if agens is not None:
                        yield from agens[2 * c]
                        yield from agens[2 * c + 1]
                    if tls is not None:
                        yield ("wait", ("I2", c))
                        yield from gen_transpose(tls[c], MTs[nb % 2], b_MTs[nb % 2], c)

            def sel_lane(par):
                if tls is not None:
                    for c in (par, par + 2):
                        yield ("wait", ("I1", c))
                        yield from gen_select(tls[c])
                        yield ("set", ("I2", c))
            run_lanes([pe_lane(), sel_lane(0), sel_lane(1)])

        iq, iqb_, iqs = iqr.next()
        src = D["IQ"][:, TP:T].rearrange("(a two p) t -> two p a t", two=2, p=64)
        kb.dma("sync", iq[0:64, 0, :, 0:64], src[0], iqs, w=[iqb_])
        kb.dma("sync", iq[64:128, 1, :, 0:64], src[1], iqs, w=[iqb_])
        tl = Tile()
        tl.P = 64; tl.Lv = LS; tl.iksb = ik2s; tl.ikb = b_iks; tl.iqbuf = iqb_
        tl.iq_l = (lambda h, iq=iq: iq[:, h % 2, h // 2, 0:64])
        tl.iwrows = slice(TP, T); tl.mreg = False; tl.do_topk = True; tl.lane = 0
        MTx, b_MTx = MTs[NB % 2], b_MTs[NB % 2]
        for g_ in (gen_scores(tl), gen_select(tl), gen_transpose(tl, MTx, b_MTx, 0)):
            for _ in g_:
                pass
        tiles = [(128, 0, 512 if j == 7 else None) for j in range(8)] + [(64, 0, 384)]
        for h in range(A_HEADS):
            def load_k(kt, ktb, kts, h=h):
                kb.dma("gpsimd", kt[:, 0:PAST], D["ckT"][h * 128:(h + 1) * 128, :], kts, w=[ktb], max_dma_last_dim=4096)
                kb.dma("sync", kt[:, PAST:LS], D["KT"][h * 128:(h + 1) * 128, TP:T], kts, w=[ktb])
            def load_v(vt, vtb, vts, h=h):
                kb.dma("gpsimd", vt[:, 0:8, :],
                       D["cv"][:, h * 128:(h + 1) * 128].rearrange("(j p) d -> p j d", p=128), vts, w=[vtb])
                kb.dma("sync", vt[0:64, 8, :], D["V"][TP:T, h * 128:(h + 1) * 128], vts, w=[vtb])
            for _ in gen_attend(h, 64, TP, tiles, load_k, load_v, MTx, b_MTx):
                pass
        self.end()

    def phase3(self):
        K = self
        kb, nc, D, DB, C = self.kb, self.nc, self.D, self.DB, self.C
        NB, TP, T = self.NB, self.TP, self.TALL
        cb_ = self.CB
        self.begin()
        sl = self.sem()
        pc = Buf("p3const")
        dtb = self.sb("dtb", [128, 32], F32); abc = self.sb("abc", [128, 32], F32)
        dsk = self.sb("dsk", [128, 16], F32); nw = self.sb("nw", [128, 16], F32)
        U = self.sb("U", [128, 128], F32); negm = self.sb("negm_sb", [128, 128], F32)
        NEGb = self.sb("NEGb", [128, 1024], F32)
        kb.dma("sync", dtb[:], D["dtbias"].broadcast_to([128, 32]), sl, w=[pc])
        kb.dma("sync", abc[:], D["alog"].broadcast_to([128, 32]), sl, w=[pc])
        kb.dma("sync", dsk[:], D["dskT"], sl, w=[pc])
        kb.dma("sync", nw[:], D["nwT"], sl, w=[pc])
        kb.dma("sync", U[:], D["utri"], sl, w=[pc])
        kb.dma("sync", negm[:], D["negm"], sl, w=[pc])
        kb.op("scalar", lambda e: e.activation(out=abc[:], in_=abc[:], func=AF.Exp), r=[pc], w=[pc])
        kb.op("vector", lambda e: e.tensor_scalar(out=abc[:], in0=abc[:], scalar1=-1.0, scalar2=None, op0=ALU.mult),
              r=[pc], w=[pc])
        hT = self.sb("hT", [128, 2048], F32); b_hT = Buf("hT")
        hTb = self.sb("hTb", [128, 2048], BF16); b_hTb = Buf("hTb")
        xcr = Ring(K, "xcb", [128, 24, 128], BF16, 2)
        xsr = Ring(K, "xsf", [128, 16, 128], F32, 2)
        zsr = Ring(K, "zsf", [128, 16, 128], F32, 2)
        dtr_ = Ring(K, "dtr", [128, 32], F32, 2)
        xtkr = Ring(K, "xtok", [128, 2560], BF16, 2, dma=False)
        smr = Ring(K, "sm3", [128, 8, 32], F32, 2, dma=False)
        xdtr = Ring(K, "xdt", [128, 2048], BF16, 2, dma=False)
        xdt2r = Ring(K, "xdt2", [128, 2048], BF16, 2, dma=False)
        decr = Ring(K, "dec", [128, 32], F32, 2, dma=False)
        Dpr = Ring(K, "Dp", [128, 1024], F32, 2, dma=False)
        exr = Ring(K, "expA", [128, 1024], F32, 2, dma=False)
        sgr = Ring(K, "seg", [128, 1024], F32, 2, dma=False)
        wtr = Ring(K, "Wt", [128, 1024], BF16, 4, dma=False)
        cpr = Ring(K, "CpT", [128, 1024], BF16, 4, dma=False)
        ygt = Ring(K, "ygt", [128, 128], F32, 2, dma=False)
        yg = self.sb("yg", [128, 16, 128], F32); b_yg = Buf("yg")
        sqr = Ring(K, "sqt", [128, 128], F32, 2, dma=False)
        sd = self.sb("sd", [128, 128], F32); b_sd = Buf("sd")
        rstd = self.sb("rstd", [128, 128], F32); b_rstd = Buf("rstd")
        bor = Ring(K, "bo", [128, 128], BF16, 4)
        acr = Ring(K, "acb", [128, 1024], F32, 2, dma=False, psum=True)
        cbp = self.ps("cbp", [128, 512], F32); b_cbp = Buf("cbp", psum=True)
        ytr = Ring(K, "yT", [128, 512], F32, 1, dma=False, psum=True)
        hup = self.ps("hup", [128, 512], F32); b_hup = Buf("hup", psum=True)
        trp = hup[:].bitcast(BF16); b_trp = b_hup
        msc = self.ps("msc", [128, 512], F32); b_msc = Buf("msc", psum=True)
        kb.op("gpsimd", lambda e: e.tensor_copy(
            out=NEGb[:].rearrange("p (r t) -> p r t", t=128),
            in_=negm[:].unsqueeze(1).broadcast_to([128, 8, 128])), r=[pc], w=[pc])

        def v3(ap, L):
            return ap.rearrange("p (r t) -> p r t", t=L)

        class Cx:
            pass

        def prologue(t0, L, have_state):
            cx = Cx()
            cx.t0, cx.L, cx.have_state = t0, L, have_state
            xc, xcb_, xcs = xcr.next()
            kb.dma("gpsimd", xc[:, :, 0:L], D["XC"][:, t0:t0 + L].rearrange("(a p) t -> p a t", p=128), xcs, w=[xcb_])
            xs, xsb_, xss = xsr.next()
            kb.dma("sync", xs[:, :, 0:L], D["XC"][0:2048, t0:t0 + L].rearrange("(a p) t -> p a t", p=128), xss, w=[xsb_])
            zs, zsb_, zss = zsr.next()
            kb.dma("sync", zs[:, :, 0:L], D["ZS"][:, t0:t0 + L].rearrange("(a p) t -> p a t", p=128), zss, w=[zsb_])
            dr, drb, drs = dtr_.next()
            kb.dma("sync", dr[0:L, :], D["IWDT"][t0:t0 + L, 16:48], drs, w=[drb])
            xtk, b_xtk, _ = xtkr.next()
            sm, b_sm, _ = smr.next()
            xdt, b_xdt, _ = xdtr.next()
            xdt2, b_xdt2, _ = xdt2r.next()
            dec, b_dec, _ = decr.next()
            cx.xc, cx.xcb_, cx.xs, cx.xsb_, cx.zs, cx.zsb_ = xc, xcb_, xs, xsb_, zs, zsb_
            cx.xtk, cx.b_xtk, cx.sm, cx.b_sm, cx.xdt, cx.b_xdt = xtk, b_xtk, sm, b_sm, xdt, b_xdt
            cx.xdt2, cx.b_xdt2, cx.dec, cx.b_dec = xdt2, b_xdt2, dec, b_dec
            for m0 in range(0, 20, 8):
                ng = min(8, 20 - m0)
                for mm in range(ng):
                    m = m0 + mm
                    kb.op("tensor", lambda e: e.transpose(
                        trp[0:L, mm * 128:(mm + 1) * 128], xc[:, m, 0:L], C["idb"][:, :]),
                        r=[xcb_, cb_], w=[b_trp], inc=(mm == ng - 1))
                kb.op("scalar", lambda e: e.activation(
                    out=xtk[0:L, m0 * 128:(m0 + ng) * 128], in_=trp[0:L, 0:ng * 128], func=AF.Copy),
                    r=[b_trp], w=[b_xtk])
            dt_ = sm[0:L, 0, :]; dA = sm[0:L, 1, :]; act = sm[0:L, 2, :]; tl = sm[0:L, 3, :]; tmp = sm[0:L, 4, :]
            cx.dA, cx.act = dA, act
            kb.op("vector", lambda e: e.tensor_tensor(out=tmp, in0=dr[0:L, :], in1=dtb[0:L, :], op=ALU.add),
                  r=[drb, pc], w=[b_sm])
            kb.op("scalar", lambda e: e.activation(out=tmp, in_=tmp, func=AF.Exp), r=[b_sm], w=[b_sm])
            kb.op("scalar", lambda e: e.activation(out=dt_, in_=tmp, func=AF.Ln, bias=1.0), r=[b_sm], w=[b_sm])
            kb.op("vector", lambda e: e.tensor_tensor(out=dA, in0=dt_, in1=abc[0:L, :], op=ALU.mult), r=[b_sm, pc], w=[b_sm])
            kb.op("tensor", lambda e: e.matmul(msc[0:L, 0:32], U[0:L, 0:L], dA, start=True, stop=True),
                  r=[pc, b_sm], w=[b_msc])
            kb.op("tensor", lambda e: e.matmul(msc[:, 32:64], C["onesf"][0:L, :], dA, start=True, stop=True),
                  r=[cb_, b_sm], w=[b_msc])
            kb.op("vector", lambda e: e.tensor_copy(out=act, in_=msc[0:L, 0:32]), r=[b_msc], w=[b_sm])
            kb.op("vector", lambda e: e.tensor_tensor(out=tl, in0=msc[0:L, 32:64], in1=act, op=ALU.subtract),
                  r=[b_msc, b_sm], w=[b_sm])
            kb.op("scalar", lambda e: e.activation(out=tl, in_=tl, func=AF.Exp), r=[b_sm], w=[b_sm])
            kb.op("scalar", lambda e: e.activation(out=dec[:], in_=msc[:, 32:64], func=AF.Exp), r=[b_msc], w=[b_dec])
            kb.op("vector", lambda e: e.tensor_tensor(
                out=xdt[0:L, :].rearrange("p (h q) -> p h q", q=64),
                in0=xtk[0:L, 0:2048].rearrange("p (h q) -> p h q", q=64),
                in1=dt_.unsqueeze(2).broadcast_to([L, 32, 64]), op=ALU.mult), r=[b_xtk, b_sm], w=[b_xdt])
            kb.op("gpsimd", lambda e: e.tensor_tensor(
                out=xdt2[0:L, :].rearrange("p (h q) -> p h q", q=64),
                in0=xdt[0:L, :].rearrange("p (h q) -> p h q", q=64),
                in1=tl.unsqueeze(2).broadcast_to([L, 32, 64]), op=ALU.mult), r=[b_xdt, b_sm], w=[b_xdt2])
            def emit_cb():
                for g in range(4):
                    kb.op("tensor", lambda e: e.matmul(cbp[0:L, g * 128:g * 128 + L], xc[:, 16 + g, 0:L],
                                                       xc[:, 20 + g, 0:L], start=True, stop=True),
                          r=[xcb_], w=[b_cbp], inc=(g == 3))
            cx.emit_cb = emit_cb
            cx.front = {}
            cx.f1 = {}
            return cx

        def front(cx, g):
            L, have_state = cx.L, cx.have_state
            W8 = 8 * L
            dA, act, xc, xcb_, b_sm = cx.dA, cx.act, cx.xc, cx.xcb_, cx.b_sm
            Dp, Dpb, _ = Dpr.next()
            kb.op("gpsimd", lambda e: e.tensor_tensor(
                out=v3(Dp[0:L, 0:W8], L), in0=dA[:, 8 * g:8 * g + 8].unsqueeze(2).broadcast_to([L, 8, L]),
                in1=U[0:L, 0:L].unsqueeze(1).broadcast_to([L, 8, L]), op=ALU.mult), r=[b_sm, pc], w=[Dpb])
            yield 1.0
            acb, b_acb, _ = acr.next()
            nmm = (W8 + 511) // 512
            for i in range(nmm):
                kb.op("tensor", lambda e: e.matmul(
                    acb[:, i * 512:(i + 1) * 512], C["onesf"][0:L, :], Dp[0:L, i * 512:(i + 1) * 512],
                    start=True, stop=True), r=[cb_, Dpb], w=[b_acb], inc=(i == nmm - 1))
            yield 1.0
            ex, exb, _ = exr.next()
            kb.op("scalar", lambda e: e.activation(out=ex[:, 0:W8], in_=acb[:, 0:W8], func=AF.Exp),
                  r=[b_acb], w=[exb])
            yield 1.0
            for i in range(nmm):
                if L == 128:
                    rhs = NEGb[0:L, i * 512:(i + 1) * 512]
                else:
                    rhs = NEGb[0:L, :].rearrange("p (r t) -> p r t", t=128)[:, :, 0:L]
                kb.op("tensor", lambda e: e.matmul(
                    acb[0:L, i * 512:(i + 1) * 512], C["idf"][0:L, 0:L], rhs,
                    start=False, stop=True, skip_group_check=True), r=[cb_, pc], w=[b_acb], inc=(i == nmm - 1))
            cx.f1[g] = (acb, b_acb, ex, exb)
            yield 1.0

        def front2(cx, g):
            L, have_state = cx.L, cx.have_state
            W8 = 8 * L
            dA, act, xc, xcb_, b_sm = cx.dA, cx.act, cx.xc, cx.xcb_, cx.b_sm
            acb, b_acb, ex, exb = cx.f1[g]
            sg, sgb, _ = sgr.next()
            kb.op("vector", lambda e: e.tensor_tensor(
                out=v3(sg[0:L, 0:W8], L), in0=v3(acb[0:L, 0:W8], L),
                in1=act[:, 8 * g:8 * g + 8].unsqueeze(2).broadcast_to([L, 8, L]), op=ALU.subtract),
                r=[b_acb, b_sm], w=[sgb])
            yield 1.0
            kb.op("scalar", lambda e: e.activation(out=sg[0:L, 0:W8], in_=sg[0:L, 0:W8], func=AF.Exp),
                  r=[sgb], w=[sgb])
            yield 1.0
            wt, wtb, _ = wtr.next()
            kb.op("vector", lambda e: e.tensor_tensor(
                out=v3(wt[0:L, 0:W8], L), in0=v3(sg[0:L, 0:W8], L),
                in1=cbp[0:L, g * 128:g * 128 + L].unsqueeze(1).broadcast_to([L, 8, L]), op=ALU.mult),
                r=[sgb, b_cbp], w=[wtb])
            cp = cpb = None
            if have_state:
                cp, cpb, _ = cpr.next()
                kb.op("gpsimd", lambda e: e.tensor_tensor(
                    out=v3(cp[:, 0:W8], L), in0=xc[:, 20 + g, 0:L].unsqueeze(1).broadcast_to([128, 8, L]),
                    in1=v3(ex[:, 0:W8], L), op=ALU.mult), r=[xcb_, exb], w=[cpb])
            cx.front[g] = (wt, wtb, cp, cpb)
            yield 1.0

        def back(cx, g):
            L, have_state, t0 = cx.L, cx.have_state, cx.t0
            wt, wtb, cp, cpb = cx.front[g]
            xdt, b_xdt, xdt2, b_xdt2, xtk, b_xtk = cx.xdt, cx.b_xdt, cx.xdt2, cx.b_xdt2, cx.xtk, cx.b_xtk
            xs, xsb_, zs, zsb_, dec, b_dec = cx.xs, cx.xsb_, cx.zs, cx.zsb_, cx.dec, cx.b_dec
            yT, yTb, _ = ytr.next()
            for r_ in range(8):
                h = 8 * g + r_
                ml = r_ // 2
                po = 64 * (h % 2)
                kb.op("tensor", lambda e: e.matmul(
                    yT[po:po + 64, ml * 128:ml * 128 + L], xdt[0:L, h * 64:(h + 1) * 64],
                    wt[0:L, r_ * L:(r_ + 1) * L], start=True, stop=(not have_state)),
                    r=[b_xdt, wtb], w=[yTb], inc=(r_ == 7 and not have_state))
                if have_state:
                    kb.op("tensor", lambda e: e.matmul(
                        yT[po:po + 64, ml * 128:ml * 128 + L], hTb[:, h * 64:(h + 1) * 64],
                        cp[:, r_ * L:(r_ + 1) * L], start=False, stop=True),
                        r=[b_hTb, cpb], w=[yTb], inc=(r_ == 7))
                if r_ % 2 == 1:
                    yield 0.5
            kb.op("tensor", lambda e: e.matmul(hup[:, :], xtk[0:L, 2048 + g * 128:2048 + (g + 1) * 128],
                                               xdt2[0:L, g * 512:(g + 1) * 512], start=True, stop=True),
                  r=[b_xtk, b_xdt2], w=[b_hup])
            yield 0.5
            hs = hT[:, g * 512:(g + 1) * 512]
            if have_state:
                kb.op("vector", lambda e: e.tensor_tensor(
                    out=hs.rearrange("p (r q) -> p r q", q=64), in0=hs.rearrange("p (r q) -> p r q", q=64),
                    in1=dec[:, 8 * g:8 * g + 8].unsqueeze(2).broadcast_to([128, 8, 64]), op=ALU.mult),
                    r=[b_dec, b_hT], w=[b_hT])
                kb.op("vector", lambda e: e.tensor_tensor(out=hs, in0=hup[:, :], in1=hs, op=ALU.add),
                      r=[b_hup, b_hT], w=[b_hT])
            else:
                kb.op("vector", lambda e: e.tensor_copy(out=hs, in_=hup[:, :]), r=[b_hup], w=[b_hT])
            kb.op("scalar", lambda e: e.activation(out=hTb[:, g * 512:(g + 1) * 512], in_=hs, func=AF.Copy),
                  r=[b_hT], w=[b_hTb])
            yield 0.7
            for ml in range(4):
                m = 4 * g + ml
                yt, ytb, _ = ygt.next()
                kb.op("vector", lambda e: e.scalar_tensor_tensor(
                    out=yt[:, 0:L], in0=xs[:, m, 0:L], scalar=dsk[:, m:m + 1], in1=yT[:, ml * 128:ml * 128 + L],
                    op0=ALU.mult, op1=ALU.add), r=[xsb_, pc, yTb], w=[ytb])
                kb.op("gpsimd", lambda e: e.tensor_tensor(
                    out=yg[:, m, 0:L], in0=yt[:, 0:L], in1=zs[:, m, 0:L], op=ALU.mult), r=[ytb, zsb_], w=[b_yg])
                sq, sqb, _ = sqr.next()
                kb.op("scalar", lambda e: e.activation(out=sq[:, 0:L], in_=yg[:, m, 0:L], func=AF.Square),
                      r=[b_yg], w=[sqb])
                kb.op("tensor", lambda e: e.matmul(msc[:, 128:128 + L], C["onesf"][:, :], sq[:, 0:L],
                                                   start=(ml == 0), stop=(ml == 3)),
                      r=[cb_, sqb], w=[b_msc])
                yield 0.7
            kb.op("scalar", lambda e: e.activation(out=sd[:, 0:L], in_=msc[:, 128:128 + L], func=AF.Ln,
                                                   scale=1.0 / 512.0, bias=EPS), r=[b_msc], w=[b_sd])
            kb.op("scalar", lambda e: e.activation(out=rstd[:, 0:L], in_=sd[:, 0:L], func=AF.Exp, scale=-0.5),
                  r=[b_sd], w=[b_rstd])
            yield 0.7
            for ml in range(4):
                m = 4 * g + ml
                bo, bob, bos = bor.next()
                kb.op("vector", lambda e: e.scalar_tensor_tensor(
                    out=bo[:, 0:L], in0=yg[:, m, 0:L], scalar=nw[:, m:m + 1], in1=rstd[:, 0:L],
                    op0=ALU.mult, op1=ALU.mult), r=[b_yg, pc, b_rstd], w=[bob])
                kb.dma("sync", D["MIXT"][1024 + m * 128:1024 + (m + 1) * 128, t0:t0 + L], bo[:, 0:L], bos, r=[bob])
                yield 0.3

        so = self.sem()
        chunks = [(c * 128, 128, c > 0) for c in range(TP // 128)]
        items = [(ci, g) for ci in range(len(chunks)) for g in range(4)]
        cxs = {}
        LAG = 2

        LAG = 3
        for k in range(len(items) + LAG):
            lanes = []
            if k < len(items):
                ci, g = items[k]
                if g == 0:
                    cxs[ci] = prologue(*chunks[ci])
                lanes.append(front(cxs[ci], g))
            if 0 <= k - 1 < len(items):
                ci, g = items[k - 1]
                if g == 0:
                    cxs[ci].emit_cb()
                lanes.append(front2(cxs[ci], g))
            if k - LAG >= 0:
                ci, g = items[k - LAG]
                lanes.append(back(cxs[ci], g))
            run_lanes(lanes)
        kb.dma("sync", D["o_ssmT"][0], hT[:], so, r=[b_hT])
        kb.dma("sync", hT[:], D["h0T"], so, w=[b_hT])
        kb.op("scalar", lambda e: e.activation(out=hTb[:], in_=hT[:], func=AF.Copy), r=[b_hT], w=[b_hTb])
        cxs_ = prologue(TP, DEC_SEQ, True)
        for k in range(4 + LAG):
            lanes = []
            if k < 4:
                lanes.append(front(cxs_, k))
            if 0 <= k - 1 < 4:
                if k - 1 == 0:
                    cxs_.emit_cb()
                lanes.append(front2(cxs_, k - 1))
            if k - LAG >= 0:
                lanes.append(back(cxs_, k - LAG))
            run_lanes(lanes)
        kb.dma("sync", D["o_ssmT"][1], hT[:], so, r=[b_hT])
        self.end()

    def outproj_ln(self, mixname, KC, wname, resid, gname, bname, dest):
        K = self
        kb, nc, D, DB, C = self.kb, self.nc, self.D, self.DB, self.C
        cb_ = self.CB
        self.begin()
        sl = self.sem()
        pc = Buf("opc")
        w = self.sb("wo", [128, KC, 2048], BF16)
        for kc in range(KC):
            kb.dma("gpsimd", w[:, kc, :], D[wname][:, kc, :], sl, w=[pc], max_dma_last_dim=4096)
        g = self.sb("lng", [128, 16], F32); bb = self.sb("lnb", [128, 16], F32)
        kb.dma("sync", g[:], D[gname], sl, w=[pc]); kb.dma("sync", bb[:], D[bname], sl, w=[pc])
        mxr = Ring(K, "mx", [128, KC, 512], BF16, 2)
        rsr = Ring(K, "rs", [128, 512], F32, 3)
        z = self.sb("z", [128, 16, 512], F32); zb = [Buf("z%d" % i) for i in range(16)]
        sqr = Ring(K, "sq", [128, 512], F32, 2, dma=False)
        sd = self.sb("sd5", [128, 512], F32); b_sd = Buf("sd5")
        tr_ = Ring(K, "t5t", [128, 512], F32, 2, dma=False)
        outr = Ring(K, "o5", [128, 512], F32, 3)
        psr = Ring(K, "ps5", [128, 512], F32, 4, dma=False, psum=True)
        mean = self.ps("mean5", [128, 512], F32); b_mean = Buf("mean5", psum=True)
        var = self.ps("var5", [128, 512], F32); b_var = Buf("var5", psum=True)
        def L1(t0, tn, m, mx, mxb):
            rs, rsb_, rss = rsr.next()
            kb.dma("sync", rs[:, 0:tn], D[resid][m * 128:(m + 1) * 128, t0:t0 + tn], rss, w=[rsb_])
            ps, pb, _ = psr.next()
            for kc in range(KC):
                kb.op("tensor", lambda e: e.matmul(
                    ps[:, 0:tn], w[:, kc, m * 128:(m + 1) * 128], mx[:, kc, 0:tn],
                    start=(kc == 0), stop=(kc == KC - 1)), r=[pc, mxb], w=[pb], inc=(kc == KC - 1))
            kb.op("vector", lambda e: e.scalar_tensor_tensor(
                out=z[:, m, 0:tn], in0=rs[:, 0:tn], scalar=ALPHA, in1=ps[:, 0:tn], op0=ALU.mult, op1=ALU.add),
                r=[rsb_, pb], w=[zb[m]])
            kb.op("tensor", lambda e: e.matmul(mean[:, 0:tn], C["onesf"][:, :], z[:, m, 0:tn],
                                               start=(m == 0), stop=(m == 15)), r=[cb_, zb[m]], w=[b_mean])

        def L2(t0, tn):
            for m in range(16):
                kb.op("vector", lambda e: e.scalar_tensor_tensor(
                    out=z[:, m, 0:tn], in0=mean[:, 0:tn], scalar=-1.0 / 2048.0, in1=z[:, m, 0:tn],
                    op0=ALU.mult, op1=ALU.add), r=[b_mean, zb[m]], w=[zb[m]])
                sq, sqb, _ = sqr.next()
                kb.op("scalar", lambda e: e.activation(out=sq[:, 0:tn], in_=z[:, m, 0:tn], func=AF.Square),
                      r=[zb[m]], w=[sqb])
                kb.op("tensor", lambda e: e.matmul(var[:, 0:tn], C["onesf"][:, :], sq[:, 0:tn],
                                                   start=(m == 0), stop=(m == 15)), r=[cb_, sqb], w=[b_var])
            rd, rdb, _ = rstdr.next()
            kb.op("scalar", lambda e: e.activation(out=sd[:, 0:tn], in_=var[:, 0:tn], func=AF.Sqrt,
                                                   scale=1.0 / 2048.0, bias=EPS), r=[b_var], w=[b_sd])
            kb.op("vector", lambda e: e.reciprocal(out=rd[:, 0:tn], in_=sd[:, 0:tn]), r=[b_sd], w=[rdb])
            return rd, rdb

        def L3(t0, tn, m, rd, rdb):
            tt, ttb, _ = tr_.next()
            kb.op("vector", lambda e: e.scalar_tensor_tensor(
                out=tt[:, 0:tn], in0=z[:, m, 0:tn], scalar=g[:, m:m + 1], in1=rd[:, 0:tn],
                op0=ALU.mult, op1=ALU.mult), r=[zb[m], pc, rdb], w=[ttb])
            ot, otb, ots = outr.next()
            kb.op("scalar", lambda e: e.activation(
                out=ot[:, 0:tn], in_=tt[:, 0:tn], func=AF.Identity, bias=bb[:, m:m + 1]), r=[ttb, pc], w=[otb])
            kb.dma("sync", D[dest][m * 128:(m + 1) * 128, t0:t0 + tn], ot[:, 0:tn], ots, r=[otb])

        rstdr = Ring(K, "rstdr", [128, 512], F32, 2, dma=False)
        nblk = len(self.blocks)
        prev = None
        for bi in range(nblk + 1):
            cur = None
            if bi < nblk:
                t0, tn = self.blocks[bi]
                mx, mxb, mxs = mxr.next()
                kb.dma("sync", mx[:, :, 0:tn], D[mixname][:, t0:t0 + tn].rearrange("(a p) t -> p a t", p=128), mxs, w=[mxb])
                cur = (t0, tn, mx, mxb)
            for m in range(16):
                if prev is not None:
                    L3(prev[0], prev[1], m, prev[2], prev[3])
                if cur is not None:
                    L1(cur[0], cur[1], m, cur[2], cur[3])
            prev = None
            if cur is not None:
                rd, rdb = L2(cur[0], cur[1])
                prev = (cur[0], cur[1], rd, rdb)
        self.end()

    def phase6(self):
        K = self
        kb, nc, D, DB, C = self.kb, self.nc, self.D, self.DB, self.C
        cb_ = self.CB
        T = self.TALL
        self.begin()
        sl = self.sem()
        pc = Buf("p6c")
        ysb = self.sb("ysb", [128, 16, T], BF16); b_y = [Buf("y%d" % k) for k in range(16)]
        for kc in range(16):
            kb.dma("gpsimd", ysb[:, kc, :], D["Y0T"][kc * 128:(kc + 1) * 128, :], sl, w=[b_y[kc]], max_dma_last_dim=4096)
        wn = self.sb("wn", [128, 8, 16, 128], BF16)
        for i in range(8):
            kb.dma("gpsimd", wn[:, i], D["w1fm"][i], sl, w=[pc], max_dma_last_dim=4096)
        wkr = self.sb("wkr", [128, 2, 16, 64], BF16)
        for i in range(2):
            kb.dma("gpsimd", wkr[:, i], D["w1kr"][i], sl, w=[pc], max_dma_last_dim=4096)
        qnw = self.sb("qnw_sb", [128, 4], F32); kvnw = self.sb("kvnw_sb", [128, 4], F32)
        kb.dma("sync", qnw[:], D["qnw"], sl, w=[pc]); kb.dma("sync", kvnw[:], D["kvnw"], sl, w=[pc])
        psr = Ring(K, "ps6", [128, 512], F32, 4, dma=False, psum=True)
        ssp = self.ps("ss6", [128, 512], F32); b_ss = Buf("ss6", psum=True)
        raw = self.sb("raw6", [128, 4, 512], F32); rawb = [Buf("raw%d" % i) for i in range(4)]
        sqr = Ring(K, "sq6", [128, 512], F32, 2, dma=False)
        sd = self.sb("sd6", [128, 512], F32); b_sd = Buf("sd6")
        rstd = self.sb("rstd6", [128, 512], F32); b_rstd = Buf("rstd6")
        o32 = Ring(K, "o632", [128, 512], F32, 3)
        o16 = Ring(K, "o616", [128, 512], BF16, 3)
        csr = Ring(K, "cs6", [64, 2, 512], F32, 2)
        t1 = self.sb("t16", [64, 512], F32); b_t1 = Buf("t16")
        t2 = self.sb("t26", [64, 512], F32); b_t2 = Buf("t26")

        def mm(ps, pb, lhs_fn, t0, tn, M=128):
            for kc in range(16):
                kb.op("tensor", lambda e, kc=kc: e.matmul(ps[0:M, 0:tn], lhs_fn(kc), ysb[:, kc, t0:t0 + tn],
                                                          start=(kc == 0), stop=(kc == 15)),
                      r=[pc, b_y[kc]], w=[pb], inc=(kc == 15))

        for (t0, tn) in self.blocks:
            for grp in range(2):
                for i in range(4):
                    ps, pb, _ = psr.next()
                    mm(ps, pb, lambda kc, i=i, grp=grp: wn[:, grp * 4 + i, kc, :], t0, tn)
                    kb.op("scalar", lambda e, ps=ps, i=i: e.activation(out=raw[:, i, 0:tn], in_=ps[:, 0:tn], func=AF.Copy),
                          r=[pb], w=[rawb[i]])
                    sq, sqb, _ = sqr.next()
                    kb.op("scalar", lambda e, sq=sq, i=i: e.activation(out=sq[:, 0:tn], in_=raw[:, i, 0:tn], func=AF.Square),
                          r=[rawb[i]], w=[sqb])
                    kb.op("tensor", lambda e, sq=sq, i=i: e.matmul(ssp[:, 0:tn], C["onesf"][:, :], sq[:, 0:tn],
                                                                   start=(i == 0), stop=(i == 3)), r=[cb_, sqb], w=[b_ss])
                kb.op("scalar", lambda e: e.activation(out=sd[:, 0:tn], in_=ssp[:, 0:tn], func=AF.Sqrt,
                                                       scale=1.0 / 512.0, bias=EPS), r=[b_ss], w=[b_sd])
                kb.op("vector", lambda e: e.reciprocal(out=rstd[:, 0:tn], in_=sd[:, 0:tn]), r=[b_sd], w=[b_rstd])
                nwt = qnw if grp == 0 else kvnw
                for i in range(4):
                    rows = slice(i * 128, (i + 1) * 128)
                    if grp == 0:
                        ob, obb, obs = o16.next()
                        kb.op("vector", lambda e, ob=ob, i=i, nwt=nwt: e.scalar_tensor_tensor(
                            out=ob[:, 0:tn], in0=raw[:, i, 0:tn], scalar=nwt[:, i:i + 1], in1=rstd[:, 0:tn],
                            op0=ALU.mult, op1=ALU.mult), r=[rawb[i], pc, b_rstd], w=[obb])
                        kb.dma("sync", D["CQN"][rows, t0:t0 + tn], ob[:, 0:tn], obs, r=[obb])
                    else:
                        of, ofb, ofs = o32.next()
                        kb.op("vector", lambda e, of=of, i=i, nwt=nwt: e.scalar_tensor_tensor(
                            out=of[:, 0:tn], in0=raw[:, i, 0:tn], scalar=nwt[:, i:i + 1], in1=rstd[:, 0:tn],
                            op0=ALU.mult, op1=ALU.mult), r=[rawb[i], pc, b_rstd], w=[ofb])
                        kb.dma("sync", D["o_clatT"][rows, t0:t0 + tn], of[:, 0:tn], ofs, r=[ofb])
                        ob, obb, obs = o16.next()
                        kb.op("scalar", lambda e, ob=ob, of=of: e.activation(out=ob[:, 0:tn], in_=of[:, 0:tn], func=AF.Copy),
                              r=[ofb], w=[obb])
                        kb.dma("sync", D["LAT"][rows, t0:t0 + tn], ob[:, 0:tn], obs, r=[obb])
            cs, csb, css = csr.next()
            kb.dma("sync", cs[:, 0, 0:tn], D["rcos"][:, t0:t0 + tn], css, w=[csb])
            kb.dma("sync", cs[:, 1, 0:tn], D["rsin"][:, t0:t0 + tn], css, w=[csb])
            pa, pab, _ = psr.next()
            mm(pa, pab, lambda kc: wkr[:, 0, kc, :], t0, tn, M=64)
            pbb_, pbbb, _ = psr.next()
            mm(pbb_, pbbb, lambda kc: wkr[:, 1, kc, :], t0, tn, M=64)
            kb.op("vector", lambda e, pa=pa, cs=cs: e.tensor_tensor(out=t1[:, 0:tn], in0=pa[0:64, 0:tn], in1=cs[:, 0, 0:tn], op=ALU.mult),
                  r=[pab, csb], w=[b_t1])
            kb.op("vector", lambda e, pbb_=pbb_, cs=cs: e.tensor_tensor(out=t2[:, 0:tn], in0=pbb_[0:64, 0:tn], in1=cs[:, 1, 0:tn], op=ALU.mult),
                  r=[pbbb, csb], w=[b_t2])
            of, ofb, ofs = o32.next()
            kb.op("gpsimd", lambda e, of=of: e.tensor_tensor(out=of[0:64, 0:tn], in0=t1[:, 0:tn], in1=t2[:, 0:tn], op=ALU.add),
                  r=[b_t1, b_t2], w=[ofb])
            kb.dma("sync", D["o_ckrT"][:, t0:t0 + tn], of[0:64, 0:tn], ofs, r=[ofb])
            ob, obb, obs = o16.next()
            kb.op("scalar", lambda e, ob=ob, of=of: e.activation(out=ob[0:64, 0:tn], in_=of[0:64, 0:tn], func=AF.Copy),
                  r=[ofb], w=[obb])
            kb.dma("sync", D["KR"][:, t0:t0 + tn], ob[0:64, 0:tn], obs, r=[obb])
        self.end()
        self.begin()
        sl = self.sem()
        ysb = self.sb("ysb", [128, 16, T], BF16); b_y = [Buf("y%d" % k) for k in range(16)]
        for kc in range(16):
            kb.dma("gpsimd", ysb[:, kc, :], D["Y0T"][kc * 128:(kc + 1) * 128, :], sl, w=[b_y[kc]], max_dma_last_dim=4096)
        psr = Ring(K, "ps6", [128, 512], F32, 4, dma=False, psum=True)
        o32 = Ring(K, "o632", [128, 512], F32, 3)
        wr = Ring(K, "wg6", [128, 16, 128], BF16, 3)
        for gi in range(16):
            wt, wb, ws = wr.next()
            kb.dma("gpsimd", wt[:], D["w1fm"][8 + gi], ws, w=[wb], max_dma_last_dim=4096)
            for (t0, tn) in self.blocks:
                ps, pb, _ = psr.next()
                for kc in range(16):
                    kb.op("tensor", lambda e, ps=ps, kc=kc, wt=wt: e.matmul(
                        ps[:, 0:tn], wt[:, kc, :], ysb[:, kc, t0:t0 + tn], start=(kc == 0), stop=(kc == 15)),
                        r=[wb, b_y[kc]], w=[pb], inc=(kc == 15))
                of, ofb, ofs = o32.next()
                kb.op("scalar", lambda e, ps=ps, of=of: e.activation(out=of[:, 0:tn], in_=ps[:, 0:tn], func=AF.Silu),
                      r=[pb], w=[ofb])
                kb.dma("sync", D["GS"][gi * 128:(gi + 1) * 128, t0:t0 + tn], of[:, 0:tn], ofs, r=[ofb])
        self.end()

    def phase7(self):
        K = self
        kb, nc, D, DB, C = self.kb, self.nc, self.D, self.DB, self.C
        cb_ = self.CB
        NB, TP, T = self.NB, self.TP, self.TALL
        TK = T + PAST
        self.begin()
        sl = self.sem()
        pc = Buf("p7c")
        cqn = self.sb("cqn", [128, 4, T], BF16)
        lat = self.sb("lat", [128, 4, TK], BF16)
        kra = self.sb("kra", [65, TK], BF16)
        dmk = self.sb("dmk", [128, 128], BF16)
        dmf = self.sb("dmf", [128, 128], F32)
        kb.dma("sync", cqn[:], D["CQN"].rearrange("(a p) t -> p a t", p=128), sl, w=[pc])
        kb.dma("sync", lat[:, :, 0:T], D["LAT"].rearrange("(a p) t -> p a t", p=128), sl, w=[pc])
        kb.dma("gpsimd", lat[:, :, T:TK], D["clatT"].rearrange("(a p) t -> p a t", p=128), sl, w=[pc])
        kb.dma("sync", kra[0:64, 0:T], D["KR"], sl, w=[pc])
        kb.dma("gpsimd", kra[0:64, T:TK], D["ckrT"], sl, w=[pc])
        kb.dma("sync", dmf[:], D["dmask"], sl, w=[pc])
        kb.op("vector", lambda e: e.memset(kra[64:65, :], 1.0), w=[pc])
        kb.op("vector", lambda e: e.tensor_copy(out=dmk[:], in_=dmf[:]), r=[pc], w=[pc])
        vtiles = [(c0, 128) for c0 in range(0, TP, 128)] + [(TP, DEC_SEQ)] + [(T + 128 * j, 128) for j in range(8)]
        NV = len(vtiles)
        wqr = Ring(K, "wq7", [128, 4, 256], BF16, 2)
        wkr = Ring(K, "wkv7", [128, 4, 256], BF16, 2)
        qnr = Ring(K, "qn7", [128, T], BF16, 2, dma=False)
        qrr = Ring(K, "qr7", [65, T], BF16, 2, dma=False)
        knr = Ring(K, "kn7", [128, TK], BF16, 2, dma=False)
        vr = Ring(K, "v7", [128, NV, 128], BF16, 2, dma=False)
        csr = Ring(K, "cs7", [64, 2, 512], F32, 2)
        t1 = self.sb("t17", [64, 512], F32); b_t1 = Buf("t17")
        t2 = self.sb("t27", [64, 512], F32); b_t2 = Buf("t27")
        ptr = Ring(K, "pt7", [128, 512], BF16, 3, dma=False)
        gsr = Ring(K, "gs7", [128, 512], F32, 2)
        rsb = self.sb("rsb7", [128, 512], F32); b_rs = Buf("rs7")
        tsb = self.sb("tsb7", [128, 512], F32); b_ts = Buf("ts7")
        outr = Ring(K, "o7", [128, 512], BF16, 2)
        ppr = Ring(K, "pp7", [128, 512], F32, 3, dma=False, psum=True)
        stp = Ring(K, "st7", [128, 512], F32, 2, dma=False, psum=True)
        otp = self.ps("ot7", [128, 512], F32); b_otp = Buf("ot7", psum=True)
        smp = self.ps("sm7", [128, 512], F32); b_smp = Buf("sm7", psum=True)
        for r_ in qrr.items:
            kb.op("vector", lambda e, t=r_[0]: e.memset(t[64:65, :], 0.0), w=[r_[1]])
        kblocks = [(i * 512, 512) for i in range(TK // 512)] + ([(TK - TK % 512, TK % 512)] if TK % 512 else [])

        def project(h):
            wq, wqb, wqs = wqr.next()
            kb.dma("gpsimd", wq[:], D["w1q"][h], wqs, w=[wqb])
            wk, wkb, wks = wkr.next()
            kb.dma("gpsimd", wk[:], D["w1kv"][h], wks, w=[wkb])
            qn, qnb, _ = qnr.next()
            qr, qrb, _ = qrr.next()
            kn, knb, _ = knr.next()
            v, vb, _ = vr.next()
            for (t0, tn) in self.blocks:
                ps, pb, _ = ppr.next()
                for kc in range(4):
                    kb.op("tensor", lambda e, kc=kc, ps=ps: e.matmul(ps[:, 0:tn], wq[:, kc, 0:128], cqn[:, kc, t0:t0 + tn],
                                                                    start=(kc == 0), stop=(kc == 3)), r=[wqb, pc], w=[pb], inc=(kc == 3))
                kb.op("scalar", lambda e, ps=ps: e.activation(out=qn[:, t0:t0 + tn], in_=ps[:, 0:tn], func=AF.Copy), r=[pb], w=[qnb])
                cs, csb, css = csr.next()
                kb.dma("sync", cs[:, 0, 0:tn], D["rcos"][:, t0:t0 + tn], css, w=[csb])
                kb.dma("sync", cs[:, 1, 0:tn], D["rsin"][:, t0:t0 + tn], css, w=[csb])
                pa, pab, _ = ppr.next()
                for kc in range(4):
                    kb.op("tensor", lambda e, kc=kc, pa=pa: e.matmul(pa[0:64, 0:tn], wq[:, kc, 128:192], cqn[:, kc, t0:t0 + tn],
                                                                    start=(kc == 0), stop=(kc == 3)), r=[wqb, pc], w=[pab], inc=(kc == 3))
                kb.op("vector", lambda e, pa=pa, cs=cs: e.tensor_tensor(out=t1[:, 0:tn], in0=pa[0:64, 0:tn], in1=cs[:, 0, 0:tn], op=ALU.mult),
                      r=[pab, csb], w=[b_t1])
                pb2, pb2b, _ = ppr.next()
                for kc in range(4):
                    kb.op("tensor", lambda e, kc=kc, pb2=pb2: e.matmul(pb2[0:64, 0:tn], wq[:, kc, 192:256], cqn[:, kc, t0:t0 + tn],
                                                                      start=(kc == 0), stop=(kc == 3)), r=[wqb, pc], w=[pb2b], inc=(kc == 3))
                kb.op("vector", lambda e, pb2=pb2, cs=cs: e.tensor_tensor(out=t2[:, 0:tn], in0=pb2[0:64, 0:tn], in1=cs[:, 1, 0:tn], op=ALU.mult),
                      r=[pb2b, csb], w=[b_t2])
                kb.op("gpsimd", lambda e: e.tensor_tensor(out=qr[0:64, t0:t0 + tn], in0=t1[:, 0:tn], in1=t2[:, 0:tn], op=ALU.add),
                      r=[b_t1, b_t2], w=[qrb])
            for (k0, kn_) in kblocks:
                ps, pb, _ = ppr.next()
                for kc in range(4):
                    kb.op("tensor", lambda e, kc=kc, ps=ps: e.matmul(ps[:, 0:kn_], wk[:, kc, 0:128], lat[:, kc, k0:k0 + kn_],
                                                                    start=(kc == 0), stop=(kc == 3)), r=[wkb, pc], w=[pb], inc=(kc == 3))
                kb.op("scalar", lambda e, ps=ps: e.activation(out=kn[:, k0:k0 + kn_], in_=ps[:, 0:kn_], func=AF.Copy), r=[pb], w=[knb])
            for i0 in range(0, NV, 4):
                ng = min(4, NV - i0)
                ps, pb, _ = ppr.next()
                full = all(vtiles[i0 + ii][1] == 128 for ii in range(ng))
                for ii in range(ng):
                    c0, kn_ = vtiles[i0 + ii]
                    for kc in range(4):
                        kb.op("tensor", lambda e, kc=kc, ps=ps, ii=ii, c0=c0, kn_=kn_: e.matmul(
                            ps[0:kn_, ii * 128:(ii + 1) * 128], lat[:, kc, c0:c0 + kn_], wk[:, kc, 128:256],
                            start=(kc == 0), stop=(kc == 3)), r=[wkb, pc], w=[pb], inc=(kc == 3 and (ii == ng - 1 or not full)))
                    if not full:
                        kb.op("scalar", lambda e, ps=ps, ii=ii, kn_=kn_, i0=i0: e.activation(
                            out=v[0:kn_, i0 + ii, :], in_=ps[0:kn_, ii * 128:(ii + 1) * 128], func=AF.Copy), r=[pb], w=[vb])
                if full:
                    kb.op("scalar", lambda e, ps=ps, i0=i0, ng=ng: e.activation(
                        out=v[:, i0:i0 + ng, :], in_=ps[:, 0:ng * 128].rearrange("p (a b) -> p a b", b=128), func=AF.Copy),
                        r=[pb], w=[vb])
            return (qn, qnb, qr, qrb, kn, knb, v, vb)

        def attend(h, proj, Q, qcol0, tiles):
            qn, qnb, qr, qrb, kn, knb, v, vb = proj
            gs, gsb, gss = gsr.next()
            kb.dma("sync", gs[:, 0:Q], D["GS"][h * 128:(h + 1) * 128, qcol0:qcol0 + Q], gss, w=[gsb])
            n = len(tiles)
            sts = [None] * n

            def emit_st(j):
                kc0, kn_, vi, c0, diag = tiles[j]
                st_, stb, _ = stp.next()
                sts[j] = (st_, stb)
                kb.op("tensor", lambda e: e.matmul(st_[0:kn_, c0:Q], kn[:, kc0:kc0 + kn_], qn[:, qcol0 + c0:qcol0 + Q],
                                                   start=True, stop=False), r=[knb, qnb], w=[stb], inc=False)
                kb.op("tensor", lambda e: e.matmul(st_[0:kn_, c0:Q], kra[0:65, kc0:kc0 + kn_], qr[0:65, qcol0 + c0:qcol0 + Q],
                                                   start=False, stop=True), r=[pc, qrb], w=[stb])
            emit_st(0)
            for j in range(n):
                kc0, kn_, vi, c0, diag = tiles[j]
                if j + 1 < n:
                    emit_st(j + 1)
                st_, stb = sts[j]
                pt, ptb, _ = ptr.next()
                kb.op("scalar", lambda e, st_=st_, pt=pt, kn_=kn_, c0=c0: e.activation(
                    out=pt[0:kn_, c0:Q], in_=st_[0:kn_, c0:Q], func=AF.Exp, scale=MLA_SCALE), r=[stb], w=[ptb])
                if diag:
                    kb.op("gpsimd", lambda e, pt=pt, c0=c0: e.tensor_tensor(
                        out=pt[:, c0:c0 + 128], in0=pt[:, c0:c0 + 128], in1=dmk[:, :], op=ALU.mult), r=[ptb, pc], w=[ptb])
                kb.op("tensor", lambda e, pt=pt, kn_=kn_, c0=c0, vi=vi, j=j: e.matmul(
                    otp[:, c0:Q], v[0:kn_, vi, :], pt[0:kn_, c0:Q], start=(j == 0), stop=(j == n - 1)),
                    r=[vb, ptb], w=[b_otp], inc=(j == n - 1))
                kb.op("tensor", lambda e, pt=pt, kn_=kn_, c0=c0, j=j: e.matmul(
                    smp[:, c0:Q], C["onesb"][0:kn_, :], pt[0:kn_, c0:Q], start=(j == 0), stop=(j == n - 1)),
                    r=[cb_, ptb], w=[b_smp], inc=(j == n - 1))
            kb.op("vector", lambda e: e.reciprocal(out=rsb[:, 0:Q], in_=smp[:, 0:Q]), r=[b_smp], w=[b_rs])
            kb.op("vector", lambda e: e.tensor_tensor(out=tsb[:, 0:Q], in0=otp[:, 0:Q], in1=rsb[:, 0:Q], op=ALU.mult),
                  r=[b_otp, b_rs], w=[b_ts])
            ot, otb, ots = outr.next()
            kb.op("vector", lambda e: e.tensor_tensor(out=ot[:, 0:Q], in0=tsb[:, 0:Q], in1=gs[:, 0:Q], op=ALU.mult),
                  r=[b_ts, gsb], w=[otb])
            kb.dma("sync", D["O1T"][h * 128:(h + 1) * 128, qcol0:qcol0 + Q], ot[:, 0:Q], ots, r=[otb])

        NP = TP // 128
        for h in range(C_HEADS):
            proj = project(h)
            for qb in range(NB):
                tiles = []
                for j in range(4 * qb + 4):
                    r_ = j - 4 * qb
                    tiles.append((128 * j, 128, j, 128 * max(r_, 0), r_ >= 0))
                attend(h, proj, 512, qb * 512, tiles)
            tiles = [(T + 128 * j, 128, NP + 1 + j, 0, False) for j in range(8)] + [(TP, DEC_SEQ, NP, 0, False)]
            attend(h, proj, DEC_SEQ, TP, tiles)
        self.end()

    def build(self, phases=(1,)):
        with ExitStack() as gst:
            self.kb.setup(gst)
            self.declare()
            self.phase0(gst)
            if 9 in phases:
                self.begin()
                tt = self.sb("triv", [128, 8], F32)
                tb_ = Buf("triv")
                self.kb.op("vector", lambda e: e.memset(tt[:], 1.0), w=[tb_])
                self.kb.op("scalar", lambda e: e.activation(out=tt[:], in_=tt[:], func=AF.Copy), r=[tb_], w=[tb_])
                self.end()
            if 1 in phases:
                self.phase1()
            if 2 in phases:
                self.phase2()
            if 3 in phases:
                self.phase3()
            if 5 in phases:
                self.outproj_ln("MIXT", 24, "wo0", "xT", "ln0g", "ln0b", "Y0T")
            if 6 in phases:
                self.phase6()
            if 7 in phases:
                self.phase7()
            if 8 in phases:
                self.outproj_ln("O1T", 16, "wo1", "Y0T", "ln1g", "ln1b", "o_yT")
        return self.nc


def _tile_fm(Wcols):
    k, c = Wcols.shape
    return np.ascontiguousarray(Wcols.reshape(k // 128, 128, c).transpose(1, 0, 2))


def _t5_onehot():
    rel = np.arange(1152, dtype=np.int32) - 511
    half, max_exact = 16, 8
    ret = np.where(rel < 0, half, 0)
    n = np.abs(rel)
    nf = np.maximum(n, 1).astype(np.float32)
    large = max_exact + (np.log(nf / np.float32(max_exact)) / np.float32(math.log(128 / max_exact))
                         * np.float32(half - max_exact)).astype(np.int32)
    large = np.minimum(large, half - 1)
    bucket = ret + np.where(n < max_exact, n, large)
    oh = np.zeros((32, 1152), np.float32)
    oh[bucket, np.arange(1152)] = 1.0
    oh[:, 1151] = 0.0
    return oh


def prep_core(inp, b, NB):
    TP = 512 * NB
    o = {}
    xp = inp["x_prompt"][b, :TP]
    xs = inp["x_sample"][b]
    o["xT"] = np.ascontiguousarray(np.concatenate([xp, xs], axis=0).T)
    W = inp["w_in0"][0]
    c_aq, c_ak, c_av, c_ag, c_iq, c_ik, c_iw, c_bz, c_xbc, c_dt = np.cumsum(
        [0, 1024, 1024, 1024, 1024, 1024, 64, 16, 2048, 3072])
    tl = []
    for i in range(8):
        tl.append(W[:, c_aq + i * 128:c_aq + (i + 1) * 128])
    for i in range(8):
        tl.append(W[:, c_ak + i * 128:c_ak + (i + 1) * 128])
    for i in range(8):
        tl.append(W[:, c_ag + i * 128:c_ag + (i + 1) * 128])
    for i in range(8):
        tl.append(W[:, c_iq + i * 128:c_iq + (i + 1) * 128])
    ik = W[:, c_ik:c_ik + 64]
    tl.append(np.concatenate([ik, ik], axis=1))
    for i in range(16):
        tl.append(W[:, c_bz + i * 128:c_bz + (i + 1) * 128])
    for i in range(24):
        tl.append(W[:, c_xbc + i * 128:c_xbc + (i + 1) * 128])
    o["w0fm"] = np.stack([_tile_fm(t) for t in tl])
    o["w0v"] = np.stack([np.ascontiguousarray(
        W[:, c_av + g * 512:c_av + (g + 1) * 512].reshape(16, 128, 512).transpose(1, 0, 2)) for g in range(2)])
    ws = np.concatenate([W[:, c_iw:c_iw + 16], W[:, c_dt:c_dt + 32]], axis=1)
    o["w0s"] = np.ascontiguousarray(ws.reshape(16, 128, 48).transpose(1, 0, 2))
    o["convw"] = np.ascontiguousarray(inp["conv_w"][0].T.reshape(24, 128, 4).transpose(1, 0, 2))
    o["convb"] = np.ascontiguousarray(inp["conv_b"][0].reshape(24, 128).T)
    o["sconvT"] = np.ascontiguousarray(inp["state_b_conv"][0, b].T)
    o["ident"] = np.eye(128, dtype=np.float32)
    o["antiid"] = np.eye(128, dtype=np.float32)[::-1]
    o["t5"] = inp["t5_bias"]
    o["t5oh"] = _t5_onehot()
    o["ckT"] = inp["cache_a_k"][0, b].reshape(PAST, 1024).T
    o["cv"] = inp["cache_a_v"][0, b].reshape(PAST, 1024)
    o["dtbias"] = inp["dt_bias"][0][None, :]
    o["alog"] = inp["a_log"][0][None, :]
    o["dskT"] = np.repeat(inp["d_skip"][0].reshape(16, 2), 64, axis=1).T
    o["nwT"] = inp["ssm_norm_w"][0].reshape(16, 128).T
    o["h0T"] = inp["state_b_ssm"][0, b].transpose(2, 0, 1).reshape(128, 2048)
    o["utri"] = np.triu(np.ones((128, 128), np.float32))
    o["negm"] = np.tril(np.full((128, 128), NEG, np.float32), -1)
    o["wo0"] = inp["w_out0"][0].reshape(24, 128, 2048).transpose(1, 0, 2)
    o["wo1"] = inp["w_out1"][0].reshape(16, 128, 2048).transpose(1, 0, 2)
    for nm, src in (("ln0g", "ln0_g"), ("ln0b", "ln0_b"), ("ln1g", "ln1_g"), ("ln1b", "ln1_b")):
        o[nm] = inp[src][0].reshape(16, 128).T
    W1 = inp["w_in1"][0]
    t1 = [W1[:, i * 128:(i + 1) * 128] for i in range(8)] + [W1[:, 1088 + i * 128:1088 + (i + 1) * 128] for i in range(16)]
    o["w1fm"] = np.stack([_tile_fm(t) for t in t1])
    perm = np.concatenate([np.arange(32, 64), np.arange(0, 32)])
    krw = W1[:, 1024:1088]
    o["w1kr"] = np.stack([_tile_fm(krw), _tile_fm(krw[:, perm])])
    wuq = inp["w_uq"][0].reshape(512, 16, 192)
    wq = np.concatenate([wuq, wuq[:, :, 128 + perm]], axis=2)
    o["w1q"] = wq.reshape(4, 128, 16, 256).transpose(2, 1, 0, 3)
    wukv = inp["w_ukv"][0].reshape(512, 16, 256)
    o["w1kv"] = wukv.reshape(4, 128, 16, 256).transpose(2, 1, 0, 3)
    o["qnw"] = inp["q_norm_w"][0].reshape(4, 128).T
    o["kvnw"] = inp["kv_norm_w"][0].reshape(4, 128).T
    pos = np.concatenate([np.arange(TP), PAST + np.arange(DEC_SEQ)]).astype(np.float32)
    inv = (np.float32(10000.0) ** (-np.arange(32, dtype=np.float32) / np.float32(32))).astype(np.float32)
    ang = (pos[None, :] * inv[:, None]).astype(np.float32)
    cs_, sn_ = np.cos(ang).astype(np.float32), np.sin(ang).astype(np.float32)
    o["rcos"] = np.concatenate([cs_, cs_], axis=0)
    o["rsin"] = np.concatenate([-sn_, sn_], axis=0)
    o["clatT"] = inp["cache_c_latent"][0, b].T
    o["ckrT"] = inp["cache_c_krope"][0, b].T
    dm = np.ones((128, 128), np.float32)
    dm[64:, :64] = 0.0
    o["dmask"] = dm
    kx = inp["cache_a_kidx"][0, b].T
    o["ckidxT2"] = np.concatenate([kx, kx], axis=0)
    return {k: np.ascontiguousarray(v, dtype=np.float32) for k, v in o.items()}


ALL_PHASES = (1, 2, 3, 5, 6, 7, 8)
_OUT_NAMES = ("o_yT", "o_akT", "o_av", "o_kidxT", "o_convT", "o_ssmT", "o_clatT", "o_ckrT")


def assemble(results, NB, nb):
    TP = 512 * NB
    f = lambda name: [np.asarray(r[name]) for r in results]
    yT, akT, av, kxT, cvT, ssT, clT, krT = (f(n) for n in _OUT_NAMES)
    st = lambda fn: np.stack([fn(i) for i in range(nb)])
    y_p = st(lambda i: yT[i][:, :TP].T)
    y_s = st(lambda i: yT[i][:, TP:].T)
    ak_p = st(lambda i: akT[i][:, :TP].T.reshape(TP, 8, 128))[None]
    ak_s = st(lambda i: akT[i][:, TP:].T.reshape(DEC_SEQ, 8, 128))[None]
    av_p = st(lambda i: av[i][:TP].reshape(TP, 8, 128))[None]
    av_s = st(lambda i: av[i][TP:].reshape(DEC_SEQ, 8, 128))[None]
    ki_p = st(lambda i: kxT[i][:, :TP].T)[None]
    ki_s = st(lambda i: kxT[i][:, TP:].T)[None]
    cv_p = st(lambda i: cvT[i][:, 0:3].T)[None]
    cv_s = st(lambda i: cvT[i][:, 3:6].T)[None]
    ss_p = st(lambda i: ssT[i][0].reshape(128, 32, 64).transpose(1, 2, 0))[None]
    ss_s = st(lambda i: ssT[i][1].reshape(128, 32, 64).transpose(1, 2, 0))[None]
    cl_p = st(lambda i: clT[i][:, :TP].T)[None]
    cl_s = st(lambda i: clT[i][:, TP:].T)[None]
    kr_p = st(lambda i: krT[i][:, :TP].T)[None]
    kr_s = st(lambda i: krT[i][:, TP:].T)[None]
    outs = (y_p, y_s, ak_p, ak_s, av_p, av_s, ki_p, ki_s, cv_p, cv_s, ss_p, ss_s, cl_p, cl_s, kr_p, kr_s)
    return tuple(np.ascontiguousarray(o, dtype=np.float32) for o in outs)


def kernel(**inputs):
    inp = {k: np.asarray(v) for k, v in inputs.items()}
    NB = SEQ // 512
    n = 8
    K = Kern(NB)
    nc = K.build(phases=ALL_PHASES)
    in_maps = [prep_core(inp, b, NB) for b in range(n)]
    res = run_bass_kernel_spmd(nc, in_maps, core_ids=list(range(n)))
    return assemble(res.results, NB, n)
```

```python
import math
from contextlib import ExitStack
import numpy as np
import concourse.bass as bass
import concourse.mybir as mybir
from concourse.bass_utils import run_bass_kernel_spmd

F32 = mybir.dt.float32
BF16 = mybir.dt.bfloat16
ALU = mybir.AluOpType
AF = mybir.ActivationFunctionType
AX = mybir.AxisListType

ENGS = ("tensor", "vector", "scalar", "gpsimd", "sync")

D_MODEL = 2048
SEQ = 4096
DEC_SEQ = 64
PAST = 1024
A_HEADS = 8
IDX_HEADS = 16
TOPK = 256
B_HEADS = 32
B_GROUPS = 4
C_HEADS = 16
EPS = 1e-5
ALPHA = (2 * 2) ** 0.25
A_SCALE = 128 ** -0.5
MLA_SCALE = (128 + 64) ** -0.5
W_IN0 = 10352
NEG = -30000.0


class Sem:
    def __init__(self, h, name, dma=False):
        self.h = h
        self.name = name
        self.dma = dma
        self.cum = 0
        self.bounds = []


class DSem:
    def __init__(self, kb):
        self.kb = kb
        self.hw = None
        self.sw = None

    def get(self, qn):
        if qn == "gpsimd":
            if self.sw is None:
                self.sw = self.kb.free_sw.pop()
            return self.sw
        if self.hw is None:
            self.hw = self.kb.free_hw.pop()
        return self.hw

    def release(self):
        if self.hw is not None:
            self.kb.free_hw.append(self.hw)
        if self.sw is not None:
            self.kb.free_sw.append(self.sw)
        self.hw = self.sw = None


class Buf:
    __slots__ = ("name", "w", "rs", "psum")

    def __init__(self, name="b", psum=False):
        self.name = name
        self.w = None
        self.rs = []
        self.psum = psum


class Eng:
    def __init__(self, name, sem):
        self.name = name
        self.sem = sem
        self.prog = []
        self.waited = {}
        self.pend_r = []
        self.pend_w = []


class KB:
    def __init__(self, nc):
        self.nc = nc
        self.E = {}
        self.nins = 0

    def setup(self, stack, n_hw=64, n_sw=32):
        for n in ENGS:
            h = stack.enter_context(self.nc.semaphore("e_" + n))
            self.E[n] = Eng(n, Sem(h, "e_" + n))
        self.free_hw = []
        self.free_sw = []
        for i in range(n_hw):
            h = stack.enter_context(self.nc.semaphore("dh%d" % i))
            self.free_hw.append(Sem(h, "dh%d" % i, dma=True))
        for i in range(n_sw):
            h = stack.enter_context(self.nc.semaphore("ds%d" % i))
            self.free_sw.append(Sem(h, "ds%d" % i, dma=True))
        self.all_dsems = self.free_hw + self.free_sw

    def dsem(self):
        return DSem(self)

    def _wait(self, e, ev):
        sem, val = ev
        if val is None:
            assert sem is e.sem, "dependency on pending instruction of another engine"
            return
        if sem.dma:
            b = None
            for x in reversed(sem.bounds):
                if x >= val:
                    b = x
                else:
                    break
            if b is None:
                b = sem.cum
                sem.bounds.append(b)
                if len(sem.bounds) > 8:
                    sem.bounds = sem.bounds[-8:]
            val = b
        if sem is e.sem and e.name in ("tensor", "sync"):
            return
        if e.waited.get(sem, 0) >= val:
            return
        e.waited[sem] = val
        h = sem.h
        e.prog.append(lambda eng, h=h, val=val: eng.wait_ge(h, val))
        self.nins += 1

    def _deps(self, e, r, w):
        for b in r:
            if b.w is not None:
                self._wait(e, b.w)
            if b.psum:
                for ev in b.rs:
                    if ev[0] is not e.sem:
                        self._wait(e, ev)
        for b in w:
            if b.w is not None:
                self._wait(e, b.w)
            for ev in b.rs:
                self._wait(e, ev)
            for e2 in self.E.values():
                if e2 is not e and e2.pend_r:
                    for pb in e2.pend_r:
                        assert pb is not b, "WAR on a pending (non-inc) read of %s by %s" % (b.name, e2.name)

    def op(self, en, fn, r=(), w=(), inc=True):
        e = self.E[en]
        rec = _Rec()
        fn(rec)
        assert len(rec.calls) == 1
        cname, cargs, ckw = rec.calls[0]
        fn = lambda eng, cname=cname, cargs=cargs, ckw=ckw: getattr(eng, cname)(*cargs, **ckw)
        self._deps(e, r, w)
        self.nins += 1
        if not inc:
            e.pend_r += list(r)
            e.pend_w += list(w)
            e.prog.append(lambda eng, fn=fn: fn(eng))
            for b in w:
                b.w = (e.sem, None)
                b.rs = []
            return
        e.sem.cum += 1
        ev = (e.sem, e.sem.cum)
        h = e.sem.h
        e.prog.append(lambda eng, fn=fn, h=h: fn(eng).then_inc(h, 1))
        for b in e.pend_r + list(r):
            b.rs.append(ev)
            if len(b.rs) > 12:
                b.rs = _prune(b.rs)
        for b in e.pend_w + list(w):
            b.w = ev
            b.rs = []
        e.pend_r = []
        e.pend_w = []

    def dma(self, qn, out, in_, sem, r=(), w=(), **kw):
        e = self.E[qn]
        sem = sem.get(qn)
        self._deps(e, r, w)
        if sem.bounds:
            b = sem.bounds[-1]
            if e.waited.get(sem, 0) < b:
                e.waited[sem] = b
                h = sem.h
                e.prog.append(lambda eng, h=h, b=b: eng.wait_ge(h, b))
        sem.cum += 16
        ev = (sem, sem.cum)
        h = sem.h
        e.prog.append(lambda eng, out=out, in_=in_, h=h, kw=kw:
                      eng.dma_start(out=out, in_=in_, **kw).then_inc(h, 16))
        self.nins += 1
        for b in r:
            b.rs.append(ev)
            if len(b.rs) > 12:
                b.rs = _prune(b.rs)
        for b in w:
            b.w = ev
            b.rs = []

    def barrier(self):
        evs = []
        for n in ENGS:
            e = self.E[n]
            assert not e.pend_r and not e.pend_w, "pending at barrier on " + n
            if e.sem.cum > 0:
                evs.append((e.sem, e.sem.cum))
        for s in self.all_dsems:
            if s.cum > 0:
                evs.append((s, s.cum))
        for n in ENGS:
            for ev in evs:
                self._wait(self.E[n], ev)

    def emit(self):
        nc = self.nc
        progs = {n: self.E[n].prog for n in ENGS}
        for n in ENGS:
            self.E[n].prog = []
        with nc.Block() as block:
            @block.tensor
            def _(eng):
                for f in progs["tensor"]:
                    f(eng)

            @block.vector
            def _(eng):
                for f in progs["vector"]:
                    f(eng)

            @block.scalar
            def _(eng):
                for f in progs["scalar"]:
                    f(eng)

            @block.gpsimd
            def _(eng):
                for f in progs["gpsimd"]:
                    f(eng)

            @block.sync
            def _(eng):
                for f in progs["sync"]:
                    f(eng)


def run_lanes(lanes):
    n = len(lanes)
    clocks = [0.0] * n
    done = [False] * n
    waiting = [None] * n
    flags = {}
    while not all(done):
        progressed = False
        for i in sorted(range(n), key=lambda i: clocks[i]):
            if done[i]:
                continue
            if waiting[i] is not None:
                if waiting[i] not in flags:
                    continue
                clocks[i] = max(clocks[i], flags[waiting[i]])
                waiting[i] = None
            try:
                v = next(lanes[i])
            except StopIteration:
                done[i] = True
                progressed = True
                break
            if isinstance(v, tuple):
                if v[0] == "wait":
                    if v[1] in flags:
                        clocks[i] = max(clocks[i], flags[v[1]])
                    else:
                        waiting[i] = v[1]
                else:
                    flags[v[1]] = clocks[i]
            else:
                clocks[i] += v
            progressed = True
            break
        assert progressed, "lane scheduler deadlock"


class _Rec:
    def __init__(self):
        self.calls = []

    def __getattr__(self, name):
        def f(*a, **k):
            self.calls.append((name, a, k))
            return None
        return f


def _prune(rs):
    best = {}
    for s, v in rs:
        if s not in best or best[s] < v:
            best[s] = v
    return [(s, v) for s, v in best.items()]


class Ring:
    def __init__(self, K, name, shape, dtype, n, dma=True, psum=False):
        self.items = []
        self.i = 0
        self.sems = []
        for j in range(n):
            if psum:
                t = K.ps("%s%d" % (name, j), shape, dtype)
            else:
                t = K.sb("%s%d" % (name, j), shape, dtype)
            s = K.kb.dsem() if dma else None
            if s is not None:
                K.phase_sems.append(s)
            self.items.append((t, Buf("%s%d" % (name, j), psum=psum), s))

    def next(self):
        it = self.items[self.i % len(self.items)]
        self.i += 1
        return it


class Kern:
    def __init__(self, NB, debug=()):
        self.NB = NB
        self.TP = 512 * NB
        self.TALL = self.TP + DEC_SEQ
        self.debug = set(debug)
        self.nc = bass.Bass("TRN2", target_bir_lowering=False)
        self.kb = KB(self.nc)
        self.D = {}
        self.phase_sems = []
        self.blocks = [(i * 512, 512) for i in range(NB)] + [(self.TP, DEC_SEQ)]

    def din(self, name, shape, dt=F32):
        self.D[name] = self.nc.dram_tensor(name, list(shape), dt, kind="ExternalInput").ap()
        return self.D[name]

    def dout(self, name, shape, dt=F32):
        self.D[name] = self.nc.dram_tensor(name, list(shape), dt, kind="ExternalOutput").ap()
        return self.D[name]

    def dscr(self, name, shape, dt):
        kind = "ExternalOutput" if name in self.debug else "Internal"
        self.D[name] = self.nc.dram_tensor(name, list(shape), dt, kind=kind).ap()
        self.DB[name] = Buf(name)
        return self.D[name]

    def uname(self, name):
        self.uid = getattr(self, "uid", 0) + 1
        return "%s_u%d" % (name, self.uid)

    def sb(self, name, shape, dt):
        return self.st.enter_context(self.nc.sbuf_tensor(self.uname(name), list(shape), dt))

    def ps(self, name, shape, dt=F32):
        return self.st.enter_context(self.nc.psum_tensor(self.uname(name), list(shape), dt))

    def begin(self):
        self.st = ExitStack()
        self.phase_sems = []

    def end(self):
        self.kb.barrier()
        self.kb.emit()
        self.st.close()
        for s_ in self.phase_sems:
            s_.release()
        self.phase_sems = []

    def sem(self):
        s = self.kb.dsem()
        self.phase_sems.append(s)
        return s

    def declare(self):
        T = self.TALL
        self.DB = {}
        din, dout, dscr = self.din, self.dout, self.dscr
        din("xT", [2048, T])
        din("w0fm", [73, 128, 16, 128])
        din("w0v", [2, 128, 16, 512])
        din("w0s", [128, 16, 48])
        din("convw", [128, 24, 4])
        din("convb", [128, 24])
        din("sconvT", [3072, 3])
        din("ident", [128, 128])
        din("antiid", [128, 128])
        din("t5", [32, 8])
        din("t5oh", [32, 1152])
        din("ckT", [1024, PAST])
        din("cv", [PAST, 1024])
        din("ckidxT2", [128, PAST])
        din("dtbias", [1, 32])
        din("alog", [1, 32])
        din("dskT", [128, 16])
        din("nwT", [128, 16])
        din("h0T", [128, 2048])
        din("utri", [128, 128])
        din("negm", [128, 128])
        dout("o_ssmT", [2, 128, 2048])
        din("wo0", [128, 24, 2048]); din("wo1", [128, 16, 2048])
        din("ln0g", [128, 16]); din("ln0b", [128, 16]); din("ln1g", [128, 16]); din("ln1b", [128, 16])
        din("w1fm", [24, 128, 16, 128]); din("w1kr", [2, 128, 16, 64])
        din("w1q", [16, 128, 4, 256]); din("w1kv", [16, 128, 4, 256])
        din("qnw", [128, 4]); din("kvnw", [128, 4])
        din("rcos", [64, T]); din("rsin", [64, T])
        din("clatT", [512, PAST]); din("ckrT", [64, PAST]); din("dmask", [128, 128])
        dout("o_yT", [2048, T]); dout("o_clatT", [512, T]); dout("o_ckrT", [64, T])
        dscr("Y0T", [2048, T], F32); dscr("CQN", [512, T], BF16); dscr("LAT", [512, T], BF16)
        dscr("KR", [64, T], BF16); dscr("GS", [2048, T], F32); dscr("O1T", [2048, T], BF16)
        dout("o_akT", [1024, T])
        dout("o_av", [T, 1024])
        dout("o_kidxT", [64, T])
        dout("o_convT", [3072, 6])
        dscr("QT", [1024, T], BF16)
        dscr("KT", [1024, T], BF16)
        dscr("IQ", [1024, T], BF16)
        dscr("IK2", [128, T], BF16)
        dscr("V", [T, 1024], BF16)
        dscr("AGS", [1024, T], F32)
        dscr("IWDT", [T, 48], F32)
        dscr("ZS", [2048, T], F32)
        dscr("XC", [3072, T], F32)
        dscr("FD", [8, 1152], F32)
        dscr("MIXT", [3072, T], BF16)

    def phase1(self):
        K = self
        kb, nc, D, DB = self.kb, self.nc, self.D, self.DB
        T = self.TALL
        self.begin()
        xsb = self.sb("xsb", [128, 16, T], BF16)
        b_x = [Buf("x%d" % k) for k in range(16)]
        s_x = self.sem()
        for kc in range(16):
            kb.dma("gpsimd", xsb[:, kc, :], D["xT"][kc * 128:(kc + 1) * 128, :], s_x, w=[b_x[kc]],
                   max_dma_last_dim=4096)
        cw = self.sb("cw", [128, 24, 4], F32)
        cb = self.sb("cb", [128, 24], F32)
        b_c = Buf("convc")
        s_c = self.sem()
        kb.dma("sync", cw[:], D["convw"], s_c, w=[b_c])
        kb.dma("sync", cb[:], D["convb"], s_c, w=[b_c])

        wr = Ring(K, "wfm", [128, 16, 128], BF16, 3)
        psr = Ring(K, "ps1", [128, 512], F32, 4, dma=False, psum=True)
        sbf = Ring(K, "sbf", [128, 512], BF16, 3)
        sf32 = Ring(K, "sf32", [128, 512], F32, 3)
        csr = Ring(K, "cs", [128, 515], F32, 2)
        accr = Ring(K, "acc", [128, 512], F32, 2, dma=False)

        def mm_group(pst, pb, wt, wb, t0, tn):
            for kc in range(16):
                kb.op("tensor", lambda e, kc=kc: e.matmul(
                    pst[:, 0:tn], wt[:, kc, :], xsb[:, kc, t0:t0 + tn],
                    start=(kc == 0), stop=(kc == 15)),
                    r=[wb, b_x[kc]], w=[pb], inc=(kc == 15))

        tiles = ([("q", i) for i in range(8)] + [("k", i) for i in range(8)] + [("ag", i) for i in range(8)] +
                 [("iq", i) for i in range(8)] + [("ik2", 0)] + [("z", i) for i in range(16)] +
                 [("xbc", i) for i in range(24)])
        assert len(tiles) == 73
        import os
        P1M = os.environ.get("P1MASK", "axt")
        for ct, (kind, idx) in enumerate(tiles):
            if kind == "xbc" and "x" not in P1M:
                continue
            if kind != "xbc" and "a" not in P1M:
                continue
            if os.environ.get("P1KINDS") and kind not in os.environ["P1KINDS"].split(","):
                continue
            wt, wb, ws = wr.next()
            kb.dma("gpsimd", wt[:], D["w0fm"][ct], ws, w=[wb], max_dma_last_dim=4096)
            rows = slice(idx * 128, (idx + 1) * 128)
            for tb, (t0, tn) in enumerate(self.blocks):
                pst, pb, _ = psr.next()
                mm_group(pst, pb, wt, wb, t0, tn)
                cols = slice(t0, t0 + tn)
                if kind in ("q", "k", "iq", "ik2"):
                    name = {"q": "QT", "k": "KT", "iq": "IQ", "ik2": "IK2"}[kind]
                    st, sb_, ss = sbf.next()
                    kb.op("scalar", lambda e, st=st, pst=pst, tn=tn: e.activation(
                        out=st[:, 0:tn], in_=pst[:, 0:tn], func=AF.Copy), r=[pb], w=[sb_])
                    kb.dma("sync", D[name][rows, cols], st[:, 0:tn], ss, r=[sb_], w=[])
                    if kind in ("k", "ik2"):
                        s2, s2b, s2s = sf32.next()
                        if os.environ.get("P1ACT"):
                            kb.op("scalar", lambda e, s2=s2, pst=pst, tn=tn: e.activation(
                                out=s2[:, 0:tn], in_=pst[:, 0:tn], func=AF.Copy), r=[pb], w=[s2b])
                        else:
                            kb.op("vector", lambda e, s2=s2, pst=pst, tn=tn: e.tensor_copy(
                                out=s2[:, 0:tn], in_=pst[:, 0:tn]), r=[pb], w=[s2b])
                        if kind == "k":
                            kb.dma("sync", D["o_akT"][rows, cols], s2[:, 0:tn], s2s, r=[s2b])
                        else:
                            kb.dma("sync", D["o_kidxT"][:, cols], s2[0:64, 0:tn], s2s, r=[s2b])
                elif kind in ("ag", "z"):
                    name = {"ag": "AGS", "z": "ZS"}[kind]
                    s2, s2b, s2s = sf32.next()
                    kb.op("scalar", lambda e, s2=s2, pst=pst, tn=tn: e.activation(
                        out=s2[:, 0:tn], in_=pst[:, 0:tn], func=AF.Silu), r=[pb], w=[s2b])
                    kb.dma("sync", D[name][rows, cols], s2[:, 0:tn], s2s, r=[s2b])
                else:
                    cs, csb, css = csr.next()
                    if tb == 0:
                        kb.op("gpsimd", lambda e, cs=cs: e.memset(cs[:, 0:3], 0.0), w=[csb])
                    elif tb == self.NB:
                        kb.dma("sync", cs[:, 0:3], D["sconvT"][rows, :], css, w=[csb])
                    else:
                        pcs, pcsb, _ = csr.items[(csr.i - 2) % 2]
                        kb.op("vector", lambda e, cs=cs, pcs=pcs: e.tensor_copy(
                            out=cs[:, 0:3], in_=pcs[:, 512:515]), r=[pcsb], w=[csb])
                    kb.op("scalar", lambda e, cs=cs, pst=pst, tn=tn: e.activation(
                        out=cs[:, 3:3 + tn], in_=pst[:, 0:tn], func=AF.Copy), r=[pb], w=[csb])
                    if tb == self.NB - 1:
                        kb.dma("sync", D["o_convT"][rows, 0:3], cs[:, 512:515], css, r=[csb])
                    if tb == self.NB:
                        kb.dma("sync", D["o_convT"][rows, 3:6], cs[:, 64:67], css, r=[csb])
                    ac, acb, _ = accr.next()
                    kb.op("scalar", lambda e, ac=ac, cs=cs, tn=tn, idx=idx: e.activation(
                        out=ac[:, 0:tn], in_=cs[:, 0:tn], func=AF.Identity,
                        scale=cw[:, idx, 0:1], bias=cb[:, idx:idx + 1]), r=[csb, b_c], w=[acb])
                    for j in range(1, 4):
                        kb.op("vector", lambda e, ac=ac, cs=cs, tn=tn, idx=idx, j=j: e.scalar_tensor_tensor(
                            out=ac[:, 0:tn], in0=cs[:, j:j + tn], scalar=cw[:, idx, j:j + 1], in1=ac[:, 0:tn],
                            op0=ALU.mult, op1=ALU.add), r=[csb, b_c, acb], w=[acb])
                    s2, s2b, s2s = sf32.next()
                    kb.op("scalar", lambda e, s2=s2, ac=ac, tn=tn: e.activation(
                        out=s2[:, 0:tn], in_=ac[:, 0:tn], func=AF.Silu), r=[acb], w=[s2b])
                    kb.dma("sync", D["XC"][rows, cols], s2[:, 0:tn], s2s, r=[s2b])

        wv = self.sb("wv", [128, 16, 512], BF16)
        wvb = Buf("wv")
        wvs = self.sem()
        wsm = self.sb("wsm", [128, 16, 48], BF16)
        wsb = Buf("wsm")
        kb.dma("gpsimd", wsm[:], D["w0s"], wvs, w=[wsb], max_dma_last_dim=4096)
        ttiles = [(i * 128, 128) for i in range(self.TP // 128)] + [(self.TP, DEC_SEQ)]
        for g in range(3):
            if "t" not in P1M:
                continue
            if g < 2:
                kb.dma("gpsimd", wv[:], D["w0v"][g], wvs, w=[wvb], max_dma_last_dim=4096)
            for (t0, tn) in ttiles:
                pst, pb, _ = psr.next()
                ncol = 512 if g < 2 else 48
                wt_, wb_ = (wv, wvb) if g < 2 else (wsm, wsb)
                for kc in range(16):
                    kb.op("tensor", lambda e, kc=kc, pst=pst, wt_=wt_, t0=t0, tn=tn, ncol=ncol: e.matmul(
                        pst[0:tn, 0:ncol], xsb[:, kc, t0:t0 + tn], wt_[:, kc, 0:ncol],
                        start=(kc == 0), stop=(kc == 15)),
                        r=[wb_, b_x[kc]], w=[pb], inc=(kc == 15))
                if g < 2:
                    st, sb_, ss = sbf.next()
                    kb.op("scalar", lambda e, st=st, pst=pst, tn=tn: e.activation(
                        out=st[0:tn, :], in_=pst[0:tn, :], func=AF.Copy), r=[pb], w=[sb_])
                    kb.dma("sync", D["V"][t0:t0 + tn, g * 512:(g + 1) * 512], st[0:tn, :], ss, r=[sb_])
                    s2, s2b, s2s = sf32.next()
                    kb.op("vector", lambda e, s2=s2, pst=pst, tn=tn: e.tensor_copy(
                        out=s2[0:tn, :], in_=pst[0:tn, :]), r=[pb], w=[s2b])
                    kb.dma("sync", D["o_av"][t0:t0 + tn, g * 512:(g + 1) * 512], s2[0:tn, :], s2s, r=[s2b])
                else:
                    s2, s2b, s2s = sf32.next()
                    kb.op("vector", lambda e, s2=s2, pst=pst, tn=tn: e.tensor_copy(
                        out=s2[0:tn, 0:48], in_=pst[0:tn, 0:48]), r=[pb], w=[s2b])
                    kb.dma("sync", D["IWDT"][t0:t0 + tn, :], s2[0:tn, 0:48], s2s, r=[s2b])
        self.end()


    def phase0(self, gst):
        kb, nc, D, DB = self.kb, self.nc, self.D, self.DB
        C = {}
        def g(name, shape, dt):
            C[name] = gst.enter_context(nc.sbuf_tensor("c_" + name, list(shape), dt))
            return C[name]
        self.C = C
        self.CB = Buf("consts")
        cbuf = self.CB
        g("idf", [128, 128], F32); g("idb", [128, 128], BF16); g("jf", [128, 128], F32)
        g("onesb", [128, 128], BF16); g("onesf", [128, 128], F32); g("b15", [128, 8], F32)
        self.begin()
        s0 = self.sem()
        kb.dma("sync", C["idf"][:], D["ident"], s0, w=[cbuf])
        kb.dma("sync", C["jf"][:], D["antiid"], s0, w=[cbuf])
        kb.dma("sync", C["b15"][:], D["t5"][15:16, :].broadcast_to([128, 8]), s0, w=[cbuf])
        kb.op("vector", lambda e: e.tensor_copy(out=C["idb"][:], in_=C["idf"][:]), r=[cbuf], w=[cbuf])
        kb.op("vector", lambda e: e.memset(C["onesb"][:], 1.0), w=[cbuf])
        kb.op("vector", lambda e: e.memset(C["onesf"][:], 1.0), w=[cbuf])
        tb = self.sb("t5sb", [32, 8], F32)
        oh = self.sb("t5ohsb", [32, 1152], F32)
        fsb = self.sb("fsb", [8, 1152], F32)
        b1 = Buf(); b2 = Buf("fps", psum=True); b3 = Buf()
        kb.dma("sync", tb[:], D["t5"], s0, w=[b1])
        kb.dma("sync", oh[:], D["t5oh"], s0, w=[b1])
        fps = self.ps("fps", [8, 512], F32)
        for i in range(3):
            kb.op("tensor", lambda e, i=i: e.matmul(fps[:, 0:384], tb[:], oh[:, i * 384:(i + 1) * 384],
                                                    start=True, stop=True), r=[b1], w=[b2])
            kb.op("scalar", lambda e, i=i: e.activation(out=fsb[:, i * 384:(i + 1) * 384], in_=fps[:, 0:384],
                                                        func=AF.Copy, scale=1.0 / A_SCALE), r=[b2], w=[b3])
        kb.dma("sync", D["FD"], fsb[:], s0, r=[b3])
        self.end()

    def phase2(self):
        K = self
        import os
        DBG = os.environ.get("P2DBG", "")
        kb, nc, D, DB, C = self.kb, self.nc, self.D, self.DB, self.C
        NB, TP, T = self.NB, self.TP, self.TALL
        cb_ = self.CB
        self.begin()
        LS = PAST + DEC_SEQ
        SW = max(TP, LS)
        ik2 = self.sb("ik2", [128, TP], BF16); b_ik = Buf("ik2")
        ik2s = self.sb("ik2s", [128, LS], BF16); b_iks = Buf("ik2s")
        s_l = self.sem()
        kb.dma("sync", ik2[:], D["IK2"][:, 0:TP], s_l, w=[b_ik])
        kb.dma("gpsimd", ik2s[:, 0:PAST], D["ckidxT2"], s_l, w=[b_iks], max_dma_last_dim=4096)
        kb.dma("sync", ik2s[:, PAST:LS], D["IK2"][:, TP:T], s_l, w=[b_iks])
        iqr = Ring(K, "iqz", [128, 2, 8, 512], BF16, 1)
        iwr = Ring(K, "iwt", [128, 16], F32, 2)
        dgr = Ring(K, "dg", [128, 16, 128], BF16, 1, dma=False)
        trr = Ring(K, "trl", [128, 512], BF16, 3, dma=False)
        scr = Ring(K, "score", [128, SW], F32, 2, dma=False)
        smals = [self.sb("smal%d" % i, [128, 16], F32) for i in range(2)]
        b_sms = [Buf("smal0"), Buf("smal1")]
        Mr = Ring(K, "Msel", [128, SW], BF16, 2, dma=False)
        NKT = max(TP // 128, 9)
        MTs = [self.sb("MT%d" % i, [128, max(NKT - 4 * (1 - i), 9), 512], BF16) for i in range(2)]
        b_MTs = [Buf("MT0"), Buf("MT1")]
        ktr = Ring(K, "kth", [128, SW], BF16, 2)
        vr = Ring(K, "vh", [128, NKT, 128], BF16, 2)
        qr = Ring(K, "qth", [128, 512], BF16, 2)
        wbr = Ring(K, "wbh", [128, 1024], F32, 1)
        ptr = Ring(K, "pt", [128, 512], BF16, 5, dma=False)
        agr = Ring(K, "agt", [128, 512], F32, 1)
        rsb = self.sb("rsb", [128, 512], F32); b_rs = Buf("rs")
        tsb = rsb; b_ts = b_rs
        outr = Ring(K, "aout", [128, 512], BF16, 1)
        dpr = Ring(K, "dps", [128, 512], F32, 2, dma=False, psum=True)
        scp = self.ps("scp", [128, 512], F32); b_scp = Buf("scp", psum=True)
        trp = Ring(K, "trp", [128, 512], BF16, 1, dma=False, psum=True)
        stp = Ring(K, "stp", [128, 512], F32, 2, dma=False, psum=True)
        otp = self.ps("otp", [128, 512], F32); b_otp = Buf("otp", psum=True)
        smp = self.ps("smp", [128, 512], F32); b_smp = Buf("smp", psum=True)

        class Tile:
            pass

        def gen_scores(tl):
            P, Lv = tl.P, tl.Lv
            iwt, iwb, iws = iwr.next()
            kb.dma("sync", iwt[0:P, :], D["IWDT"][tl.iwrows, 0:16], iws, w=[iwb])
            dg, dgb, _ = dgr.next()
            kb.op("vector", lambda e: e.tensor_tensor(
                out=dg[0:P, :, 0:P],
                in0=C["idb"][0:P, 0:P].unsqueeze(1).broadcast_to([P, 16, P]),
                in1=iwt[0:P, :].unsqueeze(2).broadcast_to([P, 16, P]), op=ALU.mult),
                r=[cb_, iwb], w=[dgb])
            tl.sc, tl.scb, _ = scr.next()
            sc, scb = tl.sc, tl.scb
            nkb = (Lv + 511) // 512
            for kbi in range(nkb):
                wk = min(512, Lv - 512 * kbi)
                pend = None
                for h in range(16):
                    dp, dpb, _ = dpr.next()
                    po = (h % 2) * 64
                    kb.op("tensor", lambda e: e.matmul(
                        dp[0:P, 0:wk], tl.iq_l(h), tl.iksb[:, kbi * 512:kbi * 512 + wk],
                        start=True, stop=True), r=[tl.iqbuf, tl.ikb], w=[dpb])
                    tr, trb, _ = trr.next()
                    kb.op("scalar", lambda e: e.activation(
                        out=tr[0:P, 0:wk], in_=dp[0:P, 0:wk], func=AF.Relu), r=[dpb], w=[trb])
                    if pend is not None:
                        pend()
                    def acc(h=h, tr=tr, trb=trb, wk=wk):
                        kb.op("tensor", lambda e: e.matmul(
                            scp[0:P, 0:wk], dg[0:P, h, 0:P], tr[0:P, 0:wk], start=(h == 0), stop=(h == 15)),
                            r=[dgb, trb], w=[b_scp], inc=(h == 15))
                    pend = acc
                    yield 0.45
                pend()
                kb.op("scalar", lambda e: e.activation(
                    out=sc[0:P, kbi * 512:kbi * 512 + wk], in_=scp[0:P, 0:wk], func=AF.Copy),
                    r=[b_scp], w=[scb])
                yield 0.5

        def gen_select(tl):
            P, Lv = tl.P, tl.Lv
            sc, scb = tl.sc, tl.scb
            sm = smals[tl.lane]
            b_sm = b_sms[tl.lane]
            S = sc[0:P, 0:Lv]
            big = Lv / 960.0 + 0.15

            def col(i):
                return sm[0:P, i:i + 1]
            kb.op("vector", lambda e: e.tensor_reduce(out=col(0), in_=S, axis=AX.X, op=ALU.max), r=[scb], w=[b_sm])
            kb.op("vector", lambda e: e.tensor_reduce(out=col(1), in_=S, axis=AX.X, op=ALU.min), r=[scb], w=[b_sm])
            yield 2 * big
            kb.op("vector", lambda e: e.tensor_tensor(out=col(2), in0=col(0), in1=col(1), op=ALU.subtract), r=[b_sm], w=[b_sm])
            kb.op("vector", lambda e: e.tensor_scalar(out=col(2), in0=col(2), scalar1=1e-30, scalar2=None, op0=ALU.max), r=[b_sm], w=[b_sm])
            kb.op("vector", lambda e: e.reciprocal(out=col(3), in_=col(2)), r=[b_sm], w=[b_sm])
            kb.op("vector", lambda e: e.tensor_scalar(out=S, in0=S, scalar1=col(1), scalar2=col(3),
                                                      op0=ALU.subtract, op1=ALU.mult), r=[scb, b_sm], w=[scb])
            if tl.mreg:
                kb.op("vector", lambda e: e.memset(sc[0:64, Lv - 64:Lv], -1.0), w=[scb])
            kb.op("vector", lambda e: e.memset(col(4), 0.0), w=[b_sm])
            yield big + 0.6
            tl.Ms, tl.Mb, _ = Mr.next()
            Ms, Mb = tl.Ms, tl.Mb
            if tl.do_topk:
                for it in range(1, 25):
                    kb.op("vector", lambda e: e.tensor_scalar(
                        out=col(5), in0=col(4), scalar1=2.0 ** (1 - it), scalar2=2.0 ** (-it),
                        op0=ALU.mult, op1=ALU.add), r=[b_sm], w=[b_sm])
                    kb.op("vector", lambda e: e.tensor_scalar(
                        out=Ms[0:P, 0:Lv], in0=S, scalar1=col(5), scalar2=None, op0=ALU.is_ge, op1=ALU.add,
                        accum_out=col(6)), r=[scb, b_sm], w=[Mb, b_sm])
                    kb.op("vector", lambda e: e.tensor_single_scalar(
                        out=col(7), in_=col(6), scalar=TOPK - 0.5, op=ALU.is_ge), r=[b_sm], w=[b_sm])
                    kb.op("vector", lambda e: e.scalar_tensor_tensor(
                        out=col(4), in0=col(4), scalar=2.0, in1=col(7), op0=ALU.mult, op1=ALU.add), r=[b_sm], w=[b_sm])
                    yield big + 0.55
            kb.op("vector", lambda e: e.tensor_scalar(out=col(8), in0=col(4), scalar1=2.0 ** -24, scalar2=None,
                                                      op0=ALU.mult), r=[b_sm], w=[b_sm])
            kb.op("vector", lambda e: e.tensor_scalar(out=Ms[0:P, 0:Lv], in0=S, scalar1=col(8), scalar2=None,
                                                      op0=ALU.is_ge), r=[scb, b_sm], w=[Mb])
            yield big + 0.2

        def gen_transpose(tl, MT, b_MT, c):
            P, Lv = tl.P, tl.Lv
            Ms, Mb = tl.Ms, tl.Mb
            nkt = (Lv + 127) // 128
            for j0 in range(0, nkt, 4):
                ng = min(4, nkt - j0)
                tp, tpb, _ = trp.next()
                kn_last = 128
                for jj in range(ng):
                    j = j0 + jj
                    kn = min(128, Lv - 128 * j)
                    kn_last = kn
                    kb.op("tensor", lambda e: e.transpose(
                        tp[0:kn, jj * 128:jj * 128 + P], Ms[0:P, j * 128:j * 128 + kn], C["idb"][0:P, 0:P]),
                        r=[Mb, cb_], w=[tpb], inc=(jj == ng - 1))
                nfull = ng if kn_last == 128 else ng - 1
                if nfull > 0:
                    kb.op("scalar", lambda e: e.activation(
                        out=MT[:, j0:j0 + nfull, c * 128:c * 128 + P],
                        in_=tp[:, 0:nfull * 128].rearrange("p (a b) -> p a b", b=128)[:, :, 0:P],
                        **(dict(func=AF.Copy) if "A" in DBG else dict(func=AF.Identity, scale=-NEG, bias=NEG))), r=[tpb], w=[b_MT])
                if kn_last != 128:
                    kb.op("scalar", lambda e: e.activation(
                        out=MT[0:kn_last, j0 + ng - 1, c * 128:c * 128 + P],
                        in_=tp[0:kn_last, (ng - 1) * 128:(ng - 1) * 128 + P],
                        **(dict(func=AF.Copy) if "A" in DBG else dict(func=AF.Identity, scale=-NEG, bias=NEG))), r=[tpb], w=[b_MT])
                yield 0.5

        def gen_attend(h, Q, qcol0, tiles, load_k, load_v, MT, b_MT):
            kt, ktb, kts = ktr.next()
            load_k(kt, ktb, kts)
            vt, vtb, vts = vr.next()
            load_v(vt, vtb, vts)
            qt, qtb, qts = qr.next()
            kb.dma("sync", qt[:, 0:Q], D["QT"][h * 128:(h + 1) * 128, qcol0:qcol0 + Q], qts, w=[qtb])
            wb, wbb, wbs = wbr.next()
            kb.dma("sync", wb[:], bass.AP(D["FD"].tensor, h * 1152, [[1, 128], [1, 1024]]), wbs, w=[wbb])
            ag, agb, ags = agr.next()
            kb.dma("sync", ag[:, 0:Q], D["AGS"][h * 128:(h + 1) * 128, qcol0:qcol0 + Q], ags, w=[agb])
            n = len(tiles)
            sts = [None] * n

            def emit_st(j):
                kn, c0, win = tiles[j]
                st_, stb, _ = stp.next()
                sts[j] = (st_, stb)
                kb.op("tensor", lambda e: e.matmul(st_[0:kn, c0:Q], kt[:, j * 128:j * 128 + kn], qt[:, c0:Q],
                                                   start=True, stop=False),
                      r=[ktb, qtb], w=[stb], inc=False)
                if win is not None:
                    off = 128 - kn
                    kb.op("tensor", lambda e: e.matmul(st_[0:kn, c0:Q], C["jf"][0:kn, off:128],
                                                       wb[0:kn, win + off + c0:win + off + Q], start=False, stop=False),
                          r=[cb_, wbb], w=[stb], inc=False)
                if "B" in DBG:
                    kb.op("tensor", lambda e: e.matmul(st_[0:kn, c0:Q], C["idb"][0:kn, 0:kn], qt[0:kn, c0:Q],
                                                       start=False, stop=True), r=[cb_, qtb], w=[stb])
                else:
                    kb.op("tensor", lambda e: e.matmul(st_[0:kn, c0:Q], C["idb"][0:kn, 0:kn], MT[0:kn, j, c0:Q],
                                                       start=False, stop=True), r=[cb_, b_MT], w=[stb])
            ahead = len(stp.items) - 1
            for j0 in range(min(ahead, n)):
                emit_st(j0)
            for j in range(n):
                kn, c0, win = tiles[j]
                if j + ahead < n:
                    emit_st(j + ahead)
                st_, stb = sts[j]
                pt, ptb, _ = ptr.next()
                if win is None:
                    kb.op("scalar", lambda e: e.activation(
                        out=pt[0:kn, c0:Q], in_=st_[0:kn, c0:Q], func=AF.Exp, scale=A_SCALE,
                        bias=C["b15"][0:kn, h:h + 1]), r=[stb, cb_], w=[ptb])
                else:
                    kb.op("scalar", lambda e: e.activation(
                        out=pt[0:kn, c0:Q], in_=st_[0:kn, c0:Q], func=AF.Exp, scale=A_SCALE),
                        r=[stb], w=[ptb])
                if "C" in DBG:
                    kb.op("gpsimd", lambda e: e.tensor_tensor(
                        out=pt[0:kn, c0:Q], in0=pt[0:kn, c0:Q], in1=MT[0:kn, j, c0:Q], op=ALU.mult),
                        r=[ptb, b_MT], w=[ptb])
                kb.op("tensor", lambda e: e.matmul(
                    otp[:, c0:Q], vt[0:kn, j, :], pt[0:kn, c0:Q], start=(j == 0), stop=(j == n - 1)),
                    r=[vtb, ptb], w=[b_otp], inc=(j == n - 1))
                kb.op("tensor", lambda e: e.matmul(
                    smp[:, c0:Q], C["onesb"][0:kn, :], pt[0:kn, c0:Q], start=(j == 0), stop=(j == n - 1)),
                    r=[cb_, ptb], w=[b_smp], inc=(j == n - 1))
                yield (1.1 if win is None else 2.0) * (Q - c0) / 512.0 + 0.05
            kb.op("vector", lambda e: e.reciprocal(out=rsb[:, 0:Q], in_=smp[:, 0:Q]), r=[b_smp], w=[b_rs])
            kb.op("vector", lambda e: e.tensor_tensor(out=tsb[:, 0:Q], in0=otp[:, 0:Q], in1=rsb[:, 0:Q], op=ALU.mult),
                  r=[b_otp, b_rs], w=[b_ts])
            ot, otb, ots = outr.next()
            kb.op("vector", lambda e: e.tensor_tensor(out=ot[:, 0:Q], in0=tsb[:, 0:Q], in1=ag[:, 0:Q], op=ALU.mult),
                  r=[b_ts, agb], w=[otb])
            kb.dma("sync", D["MIXT"][h * 128:(h + 1) * 128, qcol0:qcol0 + Q], ot[:, 0:Q], ots, r=[otb])
            yield 0.3

        def prompt_tiles(qb):
            tiles = []
            for j in range(4 * qb + 4):
                r_ = j - 4 * qb
                tiles.append((128, 128 * max(r_, 0), (128 * (3 - r_)) if r_ >= -1 else None))
            return tiles

        def attn_block(qb):
            Lk = 512 * (qb + 1)
            tiles = prompt_tiles(qb)
            gens = []
            for h in range(A_HEADS):
                def load_k(kt, ktb, kts, h=h, Lk=Lk):
                    kb.dma("sync", kt[:, 0:Lk], D["KT"][h * 128:(h + 1) * 128, 0:Lk], kts, w=[ktb])
                def load_v(vt, vtb, vts, h=h, Lk=Lk):
                    kb.dma("sync", vt[:, 0:Lk // 128, :],
                           D["V"][0:Lk, h * 128:(h + 1) * 128].rearrange("(j p) d -> p j d", p=128), vts, w=[vtb])
                gens.append(gen_attend(h, 512, qb * 512, tiles, load_k, load_v, MTs[qb % 2], b_MTs[qb % 2]))
            return gens

        def index_tiles(nb):
            iq, iqb_, iqs = iqr.next()
            src = D["IQ"][:, nb * 512:(nb + 1) * 512].rearrange("(a two p) t -> two p a t", two=2, p=64)
            kb.dma("sync", iq[0:64, 0, :, :], src[0], iqs, w=[iqb_])
            kb.dma("sync", iq[64:128, 1, :, :], src[1], iqs, w=[iqb_])
            tls = []
            for c in range(4):
                i = nb * 4 + c
                tl = Tile()
                tl.P = 128; tl.Lv = 128 * (i + 1); tl.iksb = ik2; tl.ikb = b_ik; tl.iqbuf = iqb_
                tl.iq_l = (lambda h, iq=iq, c=c: iq[:, h % 2, h // 2, c * 128:(c + 1) * 128])
                tl.iwrows = slice(i * 128, (i + 1) * 128); tl.mreg = True; tl.do_topk = (i >= 2); tl.lane = 0
                tls.append(tl)
            return tls

        for it_ in iqr.items:
            kb.op("gpsimd", lambda e: e.memset(it_[0][:], 0.0), w=[it_[1]])
        kb.op("gpsimd", lambda e: e.memset(MTs[0][:], 0.0), w=[b_MTs[0]])
        kb.op("gpsimd", lambda e: e.memset(MTs[1][:], 0.0), w=[b_MTs[1]])
        for step in range(NB + 1):
            nb = step if step < NB else None
            qb = step - 1 if step >= 1 else None
            tls = index_tiles(nb) if nb is not None else None
            agens = attn_block(qb) if qb is not None else None

            def pe_lane():
                if tls is not None:
                    yield from gen_scores(tls[0])
                    yield ("set", ("I1", 0))
                for c in range(4):
                    if tls is not None and c < 3:
                        yield from gen_scores(tls[c + 1])
                        yield ("set", ("I1", c + 1))
                    if agens is not None:
                        yield from agens[2 * c]
                        yield from agens[2 * c + 1]
                    if tls is not None:
                        yield ("wait", ("I2", c))
                        yield from gen_transpose(tls[c], MTs[nb % 2], b_MTs[nb % 2], c)

            def dve_lane():
                if tls is not None:
                    for c in range(4):
                        yield ("wait", ("I1", c))
                        yield from gen_select(tls[c])
                        yield ("set", ("I2", c))
            if tls is None:
                stp.items = stp.items + dpr.items
            run_lanes([pe_lane(), dve_lane()])
            if tls is None:
                stp.items = stp.items[:2]

        iq, iqb_, iqs = iqr.next()
        src = D["IQ"][:, TP:T].rearrange("(a two p) t -> two p a t", two=2, p=64)
        kb.dma("sync", iq[0:64, 0, :, 0:64], src[0], iqs, w=[iqb_])
        kb.dma("sync", iq[64:128, 1, :, 0:64], src[1], iqs, w=[iqb_])
        tl = Tile()
        tl.P = 64; tl.Lv = LS; tl.iksb = ik2s; tl.ikb = b_iks; tl.iqbuf = iqb_
        tl.iq_l = (lambda h, iq=iq: iq[:, h % 2, h // 2, 0:64])
        tl.iwrows = slice(TP, T); tl.mreg = False; tl.do_topk = True; tl.lane = 0
        MTx, b_MTx = MTs[NB % 2], b_MTs[NB % 2]
        for g_ in (gen_scores(tl), gen_select(tl), gen_transpose(tl, MTx, b_MTx, 0)):
            for _ in g_:
                pass
        tiles = [(128, 0, 512 if j == 7 else None) for j in range(8)] + [(64, 0, 384)]
        for h in range(A_HEADS):
            def load_k(kt, ktb, kts, h=h):
                kb.dma("gpsimd", kt[:, 0:PAST], D["ckT"][h * 128:(h + 1) * 128, :], kts, w=[ktb], max_dma_last_dim=4096)
                kb.dma("sync", kt[:, PAST:LS], D["KT"][h * 128:(h + 1) * 128, TP:T], kts, w=[ktb])
            def load_v(vt, vtb, vts, h=h):
                kb.dma("gpsimd", vt[:, 0:8, :],
                       D["cv"][:, h * 128:(h + 1) * 128].rearrange("(j p) d -> p j d", p=128), vts, w=[vtb])
                kb.dma("sync", vt[0:64, 8, :], D["V"][TP:T, h * 128:(h + 1) * 128], vts, w=[vtb])
            for _ in gen_attend(h, 64, TP, tiles, load_k, load_v, MTx, b_MTx):
                pass
        self.end()

    def phase3(self):
        K = self
        kb, nc, D, DB, C = self.kb, self.nc, self.D, self.DB, self.C
        NB, TP, T = self.NB, self.TP, self.TALL
        cb_ = self.CB
        self.begin()
        sl = self.sem()
        pc = Buf("p3const")
        dtb = self.sb("dtb", [128, 32], F32); abc = self.sb("abc", [128, 32], F32)
        dsk = self.sb("dsk", [128, 16], F32); nw = self.sb("nw", [128, 16], F32)
        U = self.sb("U", [128, 128], F32); negm = self.sb("negm_sb", [128, 128], F32)
        NEGb = self.sb("NEGb", [128, 1024], F32)
        kb.dma("sync", dtb[:], D["dtbias"].broadcast_to([128, 32]), sl, w=[pc])
        kb.dma("sync", abc[:], D["alog"].broadcast_to([128, 32]), sl, w=[pc])
        kb.dma("sync", dsk[:], D["dskT"], sl, w=[pc])
        kb.dma("sync", nw[:], D["nwT"], sl, w=[pc])
        kb.dma("sync", U[:], D["utri"], sl, w=[pc])
        kb.dma("sync", negm[:], D["negm"], sl, w=[pc])
        kb.op("scalar", lambda e: e.activation(out=abc[:], in_=abc[:], func=AF.Exp), r=[pc], w=[pc])
        kb.op("vector", lambda e: e.tensor_scalar(out=abc[:], in0=abc[:], scalar1=-1.0, scalar2=None, op0=ALU.mult),
              r=[pc], w=[pc])
        hT = self.sb("hT", [128, 2048], F32); b_hT = Buf("hT")
        hTb = self.sb("hTb", [128, 2048], BF16); b_hTb = Buf("hTb")
        xcr = Ring(K, "xcb", [128, 24, 128], BF16, 2)
        xsr = Ring(K, "xsf", [128, 16, 128], F32, 2)
        zsr = Ring(K, "zsf", [128, 16, 128], F32, 2)
        dtr_ = Ring(K, "dtr", [128, 32], F32, 2)
        xtkr = Ring(K, "xtok", [128, 2560], BF16, 2, dma=False)
        smr = Ring(K, "sm3", [128, 8, 32], F32, 2, dma=False)
        xdtr = Ring(K, "xdt", [128, 2048], BF16, 2, dma=False)
        xdt2r = Ring(K, "xdt2", [128, 2048], BF16, 2, dma=False)
        decr = Ring(K, "dec", [128, 32], F32, 2, dma=False)
        Dpr = Ring(K, "Dp", [128, 1024], F32, 2, dma=False)
        exr = Ring(K, "expA", [128, 1024], F32, 2, dma=False)
        sgr = Ring(K, "seg", [128, 1024], F32, 2, dma=False)
        wtr = Ring(K, "Wt", [128, 1024], BF16, 4, dma=False)
        cpr = Ring(K, "CpT", [128, 1024], BF16, 4, dma=False)
        ygt = Ring(K, "ygt", [128, 128], F32, 2, dma=False)
        yg = self.sb("yg", [128, 16, 128], F32); b_yg = Buf("yg")
        sqr = Ring(K, "sqt", [128, 128], F32, 2, dma=False)
        sd = self.sb("sd", [128, 128], F32); b_sd = Buf("sd")
        rstd = self.sb("rstd", [128, 128], F32); b_rstd = Buf("rstd")
        bor = Ring(K, "bo", [128, 128], BF16, 4)
        acr = Ring(K, "acb", [128, 1024], F32, 2, dma=False, psum=True)
        cbp = self.ps("cbp", [128, 512], F32); b_cbp = Buf("cbp", psum=True)
        ytr = Ring(K, "yT", [128, 512], F32, 1, dma=False, psum=True)
        hup = self.ps("hup", [128, 512], F32); b_hup = Buf("hup", psum=True)
        trp = hup[:].bitcast(BF16); b_trp = b_hup
        msc = self.ps("msc", [128, 512], F32); b_msc = Buf("msc", psum=True)
        kb.op("gpsimd", lambda e: e.tensor_copy(
            out=NEGb[:].rearrange("p (r t) -> p r t", t=128),
            in_=negm[:].unsqueeze(1).broadcast_to([128, 8, 128])), r=[pc], w=[pc])

        def v3(ap, L):
            return ap.rearrange("p (r t) -> p r t", t=L)

        class Cx:
            pass

        def prologue(t0, L, have_state):
            cx = Cx()
            cx.t0, cx.L, cx.have_state = t0, L, have_state
            xc, xcb_, xcs = xcr.next()
            kb.dma("gpsimd", xc[:, :, 0:L], D["XC"][:, t0:t0 + L].rearrange("(a p) t -> p a t", p=128), xcs, w=[xcb_])
            xs, xsb_, xss = xsr.next()
            kb.dma("sync", xs[:, :, 0:L], D["XC"][0:2048, t0:t0 + L].rearrange("(a p) t -> p a t", p=128), xss, w=[xsb_])
            zs, zsb_, zss = zsr.next()
            kb.dma("sync", zs[:, :, 0:L], D["ZS"][:, t0:t0 + L].rearrange("(a p) t -> p a t", p=128), zss, w=[zsb_])
            dr, drb, drs = dtr_.next()
            kb.dma("sync", dr[0:L, :], D["IWDT"][t0:t0 + L, 16:48], drs, w=[drb])
            xtk, b_xtk, _ = xtkr.next()
            sm, b_sm, _ = smr.next()
            xdt, b_xdt, _ = xdtr.next()
            xdt2, b_xdt2, _ = xdt2r.next()
            dec, b_dec, _ = decr.next()
            cx.xc, cx.xcb_, cx.xs, cx.xsb_, cx.zs, cx.zsb_ = xc, xcb_, xs, xsb_, zs, zsb_
            cx.xtk, cx.b_xtk, cx.sm, cx.b_sm, cx.xdt, cx.b_xdt = xtk, b_xtk, sm, b_sm, xdt, b_xdt
            cx.xdt2, cx.b_xdt2, cx.dec, cx.b_dec = xdt2, b_xdt2, dec, b_dec
            for m0 in range(0, 20, 8):
                ng = min(8, 20 - m0)
                for mm in range(ng):
                    m = m0 + mm
                    kb.op("tensor", lambda e: e.transpose(
                        trp[0:L, mm * 128:(mm + 1) * 128], xc[:, m, 0:L], C["idb"][:, :]),
                        r=[xcb_, cb_], w=[b_trp], inc=(mm == ng - 1))
                kb.op("scalar", lambda e: e.activation(
                    out=xtk[0:L, m0 * 128:(m0 + ng) * 128], in_=trp[0:L, 0:ng * 128], func=AF.Copy),
                    r=[b_trp], w=[b_xtk])
            dt_ = sm[0:L, 0, :]; dA = sm[0:L, 1, :]; act = sm[0:L, 2, :]; tl = sm[0:L, 3, :]; tmp = sm[0:L, 4, :]
            cx.dA, cx.act = dA, act
            kb.op("vector", lambda e: e.tensor_tensor(out=tmp, in0=dr[0:L, :], in1=dtb[0:L, :], op=ALU.add),
                  r=[drb, pc], w=[b_sm])
            kb.op("scalar", lambda e: e.activation(out=tmp, in_=tmp, func=AF.Exp), r=[b_sm], w=[b_sm])
            kb.op("scalar", lambda e: e.activation(out=dt_, in_=tmp, func=AF.Ln, bias=1.0), r=[b_sm], w=[b_sm])
            kb.op("vector", lambda e: e.tensor_tensor(out=dA, in0=dt_, in1=abc[0:L, :], op=ALU.mult), r=[b_sm, pc], w=[b_sm])
            kb.op("tensor", lambda e: e.matmul(msc[0:L, 0:32], U[0:L, 0:L], dA, start=True, stop=True),
                  r=[pc, b_sm], w=[b_msc])
            kb.op("tensor", lambda e: e.matmul(msc[:, 32:64], C["onesf"][0:L, :], dA, start=True, stop=True),
                  r=[cb_, b_sm], w=[b_msc])
            kb.op("vector", lambda e: e.tensor_copy(out=act, in_=msc[0:L, 0:32]), r=[b_msc], w=[b_sm])
            kb.op("vector", lambda e: e.tensor_tensor(out=tl, in0=msc[0:L, 32:64], in1=act, op=ALU.subtract),
                  r=[b_msc, b_sm], w=[b_sm])
            kb.op("scalar", lambda e: e.activation(out=tl, in_=tl, func=AF.Exp), r=[b_sm], w=[b_sm])
            kb.op("scalar", lambda e: e.activation(out=dec[:], in_=msc[:, 32:64], func=AF.Exp), r=[b_msc], w=[b_dec])
            kb.op("vector", lambda e: e.tensor_tensor(
                out=xdt[0:L, :].rearrange("p (h q) -> p h q", q=64),
                in0=xtk[0:L, 0:2048].rearrange("p (h q) -> p h q", q=64),
                in1=dt_.unsqueeze(2).broadcast_to([L, 32, 64]), op=ALU.mult), r=[b_xtk, b_sm], w=[b_xdt])
            kb.op("gpsimd", lambda e: e.tensor_tensor(
                out=xdt2[0:L, :].rearrange("p (h q) -> p h q", q=64),
                in0=xdt[0:L, :].rearrange("p (h q) -> p h q", q=64),
                in1=tl.unsqueeze(2).broadcast_to([L, 32, 64]), op=ALU.mult), r=[b_xdt, b_sm], w=[b_xdt2])
            def emit_cb():
                for g in range(4):
                    kb.op("tensor", lambda e: e.matmul(cbp[0:L, g * 128:g * 128 + L], xc[:, 16 + g, 0:L],
                                                       xc[:, 20 + g, 0:L], start=True, stop=True),
                          r=[xcb_], w=[b_cbp], inc=(g == 3))
            cx.emit_cb = emit_cb
            cx.front = {}
            cx.f1 = {}
            return cx

        def front(cx, g):
            L, have_state = cx.L, cx.have_state
            W8 = 8 * L
            dA, act, xc, xcb_, b_sm = cx.dA, cx.act, cx.xc, cx.xcb_, cx.b_sm
            Dp, Dpb, _ = Dpr.next()
            kb.op("gpsimd", lambda e: e.tensor_tensor(
                out=v3(Dp[0:L, 0:W8], L), in0=dA[:, 8 * g:8 * g + 8].unsqueeze(2).broadcast_to([L, 8, L]),
                in1=U[0:L, 0:L].unsqueeze(1).broadcast_to([L, 8, L]), op=ALU.mult), r=[b_sm, pc], w=[Dpb])
            yield 1.0
            acb, b_acb, _ = acr.next()
            nmm = (W8 + 511) // 512
            for i in range(nmm):
                kb.op("tensor", lambda e: e.matmul(
                    acb[:, i * 512:(i + 1) * 512], C["onesf"][0:L, :], Dp[0:L, i * 512:(i + 1) * 512],
                    start=True, stop=True), r=[cb_, Dpb], w=[b_acb], inc=(i == nmm - 1))
            yield 1.0
            ex, exb, _ = exr.next()
            kb.op("scalar", lambda e: e.activation(out=ex[:, 0:W8], in_=acb[:, 0:W8], func=AF.Exp),
                  r=[b_acb], w=[exb])
            yield 1.0
            for i in range(nmm):
                if L == 128:
                    rhs = NEGb[0:L, i * 512:(i + 1) * 512]
                else:
                    rhs = NEGb[0:L, :].rearrange("p (r t) -> p r t", t=128)[:, :, 0:L]
                kb.op("tensor", lambda e: e.matmul(
                    acb[0:L, i * 512:(i + 1) * 512], C["idf"][0:L, 0:L], rhs,
                    start=False, stop=True, skip_group_check=True), r=[cb_, pc], w=[b_acb], inc=(i == nmm - 1))
            cx.f1[g] = (acb, b_acb, ex, exb)
            yield 1.0

        def front2(cx, g):
            L, have_state = cx.L, cx.have_state
            W8 = 8 * L
            dA, act, xc, xcb_, b_sm = cx.dA, cx.act, cx.xc, cx.xcb_, cx.b_sm
            acb, b_acb, ex, exb = cx.f1[g]
            sg, sgb, _ = sgr.next()
            kb.op("vector", lambda e: e.tensor_tensor(
                out=v3(sg[0:L, 0:W8], L), in0=v3(acb[0:L, 0:W8], L),
                in1=act[:, 8 * g:8 * g + 8].unsqueeze(2).broadcast_to([L, 8, L]), op=ALU.subtract),
                r=[b_acb, b_sm], w=[sgb])
            yield 1.0
            kb.op("scalar", lambda e: e.activation(out=sg[0:L, 0:W8], in_=sg[0:L, 0:W8], func=AF.Exp),
                  r=[sgb], w=[sgb])
            yield 1.0
            wt, wtb, _ = wtr.next()
            kb.op("vector", lambda e: e.tensor_tensor(
                out=v3(wt[0:L, 0:W8], L), in0=v3(sg[0:L, 0:W8], L),
                in1=cbp[0:L, g * 128:g * 128 + L].unsqueeze(1).broadcast_to([L, 8, L]), op=ALU.mult),
                r=[sgb, b_cbp], w=[wtb])
            cp = cpb = None
            if have_state:
                cp, cpb, _ = cpr.next()
                kb.op("gpsimd", lambda e: e.tensor_tensor(
                    out=v3(cp[:, 0:W8], L), in0=xc[:, 20 + g, 0:L].unsqueeze(1).broadcast_to([128, 8, L]),
                    in1=v3(ex[:, 0:W8], L), op=ALU.mult), r=[xcb_, exb], w=[cpb])
            cx.front[g] = (wt, wtb, cp, cpb)
            yield 1.0

        def back(cx, g):
            L, have_state, t0 = cx.L, cx.have_state, cx.t0
            wt, wtb, cp, cpb = cx.front[g]
            xdt, b_xdt, xdt2, b_xdt2, xtk, b_xtk = cx.xdt, cx.b_xdt, cx.xdt2, cx.b_xdt2, cx.xtk, cx.b_xtk
            xs, xsb_, zs, zsb_, dec, b_dec = cx.xs, cx.xsb_, cx.zs, cx.zsb_, cx.dec, cx.b_dec
            yT, yTb, _ = ytr.next()
            for r_ in range(8):
                h = 8 * g + r_
                ml = r_ // 2
                po = 64 * (h % 2)
                kb.op("tensor", lambda e: e.matmul(
                    yT[po:po + 64, ml * 128:ml * 128 + L], xdt[0:L, h * 64:(h + 1) * 64],
                    wt[0:L, r_ * L:(r_ + 1) * L], start=True, stop=(not have_state)),
                    r=[b_xdt, wtb], w=[yTb], inc=(r_ == 7 and not have_state))
                if have_state:
                    kb.op("tensor", lambda e: e.matmul(
                        yT[po:po + 64, ml * 128:ml * 128 + L], hTb[:, h * 64:(h + 1) * 64],
                        cp[:, r_ * L:(r_ + 1) * L], start=False, stop=True),
                        r=[b_hTb, cpb], w=[yTb], inc=(r_ == 7))
                if r_ % 2 == 1:
                    yield 0.5
            kb.op("tensor", lambda e: e.matmul(hup[:, :], xtk[0:L, 2048 + g * 128:2048 + (g + 1) * 128],
                                               xdt2[0:L, g * 512:(g + 1) * 512], start=True, stop=True),
                  r=[b_xtk, b_xdt2], w=[b_hup])
            yield 0.5
            hs = hT[:, g * 512:(g + 1) * 512]
            if have_state:
                kb.op("vector", lambda e: e.tensor_tensor(
                    out=hs.rearrange("p (r q) -> p r q", q=64), in0=hs.rearrange("p (r q) -> p r q", q=64),
                    in1=dec[:, 8 * g:8 * g + 8].unsqueeze(2).broadcast_to([128, 8, 64]), op=ALU.mult),
                    r=[b_dec, b_hT], w=[b_hT])
                kb.op("vector", lambda e: e.tensor_tensor(out=hs, in0=hup[:, :], in1=hs, op=ALU.add),
                      r=[b_hup, b_hT], w=[b_hT])
            else:
                kb.op("vector", lambda e: e.tensor_copy(out=hs, in_=hup[:, :]), r=[b_hup], w=[b_hT])
            kb.op("scalar", lambda e: e.activation(out=hTb[:, g * 512:(g + 1) * 512], in_=hs, func=AF.Copy),
                  r=[b_hT], w=[b_hTb])
            yield 0.7
            for ml in range(4):
                m = 4 * g + ml
                yt, ytb, _ = ygt.next()
                kb.op("vector", lambda e: e.scalar_tensor_tensor(
                    out=yt[:, 0:L], in0=xs[:, m, 0:L], scalar=dsk[:, m:m + 1], in1=yT[:, ml * 128:ml * 128 + L],
                    op0=ALU.mult, op1=ALU.add), r=[xsb_, pc, yTb], w=[ytb])
                kb.op("gpsimd", lambda e: e.tensor_tensor(
                    out=yg[:, m, 0:L], in0=yt[:, 0:L], in1=zs[:, m, 0:L], op=ALU.mult), r=[ytb, zsb_], w=[b_yg])
                sq, sqb, _ = sqr.next()
                kb.op("scalar", lambda e: e.activation(out=sq[:, 0:L], in_=yg[:, m, 0:L], func=AF.Square),
                      r=[b_yg], w=[sqb])
                kb.op("tensor", lambda e: e.matmul(msc[:, 128:128 + L], C["onesf"][:, :], sq[:, 0:L],
                                                   start=(ml == 0), stop=(ml == 3)),
                      r=[cb_, sqb], w=[b_msc])
                yield 0.7
            kb.op("scalar", lambda e: e.activation(out=sd[:, 0:L], in_=msc[:, 128:128 + L], func=AF.Ln,
                                                   scale=1.0 / 512.0, bias=EPS), r=[b_msc], w=[b_sd])
            kb.op("scalar", lambda e: e.activation(out=rstd[:, 0:L], in_=sd[:, 0:L], func=AF.Exp, scale=-0.5),
                  r=[b_sd], w=[b_rstd])
            yield 0.7
            for ml in range(4):
                m = 4 * g + ml
                bo, bob, bos = bor.next()
                kb.op("vector", lambda e: e.scalar_tensor_tensor(
                    out=bo[:, 0:L], in0=yg[:, m, 0:L], scalar=nw[:, m:m + 1], in1=rstd[:, 0:L],
                    op0=ALU.mult, op1=ALU.mult), r=[b_yg, pc, b_rstd], w=[bob])
                kb.dma("sync", D["MIXT"][1024 + m * 128:1024 + (m + 1) * 128, t0:t0 + L], bo[:, 0:L], bos, r=[bob])
                yield 0.3

        so = self.sem()
        chunks = [(c * 128, 128, c > 0) for c in range(TP // 128)]
        items = [(ci, g) for ci in range(len(chunks)) for g in range(4)]
        cxs = {}
        LAG = 2

        LAG = 3
        for k in range(len(items) + LAG):
            lanes = []
            if k < len(items):
                ci, g = items[k]
                if g == 0:
                    cxs[ci] = prologue(*chunks[ci])
                lanes.append(front(cxs[ci], g))
            if 0 <= k - 1 < len(items):
                ci, g = items[k - 1]
                if g == 0:
                    cxs[ci].emit_cb()
                lanes.append(front2(cxs[ci], g))
            if k - LAG >= 0:
                ci, g = items[k - LAG]
                lanes.append(back(cxs[ci], g))
            run_lanes(lanes)
        kb.dma("sync", D["o_ssmT"][0], hT[:], so, r=[b_hT])
        kb.dma("sync", hT[:], D["h0T"], so, w=[b_hT])
        kb.op("scalar", lambda e: e.activation(out=hTb[:], in_=hT[:], func=AF.Copy), r=[b_hT], w=[b_hTb])
        cxs_ = prologue(TP, DEC_SEQ, True)
        for k in range(4 + LAG):
            lanes = []
            if k < 4:
                lanes.append(front(cxs_, k))
            if 0 <= k - 1 < 4:
                if k - 1 == 0:
                    cxs_.emit_cb()
                lanes.append(front2(cxs_, k - 1))
            if k - LAG >= 0:
                lanes.append(back(cxs_, k - LAG))
            run_lanes(lanes)
        kb.dma("sync", D["o_ssmT"][1], hT[:], so, r=[b_hT])
        self.end()

    def outproj_ln(self, mixname, KC, wname, resid, gname, bname, dest):
        K = self
        kb, nc, D, DB, C = self.kb, self.nc, self.D, self.DB, self.C
        cb_ = self.CB
        self.begin()
        sl = self.sem()
        pc = Buf("opc")
        w = self.sb("wo", [128, KC, 2048], BF16)
        for kc in range(KC):
            kb.dma("gpsimd", w[:, kc, :], D[wname][:, kc, :], sl, w=[pc], max_dma_last_dim=4096)
        g = self.sb("lng", [128, 16], F32); bb = self.sb("lnb", [128, 16], F32)
        kb.dma("sync", g[:], D[gname], sl, w=[pc]); kb.dma("sync", bb[:], D[bname], sl, w=[pc])
        mxr = Ring(K, "mx", [128, KC, 512], BF16, 2)
        rsr = Ring(K, "rs", [128, 512], F32, 3)
        z = self.sb("z", [128, 16, 512], F32); zb = [Buf("z%d" % i) for i in range(16)]
        sqr = Ring(K, "sq", [128, 512], F32, 2, dma=False)
        sd = self.sb("sd5", [128, 512], F32); b_sd = Buf("sd5")
        tr_ = Ring(K, "t5t", [128, 512], F32, 2, dma=False)
        outr = Ring(K, "o5", [128, 512], F32, 3)
        psr = Ring(K, "ps5", [128, 512], F32, 4, dma=False, psum=True)
        mean = self.ps("mean5", [128, 512], F32); b_mean = Buf("mean5", psum=True)
        var = self.ps("var5", [128, 512], F32); b_var = Buf("var5", psum=True)
        def L1(t0, tn, m, mx, mxb):
            rs, rsb_, rss = rsr.next()
            kb.dma("sync", rs[:, 0:tn], D[resid][m * 128:(m + 1) * 128, t0:t0 + tn], rss, w=[rsb_])
            ps, pb, _ = psr.next()
            for kc in range(KC):
                kb.op("tensor", lambda e: e.matmul(
                    ps[:, 0:tn], w[:, kc, m * 128:(m + 1) * 128], mx[:, kc, 0:tn],
                    start=(kc == 0), stop=(kc == KC - 1)), r=[pc, mxb], w=[pb], inc=(kc == KC - 1))
            kb.op("vector", lambda e: e.scalar_tensor_tensor(
                out=z[:, m, 0:tn], in0=rs[:, 0:tn], scalar=ALPHA, in1=ps[:, 0:tn], op0=ALU.mult, op1=ALU.add),
                r=[rsb_, pb], w=[zb[m]])
            kb.op("tensor", lambda e: e.matmul(mean[:, 0:tn], C["onesf"][:, :], z[:, m, 0:tn],
                                               start=(m == 0), stop=(m == 15)), r=[cb_, zb[m]], w=[b_mean])

        def L2(t0, tn):
            for m in range(16):
                kb.op("vector", lambda e: e.scalar_tensor_tensor(
                    out=z[:, m, 0:tn], in0=mean[:, 0:tn], scalar=-1.0 / 2048.0, in1=z[:, m, 0:tn],
                    op0=ALU.mult, op1=ALU.add), r=[b_mean, zb[m]], w=[zb[m]])
                sq, sqb, _ = sqr.next()
                kb.op("scalar", lambda e: e.activation(out=sq[:, 0:tn], in_=z[:, m, 0:tn], func=AF.Square),
                      r=[zb[m]], w=[sqb])
                kb.op("tensor", lambda e: e.matmul(var[:, 0:tn], C["onesf"][:, :], sq[:, 0:tn],
                                                   start=(m == 0), stop=(m == 15)), r=[cb_, sqb], w=[b_var])
            rd, rdb, _ = rstdr.next()
            kb.op("scalar", lambda e: e.activation(out=sd[:, 0:tn], in_=var[:, 0:tn], func=AF.Sqrt,
                                                   scale=1.0 / 2048.0, bias=EPS), r=[b_var], w=[b_sd])
            kb.op("vector", lambda e: e.reciprocal(out=rd[:, 0:tn], in_=sd[:, 0:tn]), r=[b_sd], w=[rdb])
            return rd, rdb

        def L3(t0, tn, m, rd, rdb):
            tt, ttb, _ = tr_.next()
            kb.op("vector", lambda e: e.scalar_tensor_tensor(
                out=tt[:, 0:tn], in0=z[:, m, 0:tn], scalar=g[:, m:m + 1], in1=rd[:, 0:tn],
                op0=ALU.mult, op1=ALU.mult), r=[zb[m], pc, rdb], w=[ttb])
            ot, otb, ots = outr.next()
            kb.op("scalar", lambda e: e.activation(
                out=ot[:, 0:tn], in_=tt[:, 0:tn], func=AF.Identity, bias=bb[:, m:m + 1]), r=[ttb, pc], w=[otb])
            kb.dma("sync", D[dest][m * 128:(m + 1) * 128, t0:t0 + tn], ot[:, 0:tn], ots, r=[otb])

        rstdr = Ring(K, "rstdr", [128, 512], F32, 2, dma=False)
        nblk = len(self.blocks)
        prev = None
        for bi in range(nblk + 1):
            cur = None
            if bi < nblk:
                t0, tn = self.blocks[bi]
                mx, mxb, mxs = mxr.next()
                kb.dma("sync", mx[:, :, 0:tn], D[mixname][:, t0:t0 + tn].rearrange("(a p) t -> p a t", p=128), mxs, w=[mxb])
                cur = (t0, tn, mx, mxb)
            for m in range(16):
                if prev is not None:
                    L3(prev[0], prev[1], m, prev[2], prev[3])
                if cur is not None:
                    L1(cur[0], cur[1], m, cur[2], cur[3])
            prev = None
            if cur is not None:
                rd, rdb = L2(cur[0], cur[1])
                prev = (cur[0], cur[1], rd, rdb)
        self.end()

    def phase6(self):
        K = self
        kb, nc, D, DB, C = self.kb, self.nc, self.D, self.DB, self.C
        cb_ = self.CB
        T = self.TALL
        self.begin()
        sl = self.sem()
        pc = Buf("p6c")
        ysb = self.sb("ysb", [128, 16, T], BF16); b_y = [Buf("y%d" % k) for k in range(16)]
        for kc in range(16):
            kb.dma("gpsimd", ysb[:, kc, :], D["Y0T"][kc * 128:(kc + 1) * 128, :], sl, w=[b_y[kc]], max_dma_last_dim=4096)
        wn = self.sb("wn", [128, 8, 16, 128], BF16)
        for i in range(8):
            kb.dma("gpsimd", wn[:, i], D["w1fm"][i], sl, w=[pc], max_dma_last_dim=4096)
        wkr = self.sb("wkr", [128, 2, 16, 64], BF16)
        for i in range(2):
            kb.dma("gpsimd", wkr[:, i], D["w1kr"][i], sl, w=[pc], max_dma_last_dim=4096)
        qnw = self.sb("qnw_sb", [128, 4], F32); kvnw = self.sb("kvnw_sb", [128, 4], F32)
        kb.dma("sync", qnw[:], D["qnw"], sl, w=[pc]); kb.dma("sync", kvnw[:], D["kvnw"], sl, w=[pc])
        psr = Ring(K, "ps6", [128, 512], F32, 4, dma=False, psum=True)
        ssp = self.ps("ss6", [128, 512], F32); b_ss = Buf("ss6", psum=True)
        raw = self.sb("raw6", [128, 4, 512], F32); rawb = [Buf("raw%d" % i) for i in range(4)]
        sqr = Ring(K, "sq6", [128, 512], F32, 2, dma=False)
        sd = self.sb("sd6", [128, 512], F32); b_sd = Buf("sd6")
        rstd = self.sb("rstd6", [128, 512], F32); b_rstd = Buf("rstd6")
        o32 = Ring(K, "o632", [128, 512], F32, 3)
        o16 = Ring(K, "o616", [128, 512], BF16, 3)
        csr = Ring(K, "cs6", [64, 2, 512], F32, 2)
        t1 = self.sb("t16", [64, 512], F32); b_t1 = Buf("t16")
        t2 = self.sb("t26", [64, 512], F32); b_t2 = Buf("t26")

        def mm(ps, pb, lhs_fn, t0, tn, M=128):
            for kc in range(16):
                kb.op("tensor", lambda e, kc=kc: e.matmul(ps[0:M, 0:tn], lhs_fn(kc), ysb[:, kc, t0:t0 + tn],
                                                          start=(kc == 0), stop=(kc == 15)),
                      r=[pc, b_y[kc]], w=[pb], inc=(kc == 15))

        for (t0, tn) in self.blocks:
            for grp in range(2):
                for i in range(4):
                    ps, pb, _ = psr.next()
                    mm(ps, pb, lambda kc, i=i, grp=grp: wn[:, grp * 4 + i, kc, :], t0, tn)
                    kb.op("scalar", lambda e, ps=ps, i=i: e.activation(out=raw[:, i, 0:tn], in_=ps[:, 0:tn], func=AF.Copy),
                          r=[pb], w=[rawb[i]])
                    sq, sqb, _ = sqr.next()
                    kb.op("scalar", lambda e, sq=sq, i=i: e.activation(out=sq[:, 0:tn], in_=raw[:, i, 0:tn], func=AF.Square),
                          r=[rawb[i]], w=[sqb])
                    kb.op("tensor", lambda e, sq=sq, i=i: e.matmul(ssp[:, 0:tn], C["onesf"][:, :], sq[:, 0:tn],
                                                                   start=(i == 0), stop=(i == 3)), r=[cb_, sqb], w=[b_ss])
                kb.op("scalar", lambda e: e.activation(out=sd[:, 0:tn], in_=ssp[:, 0:tn], func=AF.Sqrt,
                                                       scale=1.0 / 512.0, bias=EPS), r=[b_ss], w=[b_sd])
                kb.op("vector", lambda e: e.reciprocal(out=rstd[:, 0:tn], in_=sd[:, 0:tn]), r=[b_sd], w=[b_rstd])
                nwt = qnw if grp == 0 else kvnw
                for i in range(4):
                    rows = slice(i * 128, (i + 1) * 128)
                    if grp == 0:
                        ob, obb, obs = o16.next()
                        kb.op("vector", lambda e, ob=ob, i=i, nwt=nwt: e.scalar_tensor_tensor(
                            out=ob[:, 0:tn], in0=raw[:, i, 0:tn], scalar=nwt[:, i:i + 1], in1=rstd[:, 0:tn],
                            op0=ALU.mult, op1=ALU.mult), r=[rawb[i], pc, b_rstd], w=[obb])
                        kb.dma("sync", D["CQN"][rows, t0:t0 + tn], ob[:, 0:tn], obs, r=[obb])
                    else:
                        of, ofb, ofs = o32.next()
                        kb.op("vector", lambda e, of=of, i=i, nwt=nwt: e.scalar_tensor_tensor(
                            out=of[:, 0:tn], in0=raw[:, i, 0:tn], scalar=nwt[:, i:i + 1], in1=rstd[:, 0:tn],
                            op0=ALU.mult, op1=ALU.mult), r=[rawb[i], pc, b_rstd], w=[ofb])
                        kb.dma("sync", D["o_clatT"][rows, t0:t0 + tn], of[:, 0:tn], ofs, r=[ofb])
                        ob, obb, obs = o16.next()
                        kb.op("scalar", lambda e, ob=ob, of=of: e.activation(out=ob[:, 0:tn], in_=of[:, 0:tn], func=AF.Copy),
                              r=[ofb], w=[obb])
                        kb.dma("sync", D["LAT"][rows, t0:t0 + tn], ob[:, 0:tn], obs, r=[obb])
            cs, csb, css = csr.next()
            kb.dma("sync", cs[:, 0, 0:tn], D["rcos"][:, t0:t0 + tn], css, w=[csb])
            kb.dma("sync", cs[:, 1, 0:tn], D["rsin"][:, t0:t0 + tn], css, w=[csb])
            pa, pab, _ = psr.next()
            mm(pa, pab, lambda kc: wkr[:, 0, kc, :], t0, tn, M=64)
            pbb_, pbbb, _ = psr.next()
            mm(pbb_, pbbb, lambda kc: wkr[:, 1, kc, :], t0, tn, M=64)
            kb.op("vector", lambda e, pa=pa, cs=cs: e.tensor_tensor(out=t1[:, 0:tn], in0=pa[0:64, 0:tn], in1=cs[:, 0, 0:tn], op=ALU.mult),
                  r=[pab, csb], w=[b_t1])
            kb.op("vector", lambda e, pbb_=pbb_, cs=cs: e.tensor_tensor(out=t2[:, 0:tn], in0=pbb_[0:64, 0:tn], in1=cs[:, 1, 0:tn], op=ALU.mult),
                  r=[pbbb, csb], w=[b_t2])
            of, ofb, ofs = o32.next()
            kb.op("gpsimd", lambda e, of=of: e.tensor_tensor(out=of[0:64, 0:tn], in0=t1[:, 0:tn], in1=t2[:, 0:tn], op=ALU.add),
                  r=[b_t1, b_t2], w=[ofb])
            kb.dma("sync", D["o_ckrT"][:, t0:t0 + tn], of[0:64, 0:tn], ofs, r=[ofb])
            ob, obb, obs = o16.next()
            kb.op("scalar", lambda e, ob=ob, of=of: e.activation(out=ob[0:64, 0:tn], in_=of[0:64, 0:tn], func=AF.Copy),
                  r=[ofb], w=[obb])
            kb.dma("sync", D["KR"][:, t0:t0 + tn], ob[0:64, 0:tn], obs, r=[obb])
        self.end()
        self.begin()
        sl = self.sem()
        ysb = self.sb("ysb", [128, 16, T], BF16); b_y = [Buf("y%d" % k) for k in range(16)]
        for kc in range(16):
            kb.dma("gpsimd", ysb[:, kc, :], D["Y0T"][kc * 128:(kc + 1) * 128, :], sl, w=[b_y[kc]], max_dma_last_dim=4096)
        psr = Ring(K, "ps6", [128, 512], F32, 4, dma=False, psum=True)
        o32 = Ring(K, "o632", [128, 512], F32, 3)
        wr = Ring(K, "wg6", [128, 16, 128], BF16, 3)
        for gi in range(16):
            wt, wb, ws = wr.next()
            kb.dma("gpsimd", wt[:], D["w1fm"][8 + gi], ws, w=[wb], max_dma_last_dim=4096)
            for (t0, tn) in self.blocks:
                ps, pb, _ = psr.next()
                for kc in range(16):
                    kb.op("tensor", lambda e, ps=ps, kc=kc, wt=wt: e.matmul(
                        ps[:, 0:tn], wt[:, kc, :], ysb[:, kc, t0:t0 + tn], start=(kc == 0), stop=(kc == 15)),
                        r=[wb, b_y[kc]], w=[pb], inc=(kc == 15))
                of, ofb, ofs = o32.next()
                kb.op("scalar", lambda e, ps=ps, of=of: e.activation(out=of[:, 0:tn], in_=ps[:, 0:tn], func=AF.Silu),
                      r=[pb], w=[ofb])
                kb.dma("sync", D["GS"][gi * 128:(gi + 1) * 128, t0:t0 + tn], of[:, 0:tn], ofs, r=[ofb])
        self.end()

    def phase7(self):
        K = self
        kb, nc, D, DB, C = self.kb, self.nc, self.D, self.DB, self.C
        cb_ = self.CB
        NB, TP, T = self.NB, self.TP, self.TALL
        TK = T + PAST
        self.begin()
        sl = self.sem()
        pc = Buf("p7c")
        cqn = self.sb("cqn", [128, 4, T], BF16)
        lat = self.sb("lat", [128, 4, TK], BF16)
        kra = self.sb("kra", [65, TK], BF16)
        dmk = self.sb("dmk", [128, 128], BF16)
        dmf = self.sb("dmf", [128, 128], F32)
        kb.dma("sync", cqn[:], D["CQN"].rearrange("(a p) t -> p a t", p=128), sl, w=[pc])
        kb.dma("sync", lat[:, :, 0:T], D["LAT"].rearrange("(a p) t -> p a t", p=128), sl, w=[pc])
        kb.dma("gpsimd", lat[:, :, T:TK], D["clatT"].rearrange("(a p) t -> p a t", p=128), sl, w=[pc])
        kb.dma("sync", kra[0:64, 0:T], D["KR"], sl, w=[pc])
        kb.dma("gpsimd", kra[0:64, T:TK], D["ckrT"], sl, w=[pc])
        kb.dma("sync", dmf[:], D["dmask"], sl, w=[pc])
        kb.op("vector", lambda e: e.memset(kra[64:65, :], 1.0), w=[pc])
        kb.op("vector", lambda e: e.tensor_copy(out=dmk[:], in_=dmf[:]), r=[pc], w=[pc])
        vtiles = [(c0, 128) for c0 in range(0, TP, 128)] + [(TP, DEC_SEQ)] + [(T + 128 * j, 128) for j in range(8)]
        NV = len(vtiles)
        wqr = Ring(K, "wq7", [128, 4, 256], BF16, 2)
        wkr = Ring(K, "wkv7", [128, 4, 256], BF16, 2)
        qnr = Ring(K, "qn7", [128, T], BF16, 2, dma=False)
        qrr = Ring(K, "qr7", [65, T], BF16, 2, dma=False)
        knr = Ring(K, "kn7", [128, TK], BF16, 2, dma=False)
        vr = Ring(K, "v7", [128, NV, 128], BF16, 2, dma=False)
        csr = Ring(K, "cs7", [64, 2, 512], F32, 2)
        t1 = self.sb("t17", [64, 512], F32); b_t1 = Buf("t17")
        t2 = self.sb("t27", [64, 512], F32); b_t2 = Buf("t27")
        ptr = Ring(K, "pt7", [128, 512], BF16, 5, dma=False)
        gsr = Ring(K, "gs7", [128, 512], F32, 2)
        rsb = self.sb("rsb7", [128, 512], F32); b_rs = Buf("rs7")
        tsb = self.sb("tsb7", [128, 512], F32); b_ts = Buf("ts7")
        outr = Ring(K, "o7", [128, 512], BF16, 2)
        ppr = Ring(K, "pp7", [128, 512], F32, 3, dma=False, psum=True)
        stp = Ring(K, "st7", [128, 512], F32, 3, dma=False, psum=True)
        otp = self.ps("ot7", [128, 512], F32); b_otp = Buf("ot7", psum=True)
        smp = self.ps("sm7", [128, 512], F32); b_smp = Buf("sm7", psum=True)
        for r_ in qrr.items:
            kb.op("vector", lambda e, t=r_[0]: e.memset(t[64:65, :], 0.0), w=[r_[1]])
        kblocks = [(i * 512, 512) for i in range(TK // 512)] + ([(TK - TK % 512, TK % 512)] if TK % 512 else [])

        def project(h):
            wq, wqb, wqs = wqr.next()
            kb.dma("gpsimd", wq[:], D["w1q"][h], wqs, w=[wqb])
            wk, wkb, wks = wkr.next()
            kb.dma("gpsimd", wk[:], D["w1kv"][h], wks, w=[wkb])
            qn, qnb, _ = qnr.next()
            qr, qrb, _ = qrr.next()
            kn, knb, _ = knr.next()
            v, vb, _ = vr.next()
            for (t0, tn) in self.blocks:
                ps, pb, _ = ppr.next()
                for kc in range(4):
                    kb.op("tensor", lambda e, kc=kc, ps=ps: e.matmul(ps[:, 0:tn], wq[:, kc, 0:128], cqn[:, kc, t0:t0 + tn],
                                                                    start=(kc == 0), stop=(kc == 3)), r=[wqb, pc], w=[pb], inc=(kc == 3))
                kb.op("scalar", lambda e, ps=ps: e.activation(out=qn[:, t0:t0 + tn], in_=ps[:, 0:tn], func=AF.Copy), r=[pb], w=[qnb])
                cs, csb, css = csr.next()
                kb.dma("sync", cs[:, 0, 0:tn], D["rcos"][:, t0:t0 + tn], css, w=[csb])
                kb.dma("sync", cs[:, 1, 0:tn], D["rsin"][:, t0:t0 + tn], css, w=[csb])
                pa, pab, _ = ppr.next()
                for kc in range(4):
                    kb.op("tensor", lambda e, kc=kc, pa=pa: e.matmul(pa[0:64, 0:tn], wq[:, kc, 128:192], cqn[:, kc, t0:t0 + tn],
                                                                    start=(kc == 0), stop=(kc == 3)), r=[wqb, pc], w=[pab], inc=(kc == 3))
                kb.op("vector", lambda e, pa=pa, cs=cs: e.tensor_tensor(out=t1[:, 0:tn], in0=pa[0:64, 0:tn], in1=cs[:, 0, 0:tn], op=ALU.mult),
                      r=[pab, csb], w=[b_t1])
                pb2, pb2b, _ = ppr.next()
                for kc in range(4):
                    kb.op("tensor", lambda e, kc=kc, pb2=pb2: e.matmul(pb2[0:64, 0:tn], wq[:, kc, 192:256], cqn[:, kc, t0:t0 + tn],
                                                                      start=(kc == 0), stop=(kc == 3)), r=[wqb, pc], w=[pb2b], inc=(kc == 3))
                kb.op("vector", lambda e, pb2=pb2, cs=cs: e.tensor_tensor(out=t2[:, 0:tn], in0=pb2[0:64, 0:tn], in1=cs[:, 1, 0:tn], op=ALU.mult),
                      r=[pb2b, csb], w=[b_t2])
                kb.op("gpsimd", lambda e: e.tensor_tensor(out=qr[0:64, t0:t0 + tn], in0=t1[:, 0:tn], in1=t2[:, 0:tn], op=ALU.add),
                      r=[b_t1, b_t2], w=[qrb])
            for (k0, kn_) in kblocks:
                ps, pb, _ = ppr.next()
                for kc in range(4):
                    kb.op("tensor", lambda e, kc=kc, ps=ps: e.matmul(ps[:, 0:kn_], wk[:, kc, 0:128], lat[:, kc, k0:k0 + kn_],
                                                                    start=(kc == 0), stop=(kc == 3)), r=[wkb, pc], w=[pb], inc=(kc == 3))
                kb.op("scalar", lambda e, ps=ps: e.activation(out=kn[:, k0:k0 + kn_], in_=ps[:, 0:kn_], func=AF.Copy), r=[pb], w=[knb])
            for i0 in range(0, NV, 4):
                ng = min(4, NV - i0)
                ps, pb, _ = ppr.next()
                full = all(vtiles[i0 + ii][1] == 128 for ii in range(ng))
                for ii in range(ng):
                    c0, kn_ = vtiles[i0 + ii]
                    for kc in range(4):
                        kb.op("tensor", lambda e, kc=kc, ps=ps, ii=ii, c0=c0, kn_=kn_: e.matmul(
                            ps[0:kn_, ii * 128:(ii + 1) * 128], lat[:, kc, c0:c0 + kn_], wk[:, kc, 128:256],
                            start=(kc == 0), stop=(kc == 3)), r=[wkb, pc], w=[pb], inc=(kc == 3 and (ii == ng - 1 or not full)))
                    if not full:
                        kb.op("scalar", lambda e, ps=ps, ii=ii, kn_=kn_, i0=i0: e.activation(
                            out=v[0:kn_, i0 + ii, :], in_=ps[0:kn_, ii * 128:(ii + 1) * 128], func=AF.Copy), r=[pb], w=[vb])
                if full:
                    kb.op("scalar", lambda e, ps=ps, i0=i0, ng=ng: e.activation(
                        out=v[:, i0:i0 + ng, :], in_=ps[:, 0:ng * 128].rearrange("p (a b) -> p a b", b=128), func=AF.Copy),
                        r=[pb], w=[vb])
            return (qn, qnb, qr, qrb, kn, knb, v, vb)

        def attend(h, proj, Q, qcol0, tiles):
            qn, qnb, qr, qrb, kn, knb, v, vb = proj
            gs, gsb, gss = gsr.next()
            kb.dma("sync", gs[:, 0:Q], D["GS"][h * 128:(h + 1) * 128, qcol0:qcol0 + Q], gss, w=[gsb])
            n = len(tiles)
            sts = [None] * n

            def emit_st(j):
                kc0, kn_, vi, c0, diag = tiles[j]
                st_, stb, _ = stp.next()
                sts[j] = (st_, stb)
                kb.op("tensor", lambda e: e.matmul(st_[0:kn_, c0:Q], kn[:, kc0:kc0 + kn_], qn[:, qcol0 + c0:qcol0 + Q],
                                                   start=True, stop=False), r=[knb, qnb], w=[stb], inc=False)
                kb.op("tensor", lambda e: e.matmul(st_[0:kn_, c0:Q], kra[0:65, kc0:kc0 + kn_], qr[0:65, qcol0 + c0:qcol0 + Q],
                                                   start=False, stop=True), r=[pc, qrb], w=[stb])
            emit_st(0)
            if n > 1:
                emit_st(1)
            for j in range(n):
                kc0, kn_, vi, c0, diag = tiles[j]
                if j + 2 < n:
                    emit_st(j + 2)
                st_, stb = sts[j]
                pt, ptb, _ = ptr.next()
                kb.op("scalar", lambda e, st_=st_, pt=pt, kn_=kn_, c0=c0: e.activation(
                    out=pt[0:kn_, c0:Q], in_=st_[0:kn_, c0:Q], func=AF.Exp, scale=MLA_SCALE), r=[stb], w=[ptb])
                if diag:
                    kb.op("gpsimd", lambda e, pt=pt, c0=c0: e.tensor_tensor(
                        out=pt[:, c0:c0 + 128], in0=pt[:, c0:c0 + 128], in1=dmk[:, :], op=ALU.mult), r=[ptb, pc], w=[ptb])
                kb.op("tensor", lambda e, pt=pt, kn_=kn_, c0=c0, vi=vi, j=j: e.matmul(
                    otp[:, c0:Q], v[0:kn_, vi, :], pt[0:kn_, c0:Q], start=(j == 0), stop=(j == n - 1)),
                    r=[vb, ptb], w=[b_otp], inc=(j == n - 1))
                kb.op("tensor", lambda e, pt=pt, kn_=kn_, c0=c0, j=j: e.matmul(
                    smp[:, c0:Q], C["onesb"][0:kn_, :], pt[0:kn_, c0:Q], start=(j == 0), stop=(j == n - 1)),
                    r=[cb_, ptb], w=[b_smp], inc=(j == n - 1))
            kb.op("vector", lambda e: e.reciprocal(out=rsb[:, 0:Q], in_=smp[:, 0:Q]), r=[b_smp], w=[b_rs])
            kb.op("vector", lambda e: e.tensor_tensor(out=tsb[:, 0:Q], in0=otp[:, 0:Q], in1=rsb[:, 0:Q], op=ALU.mult),
                  r=[b_otp, b_rs], w=[b_ts])
            ot, otb, ots = outr.next()
            kb.op("vector", lambda e: e.tensor_tensor(out=ot[:, 0:Q], in0=tsb[:, 0:Q], in1=gs[:, 0:Q], op=ALU.mult),
                  r=[b_ts, gsb], w=[otb])
            kb.dma("sync", D["O1T"][h * 128:(h + 1) * 128, qcol0:qcol0 + Q], ot[:, 0:Q], ots, r=[otb])

        NP = TP // 128
        for h in range(C_HEADS):
            proj = project(h)
            for qb in range(NB):
                tiles = []
                for j in range(4 * qb + 4):
                    r_ = j - 4 * qb
                    tiles.append((128 * j, 128, j, 128 * max(r_, 0), r_ >= 0))
                attend(h, proj, 512, qb * 512, tiles)
            tiles = [(T + 128 * j, 128, NP + 1 + j, 0, False) for j in range(8)] + [(TP, DEC_SEQ, NP, 0, False)]
            attend(h, proj, DEC_SEQ, TP, tiles)
        self.end()

    def build(self, phases=(1,)):
        with ExitStack() as gst:
            self.kb.setup(gst)
            self.declare()
            self.phase0(gst)
            if 9 in phases:
                self.begin()
                tt = self.sb("triv", [128, 8], F32)
                tb_ = Buf("triv")
                self.kb.op("vector", lambda e: e.memset(tt[:], 1.0), w=[tb_])
                self.kb.op("scalar", lambda e: e.activation(out=tt[:], in_=tt[:], func=AF.Copy), r=[tb_], w=[tb_])
                self.end()
            if 1 in phases:
                self.phase1()
            if 2 in phases:
                self.phase2()
            if 3 in phases:
                self.phase3()
            if 5 in phases:
                self.outproj_ln("MIXT", 24, "wo0", "xT", "ln0g", "ln0b", "Y0T")
            if 6 in phases:
                self.phase6()
            if 7 in phases:
                self.phase7()
            if 8 in phases:
                self.outproj_ln("O1T", 16, "wo1", "Y0T", "ln1g", "ln1b", "o_yT")
        return self.nc


def _tile_fm(Wcols):
    k, c = Wcols.shape
    return np.ascontiguousarray(Wcols.reshape(k // 128, 128, c).transpose(1, 0, 2))


def _t5_onehot():
    rel = np.arange(1152, dtype=np.int32) - 511
    half, max_exact = 16, 8
    ret = np.where(rel < 0, half, 0)
    n = np.abs(rel)
    nf = np.maximum(n, 1).astype(np.float32)
    large = max_exact + (np.log(nf / np.float32(max_exact)) / np.float32(math.log(128 / max_exact))
                         * np.float32(half - max_exact)).astype(np.int32)
    large = np.minimum(large, half - 1)
    bucket = ret + np.where(n < max_exact, n, large)
    oh = np.zeros((32, 1152), np.float32)
    oh[bucket, np.arange(1152)] = 1.0
    oh[:, 1151] = 0.0
    return oh


def prep_core(inp, b, NB):
    TP = 512 * NB
    o = {}
    xp = inp["x_prompt"][b, :TP]
    xs = inp["x_sample"][b]
    o["xT"] = np.ascontiguousarray(np.concatenate([xp, xs], axis=0).T)
    W = inp["w_in0"][0]
    c_aq, c_ak, c_av, c_ag, c_iq, c_ik, c_iw, c_bz, c_xbc, c_dt = np.cumsum(
        [0, 1024, 1024, 1024, 1024, 1024, 64, 16, 2048, 3072])
    tl = []
    for i in range(8):
        tl.append(W[:, c_aq + i * 128:c_aq + (i + 1) * 128])
    for i in range(8):
        tl.append(W[:, c_ak + i * 128:c_ak + (i + 1) * 128])
    for i in range(8):
        tl.append(W[:, c_ag + i * 128:c_ag + (i + 1) * 128])
    for i in range(8):
        tl.append(W[:, c_iq + i * 128:c_iq + (i + 1) * 128])
    ik = W[:, c_ik:c_ik + 64]
    tl.append(np.concatenate([ik, ik], axis=1))
    for i in range(16):
        tl.append(W[:, c_bz + i * 128:c_bz + (i + 1) * 128])
    for i in range(24):
        tl.append(W[:, c_xbc + i * 128:c_xbc + (i + 1) * 128])
    o["w0fm"] = np.stack([_tile_fm(t) for t in tl])
    o["w0v"] = np.stack([np.ascontiguousarray(
        W[:, c_av + g * 512:c_av + (g + 1) * 512].reshape(16, 128, 512).transpose(1, 0, 2)) for g in range(2)])
    ws = np.concatenate([W[:, c_iw:c_iw + 16], W[:, c_dt:c_dt + 32]], axis=1)
    o["w0s"] = np.ascontiguousarray(ws.reshape(16, 128, 48).transpose(1, 0, 2))
    o["convw"] = np.ascontiguousarray(inp["conv_w"][0].T.reshape(24, 128, 4).transpose(1, 0, 2))
    o["convb"] = np.ascontiguousarray(inp["conv_b"][0].reshape(24, 128).T)
    o["sconvT"] = np.ascontiguousarray(inp["state_b_conv"][0, b].T)
    o["ident"] = np.eye(128, dtype=np.float32)
    o["antiid"] = np.eye(128, dtype=np.float32)[::-1]
    o["t5"] = inp["t5_bias"]
    o["t5oh"] = _t5_onehot()
    o["ckT"] = inp["cache_a_k"][0, b].reshape(PAST, 1024).T
    o["cv"] = inp["cache_a_v"][0, b].reshape(PAST, 1024)
    o["dtbias"] = inp["dt_bias"][0][None, :]
    o["alog"] = inp["a_log"][0][None, :]
    o["dskT"] = np.repeat(inp["d_skip"][0].reshape(16, 2), 64, axis=1).T
    o["nwT"] = inp["ssm_norm_w"][0].reshape(16, 128).T
    o["h0T"] = inp["state_b_ssm"][0, b].transpose(2, 0, 1).reshape(128, 2048)
    o["utri"] = np.triu(np.ones((128, 128), np.float32))
    o["negm"] = np.tril(np.full((128, 128), NEG, np.float32), -1)
    o["wo0"] = inp["w_out0"][0].reshape(24, 128, 2048).transpose(1, 0, 2)
    o["wo1"] = inp["w_out1"][0].reshape(16, 128, 2048).transpose(1, 0, 2)
    for nm, src in (("ln0g", "ln0_g"), ("ln0b", "ln0_b"), ("ln1g", "ln1_g"), ("ln1b", "ln1_b")):
        o[nm] = inp[src][0].reshape(16, 128).T
    W1 = inp["w_in1"][0]
    t1 = [W1[:, i * 128:(i + 1) * 128] for i in range(8)] + [W1[:, 1088 + i * 128:1088 + (i + 1) * 128] for i in range(16)]
    o["w1fm"] = np.stack([_tile_fm(t) for t in t1])
    perm = np.concatenate([np.arange(32, 64), np.arange(0, 32)])
    krw = W1[:, 1024:1088]
    o["w1kr"] = np.stack([_tile_fm(krw), _tile_fm(krw[:, perm])])
    wuq = inp["w_uq"][0].reshape(512, 16, 192)
    wq = np.concatenate([wuq, wuq[:, :, 128 + perm]], axis=2)
    o["w1q"] = wq.reshape(4, 128, 16, 256).transpose(2, 1, 0, 3)
    wukv = inp["w_ukv"][0].reshape(512, 16, 256)
    o["w1kv"] = wukv.reshape(4, 128, 16, 256).transpose(2, 1, 0, 3)
    o["qnw"] = inp["q_norm_w"][0].reshape(4, 128).T
    o["kvnw"] = inp["kv_norm_w"][0].reshape(4, 128).T
    pos = np.concatenate([np.arange(TP), PAST + np.arange(DEC_SEQ)]).astype(np.float32)
    inv = (np.float32(10000.0) ** (-np.arange(32, dtype=np.float32) / np.float32(32))).astype(np.float32)
    ang = (pos[None, :] * inv[:, None]).astype(np.float32)
    cs_, sn_ = np.cos(ang).astype(np.float32), np.sin(ang).astype(np.float32)
    o["rcos"] = np.concatenate([cs_, cs_], axis=0)
    o["rsin"] = np.concatenate([-sn_, sn_], axis=0)
    o["clatT"] = inp["cache_c_latent"][0, b].T
    o["ckrT"] = inp["cache_c_krope"][0, b].T
    dm = np.ones((128, 128), np.float32)
    dm[64:, :64] = 0.0
    o["dmask"] = dm
    kx = inp["cache_a_kidx"][0, b].T
    o["ckidxT2"] = np.concatenate([kx, kx], axis=0)
    return {k: np.ascontiguousarray(v, dtype=np.float32) for k, v in o.items()}


ALL_PHASES = (1, 2, 3, 5, 6, 7, 8)
_OUT_NAMES = ("o_yT", "o_akT", "o_av", "o_kidxT", "o_convT", "o_ssmT", "o_clatT", "o_ckrT")


def assemble(results, NB, nb):
    TP = 512 * NB
    f = lambda name: [np.asarray(r[name]) for r in results]
    yT, akT, av, kxT, cvT, ssT, clT, krT = (f(n) for n in _OUT_NAMES)
    st = lambda fn: np.stack([fn(i) for i in range(nb)])
    y_p = st(lambda i: yT[i][:, :TP].T)
    y_s = st(lambda i: yT[i][:, TP:].T)
    ak_p = st(lambda i: akT[i][:, :TP].T.reshape(TP, 8, 128))[None]
    ak_s = st(lambda i: akT[i][:, TP:].T.reshape(DEC_SEQ, 8, 128))[None]
    av_p = st(lambda i: av[i][:TP].reshape(TP, 8, 128))[None]
    av_s = st(lambda i: av[i][TP:].reshape(DEC_SEQ, 8, 128))[None]
    ki_p = st(lambda i: kxT[i][:, :TP].T)[None]
    ki_s = st(lambda i: kxT[i][:, TP:].T)[None]
    cv_p = st(lambda i: cvT[i][:, 0:3].T)[None]
    cv_s = st(lambda i: cvT[i][:, 3:6].T)[None]
    ss_p = st(lambda i: ssT[i][0].reshape(128, 32, 64).transpose(1, 2, 0))[None]
    ss_s = st(lambda i: ssT[i][1].reshape(128, 32, 64).transpose(1, 2, 0))[None]
    cl_p = st(lambda i: clT[i][:, :TP].T)[None]
    cl_s = st(lambda i: clT[i][:, TP:].T)[None]
    kr_p = st(lambda i: krT[i][:, :TP].T)[None]
    kr_s = st(lambda i: krT[i][:, TP:].T)[None]
    outs = (y_p, y_s, ak_p, ak_s, av_p, av_s, ki_p, ki_s, cv_p, cv_s, ss_p, ss_s, cl_p, cl_s, kr_p, kr_s)
    return tuple(np.ascontiguousarray(o, dtype=np.float32) for o in outs)


def kernel(**inputs):
    inp = {k: np.asarray(v) for k, v in inputs.items()}
    NB = SEQ // 512
    n = 8
    K = Kern(NB)
    nc = K.build(phases=ALL_PHASES)
    in_maps = [prep_core(inp, b, NB) for b in range(n)]
    res = run_bass_kernel_spmd(nc, in_maps, core_ids=list(range(n)))
    return assemble(res.results, NB, n)
```
